# Optimizing a Trainium2 kernel written in Bass

```python
import jax, jax.numpy as jnp
from jax import lax
import numpy as np

D_MODEL = 1024
BATCH = 16
SEQ = 256
DEPTH = 2
DEC_BATCH = 4
DEC_SEQ = 1024
PAST_LEN = 256

GRID_W = 64
FNET_GROUPS = 4
FNET_GDIM = 64
FNET_WIDTH = FNET_GROUPS * FNET_GDIM
RET_HEADS = 6
RET_DK = 64
RET_DV = 64
HG_HEADS = 6
HG_DK = 64
HG_DV = 64
RET_WIDTH = RET_HEADS * RET_DV
HG_WIDTH = HG_HEADS * HG_DV
MIX_WIDTH = FNET_WIDTH + RET_WIDTH + HG_WIDTH
IN_SIZES = (FNET_WIDTH,
            RET_HEADS * RET_DK, RET_HEADS * RET_DK, RET_WIDTH, RET_WIDTH,
            HG_HEADS * HG_DK, HG_HEADS * HG_DK, HG_HEADS * HG_DK, HG_WIDTH, HG_WIDTH)
IN_WIDTH = sum(IN_SIZES)
D_FF = -(-(8 * D_MODEL) // (3 * 256)) * 256
CHUNK = 64
ROPE_BASE = 10000.0
LN_EPS = 1e-5
N_MOD = 6
ALPHA = (2 * DEPTH) ** 0.25
BETA = (8 * DEPTH) ** -0.25

kernel_name = "hybrid_fnet_retnet_hgrn2_diffusion_step"


def _layernorm(x, g=None, b=None):
    xf = x.astype(jnp.float32)
    mu = xf.mean(-1, keepdims=True)
    var = jnp.square(xf - mu).mean(-1, keepdims=True)
    y = (xf - mu) * lax.rsqrt(var + LN_EPS)
    if g is not None:
        y = y * g.astype(jnp.float32) + b.astype(jnp.float32)
    return y.astype(x.dtype)


def _headnorm(x):
    mu = x.mean(-1, keepdims=True)
    var = jnp.square(x - mu).mean(-1, keepdims=True)
    return (x - mu) * lax.rsqrt(var + LN_EPS)


def _rmsnorm_heads(x):
    return x * lax.rsqrt(jnp.square(x).mean(-1, keepdims=True) + LN_EPS)


def _modulation(cvec, w_mod_l, b_mod_l):
    m = jax.nn.silu(cvec) @ w_mod_l + b_mod_l
    return jnp.split(m[..., None, :], N_MOD, axis=-1)


def _axial_rope(x, rows, cols):
    half = x.shape[-1] // 2
    inv = ROPE_BASE ** (-jnp.arange(0, half, 2, dtype=jnp.float32) / half)

    def rot(t, pos):
        ang = pos.astype(jnp.float32)[:, None] * inv
        cos, sin = jnp.cos(ang), jnp.sin(ang)
        t1, t2 = t[..., : half // 2], t[..., half // 2:]
        return jnp.concatenate([t1 * cos - t2 * sin, t1 * sin + t2 * cos], axis=-1)

    return jnp.concatenate([rot(x[..., :half], rows), rot(x[..., half:], cols)], axis=-1)


def _to_chunks(t):
    B, H, L, d = t.shape
    return t.reshape(B, H, L // CHUNK, CHUNK, d).transpose(2, 0, 1, 3, 4)


def _from_chunks(t):
    n, B, H, C, d = t.shape
    return t.transpose(1, 2, 0, 3, 4).reshape(B, H, n * C, d)


def _retention_chunked(q, k, v, log_gamma, s0):
    idx = jnp.arange(CHUNK, dtype=jnp.float32)
    lg = log_gamma[:, None]
    rel = idx[:, None] - idx[None, :]
    decay = jnp.exp(jnp.maximum(rel, 0.0)[None] * lg[:, :, None]) * (rel >= 0)[None]
    xi = jnp.exp((idx + 1.0) * lg)
    zeta = jnp.exp((CHUNK - 1.0 - idx) * lg)
    g_chunk = jnp.exp(CHUNK * log_gamma)

    def body(S, inp):
        qc, kc, vc = inp
        scores = jnp.einsum('bhid,bhjd->bhij', qc, kc) * decay
        o = (jnp.einsum('bhij,bhjv->bhiv', scores, vc)
             + jnp.einsum('bhid,bhdv->bhiv', qc * xi[:, :, None], S))
        S = g_chunk[:, None, None] * S + jnp.einsum('bhjd,bhjv->bhdv', kc * zeta[:, :, None], vc)
        return S, o

    s_fin, o = lax.scan(body, s0, (_to_chunks(q), _to_chunks(k), _to_chunks(v)))
    return _from_chunks(o), s_fin


def _hgrn2_chunked(q, log_f, k, v, s0):
    tri = jnp.tril(jnp.ones((CHUNK, CHUNK), dtype=bool))
    b_all = jnp.cumsum(_to_chunks(log_f), axis=3)

    def body(S, inp):
        qc, bc, kc, vc = inp
        inter = jnp.einsum('bhid,bhdv->bhiv', qc * jnp.exp(bc), S)
        rel = bc[:, :, :, None, :] - bc[:, :, None, :, :]
        dec = jnp.exp(jnp.where(tri[:, :, None], rel, -jnp.inf))
        A = jnp.sum(qc[:, :, :, None, :] * kc[:, :, None, :, :] * dec, axis=-1)
        o = inter + jnp.einsum('bhij,bhjv->bhiv', A, vc)
        b_last = bc[:, :, -1:, :]
        S = (jnp.exp(b_last[:, :, 0, :])[..., None] * S
             + jnp.einsum('bhjd,bhjv->bhdv', kc * jnp.exp(b_last - bc), vc))
        return S, o

    s_fin, o = lax.scan(body, s0, (_to_chunks(q), b_all, _to_chunks(k), _to_chunks(v)))
    return _from_chunks(o), s_fin


def _hgrn_gate(f_logit, lb):
    log_f = jnp.logaddexp(jnp.log(lb), jnp.log1p(-lb) + jax.nn.log_sigmoid(f_logit))
    k = (1.0 - lb) * jax.nn.sigmoid(-f_logit)
    return log_f, k


def _token_mixers(h, w_in_l, w_out_l, log_decay_l, lb_l, s_ret0, s_hg0, positions):
    B, L, _ = h.shape
    f32 = jnp.float32
    points = [sum(IN_SIZES[: i + 1]) for i in range(len(IN_SIZES) - 1)]
    proj = h @ w_in_l
    u_f, rq, rk, rv, rg, hq, hff, hfb, hi, hg = jnp.split(proj, points, axis=-1)

    def heads(t, d):
        return t.reshape(B, L, -1, d).transpose(0, 2, 1, 3).astype(f32)

    def merge(t):
        return t.transpose(0, 2, 1, 3).reshape(B, L, -1)

    def flip(t):
        return jnp.flip(t, axis=2)

    o_f = jnp.fft.fft2(u_f.reshape(B, L, FNET_GROUPS, FNET_GDIM).astype(f32),
                       axes=(1, 3), norm='ortho').real.reshape(B, L, FNET_WIDTH)

    q = heads(rq, RET_DK)
    k = heads(rk, RET_DK) * (RET_DK ** -0.5)
    v = heads(rv, RET_DV)
    if positions is not None:
        q = _axial_rope(q, positions[0], positions[1])
        k = _axial_rope(k, positions[0], positions[1])
    lg = -jnp.exp(log_decay_l.astype(f32))
    s_r0 = s_ret0.astype(f32)
    o_fw, sr_fw = _retention_chunked(q, k, v, lg[0], s_r0[:, 0])
    o_bw, sr_bw = _retention_chunked(flip(q), flip(k), flip(v), lg[1], s_r0[:, 1])
    o_r = merge(_headnorm(o_fw + flip(o_bw)) * jax.nn.silu(heads(rg, RET_DV)))

    hq_ = jax.nn.silu(heads(hq, HG_DK))
    hv = heads(hi, HG_DV)
    lb = lb_l.reshape(2, HG_HEADS, 1, HG_DK)
    logf_fw, k_fw = _hgrn_gate(heads(hff, HG_DK), lb[0])
    logf_bw, k_bw = _hgrn_gate(heads(hfb, HG_DK), lb[1])
    s_h0 = s_hg0.astype(f32)
    oh_fw, sh_fw = _hgrn2_chunked(hq_, logf_fw, k_fw, hv, s_h0[:, 0])
    oh_bw, sh_bw = _hgrn2_chunked(flip(hq_), flip(logf_bw), flip(k_bw), flip(hv), s_h0[:, 1])
    o_h = merge(_rmsnorm_heads(oh_fw + flip(oh_bw)) * jax.nn.silu(heads(hg, HG_DV)))

    mix = jnp.concatenate([o_f, o_r, o_h], axis=-1).astype(h.dtype) @ w_out_l
    return mix, jnp.stack([sr_fw, sr_bw], axis=1), jnp.stack([sh_fw, sh_bw], axis=1)


def _layer(x, cvec, s_ret0, s_hg0, positions, w_mod_l, b_mod_l, w_in_l, w_out_l,
           log_decay_l, lb_l, ln_g_l, ln_b_l, w_gate_l, w_up_l, w_down_l):
    sh1, sc1, g1, sh2, sc2, g2 = _modulation(cvec, w_mod_l, b_mod_l)
    h = _layernorm(x) * (1.0 + sc1) + sh1
    mix, s_ret, s_hg = _token_mixers(h, w_in_l, w_out_l, log_decay_l, lb_l, s_ret0, s_hg0, positions)
    x = _layernorm(ALPHA * x + g1 * mix, ln_g_l[0], ln_b_l[0])
    h = _layernorm(x) * (1.0 + sc2) + sh2
    ffn = (jax.nn.silu(h @ w_gate_l) * (h @ w_up_l)) @ w_down_l
    x = _layernorm(ALPHA * x + g2 * ffn, ln_g_l[1], ln_b_l[1])
    return x, s_ret, s_hg


def setup_inputs(seed: int = 0) -> dict:
    key = jax.random.key(seed)
    ks = jax.random.split(key, 17)
    f32 = jnp.float32

    def nrm(k, shape, s):
        return s * jax.random.normal(k, shape, f32)

    base_decay = np.log(-np.log(1.0 - 2.0 ** (-5.0 - np.arange(RET_HEADS))))
    return {
        "x_prompt": nrm(ks[0], (BATCH, SEQ, D_MODEL), 1.0),
        "x_sample": nrm(ks[1], (DEC_BATCH, DEC_SEQ, D_MODEL), 1.0),
        "c": nrm(ks[2], (DEC_BATCH, D_MODEL), 1.0),
        "state_ret": nrm(ks[3], (DEC_BATCH, DEPTH, 2, RET_HEADS, RET_DK, RET_DV), 0.5),
        "state_hgrn": nrm(ks[4], (DEC_BATCH, DEPTH, 2, HG_HEADS, HG_DK, HG_DV), 0.5),
        "c_ctx": nrm(ks[5], (D_MODEL,), 1.0),
        "w_mod": nrm(ks[6], (DEPTH, D_MODEL, N_MOD * D_MODEL), 0.5 * D_MODEL ** -0.5),
        "b_mod": nrm(ks[7], (DEPTH, N_MOD * D_MODEL), 0.02),
        "w_in": nrm(ks[8], (DEPTH, D_MODEL, IN_WIDTH), D_MODEL ** -0.5),
        "w_out": nrm(ks[9], (DEPTH, MIX_WIDTH, D_MODEL), BETA * MIX_WIDTH ** -0.5),
        "ret_log_decay": jnp.asarray(base_decay, f32) + nrm(ks[10], (DEPTH, 2, RET_HEADS), 0.05),
        "hg_lower_bound": nrm(ks[11], (2, DEPTH, HG_HEADS * HG_DK), 1.0),
        "ln_g": 1.0 + nrm(ks[12], (DEPTH, 2, D_MODEL), 0.02),
        "ln_b": nrm(ks[13], (DEPTH, 2, D_MODEL), 0.02),
        "w_gate": nrm(ks[14], (DEPTH, D_MODEL, D_FF), D_MODEL ** -0.5),
        "w_up": nrm(ks[15], (DEPTH, D_MODEL, D_FF), D_MODEL ** -0.5),
        "w_down": nrm(ks[16], (DEPTH, D_FF, D_MODEL), BETA * D_FF ** -0.5),
    }


def reference(x_prompt, x_sample, c, state_ret, state_hgrn, c_ctx, w_mod, b_mod, w_in, w_out,
              ret_log_decay, hg_lower_bound, ln_g, ln_b, w_gate, w_up, w_down):
    f32 = jnp.float32
    p = jax.nn.softmax(hg_lower_bound.astype(f32), axis=1)
    cum = jnp.cumsum(p, axis=1)
    lbs = cum - cum[:, :1]

    B = x_prompt.shape[0]
    zr = jnp.zeros((B, 2, RET_HEADS, RET_DK, RET_DV), f32)
    zh = jnp.zeros((B, 2, HG_HEADS, HG_DK, HG_DV), f32)
    y = x_prompt
    ret_states, hg_states = [], []
    for l in range(DEPTH):
        y, s_r, s_h = _layer(y, c_ctx, zr, zh, None, w_mod[l], b_mod[l], w_in[l], w_out[l],
                             ret_log_decay[l], lbs[:, l], ln_g[l], ln_b[l],
                             w_gate[l], w_up[l], w_down[l])
        ret_states.append(s_r)
        hg_states.append(s_h)
    new_state_ret = jnp.stack(ret_states, axis=1)
    new_state_hgrn = jnp.stack(hg_states, axis=1)

    L = x_sample.shape[1]
    rows = L // GRID_W
    rr, cc = jnp.meshgrid(jnp.arange(rows, dtype=jnp.int32), jnp.arange(GRID_W, dtype=jnp.int32),
                          indexing='ij')
    positions = (rr.reshape(-1), cc.reshape(-1))
    z = x_sample
    for l in range(DEPTH):
        z, _, _ = _layer(z, c, state_ret[:, l], state_hgrn[:, l], positions, w_mod[l], b_mod[l],
                         w_in[l], w_out[l], ret_log_decay[l], lbs[:, l], ln_g[l], ln_b[l],
                         w_gate[l], w_up[l], w_down[l])
    return (y, z, new_state_ret, new_state_hgrn)
```

```python
import contextlib
import math
import numpy as np
import concourse.bass as bass
import concourse.mybir as mybir
from concourse.bass_utils import run_bass_kernel_spmd

F32 = mybir.dt.float32
BF16 = mybir.dt.bfloat16
AF = mybir.ActivationFunctionType
ALU = mybir.AluOpType

D = 1024
T = 1024
NCORES = 8
DEPTH = 2
DFF = 2816
NFF = 22
ALPHA = (2 * DEPTH) ** 0.25
LN_EPS = 1e-5
LN8 = math.log(0.125)

C_ID, C_POSP1, C_POSREV, C_BM, C_BMB, C_PCOL = 0, 128, 256, 384, 512, 640
C_MRF, C_MRB, C_MHF, C_MHB = 642, 650, 658, 690
C_DFTC, C_DFTS = 722, 850
C_RMF, C_RMB, C_HMF, C_HMB = 978, 1106, 1234, 1362
C_SEG = 1490
C_PZF, C_PZB = 1490 + 1024, 1490 + 1024 + 128
C_MB = 1490 + 1024 + 256
NCT = 1490 + 1024 + 256 + 2
WUNIT = 2560
NU = 6
NPOOL = 30


class Prog:
    ENG = ("pe", "act", "dve", "pool", "sp")

    def __init__(self, nc):
        self.nc = nc
        self.ops = []

    def op(self, eng, fn, reads=(), writes=(), dma=None):
        self.ops.append(dict(eng=eng, fn=fn, reads=tuple(reads), writes=tuple(writes), dma=dma))

    def emit(self, final_waits=()):
        nc = self.nc
        ops = self.ops
        last_w, readers = {}, {}
        eng_idx = {e: 0 for e in self.ENG}
        dma_gen = {}
        signaling = set()
        for o in ops:
            e = o["eng"]
            idx = eng_idx[e]
            eng_idx[e] += 1
            o["idx"] = idx
            deps = set()
            for r in o["reads"]:
                if r in last_w:
                    deps.add(last_w[r])
            for w in o["writes"]:
                if w in last_w:
                    deps.add(last_w[w])
                for rd in readers.get(w, ()):
                    deps.add(rd)
            if o["dma"] is not None:
                g = dma_gen.get(o["dma"], 0) + 1
                dma_gen[o["dma"]] = g
                ev = ("dma", o["dma"], g)
            else:
                ev = ("eng", e, idx)
            deps.discard(ev)
            o["deps"] = deps
            for d in deps:
                if d[0] == "eng":
                    signaling.add((d[1], d[2]))
            for r in o["reads"]:
                readers.setdefault(r, []).append(ev)
            for w in o["writes"]:
                last_w[w] = ev
                readers[w] = []
        final_events = [last_w[k] for k in final_waits]
        count = {}
        run = {e: 0 for e in self.ENG}
        per_eng = {e: [] for e in self.ENG}
        for o in ops:
            per_eng[o["eng"]].append(o)
            key = (o["eng"], o["idx"])
            if o["dma"] is None and key in signaling:
                run[o["eng"]] += 1
                count[key] = run[o["eng"]]
                o["signal"] = True
            else:
                o["signal"] = False
        dma_keys = sorted(dma_gen.keys())
        with contextlib.ExitStack() as st:
            sem_e = {e: st.enter_context(nc.semaphore("sem_" + e)) for e in self.ENG}
            sem_d = {k: st.enter_context(nc.semaphore("semd_" + k)) for k in dma_keys}
            block = st.enter_context(nc.Block())

            def run_engine(e, eo):
                wm = {}

                def do_waits(deps):
                    need = {}
                    for d in deps:
                        if d[0] == "eng":
                            if d[1] == "pe" and e == "pe":
                                continue
                            k = ("eng", d[1])
                            v = count[(d[1], d[2])]
                        else:
                            k = ("dma", d[1])
                            v = 16 * d[2]
                        if v > need.get(k, 0):
                            need[k] = v
                    for k, v in need.items():
                        if wm.get(k, 0) >= v:
                            continue
                        wm[k] = v
                        s = sem_e[k[1]] if k[0] == "eng" else sem_d[k[1]]
                        eo.wait_ge(s, v)

                for o in per_eng[e]:
                    do_waits(o["deps"])
                    ins = o["fn"](eo)
                    if o["dma"] is not None:
                        ins.then_inc(sem_d[o["dma"]], 16)
                    elif o["signal"]:
                        ins.then_inc(sem_e[e], 1)
                if e == "sp":
                    do_waits(list(final_events) + [("dma", k, g) for k, g in dma_gen.items()])

            @block.tensor
            def _(eng):
                run_engine("pe", eng)

            @block.scalar
            def _(eng):
                run_engine("act", eng)

            @block.vector
            def _(eng):
                run_engine("dve", eng)

            @block.gpsimd
            def _(eng):
                run_engine("pool", eng)

            @block.sync
            def _(eng):
                run_engine("sp", eng)


def K2(name):
    return [name + "_0", name + "_1"]


def build_program(stop=None):
    nc = bass.Bass("TRN2", target_bir_lowering=False)

    def din(name, shape):
        return nc.dram_tensor(name, list(shape), F32, kind="ExternalInput").ap()

    def dout(name, shape):
        return nc.dram_tensor(name, list(shape), F32, kind="ExternalOutput").ap()

    x_d = din("x", [T, D])
    smA_d = din("smA", [128, 128])
    smB_d = din("smB", [128, 128])
    ctab_d = din("ctab", [128, NCT])
    rope_d = din("rope", [128, 2, T])
    dftL_d = din("dftL", [2, T, T])
    s0r_d = din("s0r", [DEPTH, 2, 3, 128, 128])
    s0h_d = din("s0h", [DEPTH, 2, 3, 128, 128])
    wmod_d = din("w_mod", [DEPTH, D, 6 * D])
    win_d = din("w_in", [DEPTH, D, 3712])
    wout_d = din("w_out", [DEPTH, D, D])
    wg_d = din("w_gate", [DEPTH, D, DFF])
    wu_d = din("w_up", [DEPTH, D, DFF])
    wd_d = din("w_down", [DEPTH, DFF, D])
    y_d = dout("y", [T, D])
    osr_d = dout("osr", [DEPTH, 2, 4, 3, 128, 128])
    osh_d = dout("osh", [DEPTH, 2, 4, 3, 128, 128])

    P = Prog(nc)
    st = contextlib.ExitStack()
    with st:
        def sb(name, shape, dt):
            return st.enter_context(nc.sbuf_tensor(name, list(shape), dt))

        def ps(name, shape, dt):
            return st.enter_context(nc.psum_tensor(name, list(shape), dt))

        xT = sb("xT", [128, 8, T], F32)
        ctab = sb("ctab_sb", [128, NCT], F32)
        colsA = sb("colsA", [128, 128], F32)
        colsB = sb("colsB", [128, 128], F32)
        wring = sb("wring", [128, NU, WUNIT], BF16)
        pool = sb("pool", [128, NPOOL, T], BF16)
        t32 = sb("t32", [128, 5, T], F32)
        smt = t32[:, 4, 0:256]
        vbuf = sb("vbuf", [128, 8, 128], BF16)
        vh = sb("vh", [128, 2, 8, 128], BF16)
        vblk = sb("vblk", [128, 8, 4, 128], BF16)
        sst = sb("sst", [128, 2, 8, 128], F32)
        sbfall = sb("sbfall", [128, 2, 32, 128], BF16)
        pbuf = sb("pbuf", [128, 2, 4, 128], BF16)
        modt_t = sb("modt", [128, 2, 64], F32)
        scb = sb("scb", [128, 8], BF16)
        misc = sb("misc", [128, 160], F32)
        dect = sb("dect", [128, 2, 6, 128], F32)
        dmt = sb("dmt", [128, 2, 2, 32], F32)
        onesD = sb("onesD", [128, 128], BF16)
        bd64 = sb("bd64", [128, 128], BF16)
        identb = sb("identb", [128, 128], BF16)
        dftcb = sb("dftcb", [128, 2, 128], BF16)
        sgtb = sb("sgtb", [128, 2, 512], BF16)
        PB = [ps("PB%d" % i, [128, 512], F32) for i in range(4)]
        PS4 = ps("PS4", [128, 512], F32)
        PT5 = ps("PT5", [128, 1024], BF16)
        PU = [ps("PU%d" % i, [128, 512], F32) for i in range(2)]

        ident = ctab[:, C_ID:C_ID + 128]
        PT5f = PT5[:, :].bitcast(F32)

        def pbuf_(i):
            return pool[:, i, :]

        def tt(i):
            return t32[:, i, :]

        wloads = []
        wstate = dict(issued=0, pos=0)
        wflat = wring[:, :, :].rearrange("p a b -> p (a b)")
        unit_occ = [None] * NU
        slot_of = {}
        wdone_flags = {}
        cur_units = {}

        def wsl(ws, a_, b_):
            return wflat[:, ws * WUNIT + a_:ws * WUNIT + b_]

        def WK(ws):
            return ["W%d" % (ws + i) for i in range(cur_units[ws])]

        def wreq(make, nu=2):
            wloads.append((nu, make))
            return len(wloads) - 1

        def _pump():
            while wstate["issued"] < len(wloads):
                j = wstate["issued"]
                nu, make = wloads[j]
                pos = wstate["pos"]
                if pos + nu > NU:
                    pos = 0
                ok = all(unit_occ[u] is None or wdone_flags.get(unit_occ[u], False) for u in range(pos, pos + nu))
                if not ok:
                    return
                for u in range(pos, pos + nu):
                    unit_occ[u] = j
                slot_of[j] = pos
                for (o_ap, i_ap) in make(pos):
                    P.op("pool", (lambda e, o_ap=o_ap, i_ap=i_ap: e.dma_start(out=o_ap, in_=i_ap)),
                         writes=["W%d" % u for u in range(pos, pos + nu)], dma="W%d" % pos)
                wstate["pos"] = (pos + nu) % NU
                wstate["issued"] += 1

        def wuse(i):
            _pump()
            assert wstate["issued"] > i, (i, wstate["issued"])
            cur_units[slot_of[i]] = wloads[i][0]
            return slot_of[i]

        def wdone(i):
            wdone_flags[i] = True
            _pump()

        def wview(s, k, n):
            return wsl(s, 0, k * n).rearrange("p (k n) -> p k n", n=n)

        def mk_std(dram2d, c0, n):
            def make(s):
                return [(wview(s, 8, n), dram2d[:, c0:c0 + n].rearrange("(k p) n -> p k n", p=128))]
            return make

        def mk_grp(dram2d, c0, ng, stride):
            def make(s):
                res = []
                v = wview(s, 8, ng * 128)
                for g in range(ng):
                    res.append((v[:, :, g * 128:(g + 1) * 128],
                                dram2d[:, c0 + g * stride:c0 + g * stride + 128].rearrange("(k p) n -> p k n", p=128)))
                return res
            return make

        def mk_down(dram2d, c0):
            def make(s):
                return [(wview(s, NFF, 128), dram2d[:, c0:c0 + 128].rearrange("(k p) n -> p k n", p=128))]
            return make

        WI = {}

        def reg_mods(lm, js):
            for j in js:
                WI[("mod", lm, j)] = wreq(mk_std(wmod_d[lm], j * 512, 512))

        def slot_mods(l, i):
            if i <= 3:
                return [(l, 4 + 2 * i), (l, 5 + 2 * i)]
            if i <= 5 and l + 1 < DEPTH:
                return [(l + 1, 2 * (i - 4)), (l + 1, 2 * (i - 4) + 1)]
            return []

        reg_mods(0, range(4))
        for l in range(DEPTH):
            WI[("fnet", l)] = wreq(mk_std(win_d[l], 0, 256), 1)
            for (lm, j) in slot_mods(l, 0):
                reg_mods(lm, [j])
            for tbl in range(2):
                for ob in range(2):
                    WI[("dft", l, tbl, ob)] = wreq(mk_std(dftL_d[tbl], ob * 512, 512))
            for p in range(3):
                WI[("ret", l, p)] = wreq(mk_grp(win_d[l], 256 + 128 * p, 4, 384))
                for (lm, j) in slot_mods(l, 1 + p):
                    reg_mods(lm, [j])
            for p in range(3):
                WI[("hg", l, p)] = wreq(mk_grp(win_d[l], 1792 + 128 * p, 5, 384))
                for (lm, j) in slot_mods(l, 4 + p):
                    reg_mods(lm, [j])
            for ob in range(2):
                WI[("out", l, ob)] = wreq(mk_std(wout_d[l], ob * 512, 512))
            for fb in range(11):
                WI[("gate", l, fb)] = wreq(mk_std(wg_d[l], fb * 256, 256), 1)
                WI[("up", l, fb)] = wreq(mk_std(wu_d[l], fb * 256, 256), 1)
            for dc in range(8):
                WI[("down", l, dc)] = wreq(mk_down(wd_d[l], dc * 128))

        P.op("sp", lambda e: e.dma_start(out=ctab[:], in_=ctab_d), writes=["ctab"], dma="c0")
        P.op("sp", lambda e: e.dma_start(out=smt[:, 0:128], in_=smA_d), writes=["T4_0"], dma="c2")
        P.op("sp", lambda e: e.dma_start(out=smt[:, 128:256], in_=smB_d), writes=["T4_0"], dma="c3")
        P.op("dve", lambda e: e.memset(onesD[:], 1.0 / 1024.0), writes=["onesD"])
        P.op("dve", lambda e: e.tensor_scalar(out=bd64[:], in0=ctab[:, C_BM:C_BM + 128], scalar1=1.0 / 64.0, scalar2=None, op0=ALU.mult),
             reads=["ctab"], writes=["bd64"])
        P.op("dve", lambda e: e.tensor_copy(out=identb[:], in_=ident), reads=["ctab"], writes=["identb"])
        P.op("dve", lambda e: e.tensor_copy(out=dftcb[:].rearrange("p a b -> p (a b)"), in_=ctab[:, C_DFTC:C_DFTC + 256]),
             reads=["ctab"], writes=["dftcb"])
        P.op("dve", lambda e: e.memset(sbfall[:].rearrange("p a b c -> p (a b c)"), 0.0), writes=["SBA%d_%d" % (d_, t_) for d_ in range(2) for t_ in range(8)])
        P.op("pool", lambda e: e.memset(vh[:].rearrange("p a b c -> p (a b c)"), 0.0), writes=["vh"])
        P.op("pool", lambda e: e.memset(vblk[:].rearrange("p a b c -> p (a b c)"), 0.0), writes=["vblk"])
        P.op("pe", lambda e: e.transpose(out=PB[0][:, 0:128], in_=smt[:, 0:128], identity=ident), reads=["T4_0", "ctab"], writes=["PB0"])
        P.op("pe", lambda e: e.transpose(out=PB[0][:, 128:256], in_=smt[:, 128:256], identity=ident), reads=["T4_0", "ctab"], writes=["PB0"])
        P.op("act", lambda e: e.activation(out=colsA[:], in_=PB[0][:, 0:128], func=AF.Copy), reads=["PB0"], writes=["colsA"])
        P.op("act", lambda e: e.activation(out=colsB[:], in_=PB[0][:, 128:256], func=AF.Copy), reads=["PB0"], writes=["colsB"])
        P.op("act", lambda e: e.activation(out=scb[:], in_=colsA[:, 96:104], func=AF.Silu), reads=["colsA"], writes=["scb"])
        P.op("act", lambda e: e.activation(out=misc[:, 28:40], in_=colsB[:, 76:88], func=AF.Exp), reads=["colsB"], writes=["misc_lg"])
        P.op("dve", lambda e: e.tensor_scalar(out=misc[:, 16:28], in0=misc[:, 28:40], scalar1=-1.0, scalar2=None, op0=ALU.mult),
             reads=["misc_lg"], writes=["misc_lg2"])
        P.op("act", lambda e: e.activation(out=misc[:, 40:52], in_=misc[:, 16:28], func=AF.Exp, scale=128.0), reads=["misc_lg2"], writes=["misc_D"])
        P.op("act", lambda e: e.activation(out=misc[:, 52:64], in_=colsB[:, 64:76], func=AF.Exp), reads=["colsB"], writes=["misc_e"])
        P.op("dve", lambda e: e.memset(misc[:, 64:76], 0.0), writes=["misc_lb"])
        for d_ in range(2):
            e0 = misc[:, 52 + d_ * 6:52 + d_ * 6 + 3]
            e1 = misc[:, 52 + d_ * 6 + 3:52 + d_ * 6 + 6]
            dst = misc[:, 64 + d_ * 6 + 3:64 + d_ * 6 + 6]
            tmpc = misc[:, 100 + d_ * 3:103 + d_ * 3]
            P.op("dve", lambda e, e0=e0, e1=e1, tmpc=tmpc: e.tensor_tensor(out=tmpc, in0=e0, in1=e1, op=ALU.add), reads=["misc_e"], writes=["misc_t%d" % d_])
            P.op("dve", lambda e, tmpc=tmpc: e.reciprocal(out=tmpc, in_=tmpc), reads=["misc_t%d" % d_], writes=["misc_t%d" % d_])
            P.op("dve", lambda e, e1=e1, tmpc=tmpc, dst=dst: e.tensor_tensor(out=dst, in0=e1, in1=tmpc, op=ALU.mult),
                 reads=["misc_t%d" % d_, "misc_e", "misc_lb"], writes=["misc_lb"])
        P.op("dve", lambda e: e.tensor_scalar(out=misc[:, 76:88], in0=misc[:, 64:76], scalar1=-1.0, scalar2=1.0, op0=ALU.mult, op1=ALU.add),
             reads=["misc_lb"], writes=["misc_oml"])
        P.op("dve", lambda e: e.tensor_scalar(out=misc[:, 88:100], in0=misc[:, 64:76], scalar1=-1.0, scalar2=None, op0=ALU.add),
             reads=["misc_lb"], writes=["misc_lbm1"])

        for t in range(8):
            b = t % 2
            P.op("sp", lambda e, t=t, b=b: e.dma_start(out=t32[:, b, :], in_=x_d[t * 128:(t + 1) * 128, :]),
                 writes=K2("T%d" % b), dma="xin%d" % b)
            for kh in range(2):
                bk = 2 * b + kh
                for kk in range(4):
                    k = kh * 4 + kk
                    src = t32[:, b, k * 128:(k + 1) * 128]
                    P.op("pe", lambda e, bk=bk, kk=kk, src=src: e.transpose(out=PB[bk][:, kk * 128:(kk + 1) * 128], in_=src, identity=ident),
                         reads=K2("T%d" % b) + ["ctab"], writes=["PB%d" % bk])
                P.op("act" if kh == 0 else "dve",
                     (lambda e, kh=kh, t=t, bk=bk: e.activation(out=xT[:, kh * 4:kh * 4 + 4, t * 128:(t + 1) * 128],
                                                                 in_=PB[bk][:].rearrange("p (a b) -> p a b", b=128), func=AF.Copy)) if kh == 0 else
                     (lambda e, kh=kh, t=t, bk=bk: e.tensor_copy(out=xT[:, kh * 4:kh * 4 + 4, t * 128:(t + 1) * 128],
                                                                  in_=PB[bk][:].rearrange("p (a b) -> p a b", b=128))),
                     reads=["PB%d" % bk], writes=["xT%d_%d" % (k_, t // 4) for k_ in range(kh * 4, kh * 4 + 4)])

        XK = lambda k: ["xT%d_0" % k, "xT%d_1" % k]

        def norm_stats(chunks, chunk_keys, ones_ap, ones_key, center, mean_t, rstd_t, tmpb_all):
            n = len(chunks)
            for ci, (ch, ck) in enumerate(zip(chunks, chunk_keys)):
                tmpb = tmpb_all[ci % len(tmpb_all)]
                xb, xq = pbuf_(tmpb[0]), pbuf_(tmpb[1])
                if n == 1:
                    for tb in range(2):
                        hsl = slice(tb * 512, (tb + 1) * 512)
                        if center:
                            P.op("act", lambda e, ch=ch, xb=xb, hsl=hsl: e.activation(out=xb[:, hsl], in_=ch[:, hsl], func=AF.Copy),
                                 reads=[ck[tb]], writes=["B%d_%d" % (tmpb[0], tb)])
                        P.op("dve", lambda e, ch=ch, xq=xq, hsl=hsl: e.tensor_tensor(out=xq[:, hsl], in0=ch[:, hsl], in1=ch[:, hsl], op=ALU.mult),
                             reads=[ck[tb]], writes=["B%d_%d" % (tmpb[1], tb)])
                else:
                    if center:
                        P.op("act", lambda e, ch=ch, xb=xb: e.activation(out=xb, in_=ch, func=AF.Copy), reads=ck, writes=K2("B%d" % tmpb[0]))
                    P.op("dve", lambda e, ch=ch, xq=xq: e.tensor_tensor(out=xq, in0=ch, in1=ch, op=ALU.mult), reads=ck, writes=K2("B%d" % tmpb[1]))
                for tb in range(2):
                    if center:
                        P.op("pe", lambda e, tb=tb, xb=xb, ci=ci: e.matmul(PB[tb][:], lhsT=ones_ap, rhs=xb[:, tb * 512:(tb + 1) * 512],
                                                                         start=(ci == 0), stop=(ci == n - 1)),
                             reads=["B%d_%d" % (tmpb[0], tb), ones_key], writes=["PB%d" % tb])
                    P.op("pe", lambda e, tb=tb, xq=xq, ci=ci: e.matmul(PB[2 + tb][:], lhsT=ones_ap, rhs=xq[:, tb * 512:(tb + 1) * 512],
                                                                     start=(ci == 0), stop=(ci == n - 1)),
                         reads=["B%d_%d" % (tmpb[1], tb), ones_key], writes=["PB%d" % (2 + tb)])
            mean, rstd = tt(mean_t), tt(rstd_t)
            SL = [slice(0, 512), slice(512, 1024)]
            MKk = ["T%d_%d" % (mean_t, tb) for tb in range(2)]
            RKk = ["T%d_%d" % (rstd_t, tb) for tb in range(2)]
            if center:
                for tb in range(2):
                    P.op("act", lambda e, tb=tb: e.activation(out=mean[:, SL[tb]], in_=PB[tb][:], func=AF.Copy),
                         reads=["PB%d" % tb], writes=[MKk[tb]])
                for tb in range(2):
                    P.op("dve", lambda e, tb=tb: e.tensor_tensor(out=rstd[:, SL[tb]], in0=mean[:, SL[tb]], in1=mean[:, SL[tb]], op=ALU.mult),
                         reads=[MKk[tb]], writes=[RKk[tb]])
                for tb in range(2):
                    P.op("dve", lambda e, tb=tb: e.tensor_tensor(out=rstd[:, SL[tb]], in0=PB[2 + tb][:], in1=rstd[:, SL[tb]], op=ALU.subtract),
                         reads=["PB%d" % (2 + tb), RKk[tb]], writes=[RKk[tb]])
                for tb in range(2):
                    P.op("act", lambda e, tb=tb: e.activation(out=rstd[:, SL[tb]], in_=rstd[:, SL[tb]], func=AF.Ln, bias=epsc, scale=1.0),
                         reads=[RKk[tb], "epsc"], writes=[RKk[tb]])
            else:
                for tb in range(2):
                    P.op("act", lambda e, tb=tb: e.activation(out=rstd[:, SL[tb]], in_=PB[2 + tb][:], func=AF.Ln, bias=epsc, scale=1.0),
                         reads=["PB%d" % (2 + tb), "epsc"], writes=[RKk[tb]])
            for tb in range(2):
                P.op("act", lambda e, tb=tb: e.activation(out=rstd[:, SL[tb]], in_=rstd[:, SL[tb]], func=AF.Exp, scale=-0.5),
                     reads=[RKk[tb]], writes=[RKk[tb]])

        epsc = misc[:, 110:111]
        P.op("dve", lambda e: e.memset(epsc, LN_EPS), writes=["epsc"])
        c80p = misc[:, 112:113]
        c80n = misc[:, 113:114]
        P.op("dve", lambda e: e.memset(c80p, 80.0), writes=["c80"])
        P.op("dve", lambda e: e.memset(c80n, -80.0), writes=["c80"])
        ln8c = misc[:, 111:112]
        P.op("dve", lambda e: e.memset(ln8c, LN8), writes=["ln8c"])

        def ln_apply(scale_cols, bias_cols, col_keys, dst_fn, dst_keys_fn, tmp_t):
            mean, rstd = tt(0), tt(1)
            for kp in range(4):
                ks_ = (2 * kp, 2 * kp + 1)
                tqs = {k: tmp_t + (k % 2) for k in ks_}
                for k in ks_:
                    tq = tqs[k]
                    tmp = tt(tq)
                    P.op("dve", lambda e, k=k, tmp=tmp: e.tensor_tensor(out=tmp, in0=xT[:, k, :], in1=mean, op=ALU.subtract),
                         reads=XK(k) + K2("T0"), writes=K2("T%d" % tq))
                for k in ks_:
                    tq = tqs[k]
                    tmp = tt(tq)
                    P.op("dve", lambda e, tmp=tmp: e.tensor_tensor(out=tmp, in0=tmp, in1=rstd, op=ALU.mult),
                         reads=K2("T%d" % tq) + K2("T1"), writes=K2("T%d" % tq))
                for k in ks_:
                    tq = tqs[k]
                    tmp = tt(tq)
                    P.op("act", lambda e, k=k, tmp=tmp: e.activation(out=dst_fn(k), in_=tmp, func=AF.Identity, scale=scale_cols[:, k:k + 1], bias=bias_cols[:, k:k + 1]),
                         reads=K2("T%d" % tq) + col_keys, writes=dst_keys_fn(k))

        def layer_norm_x(scale_cols, bias_cols, col_keys, dst_fn, dst_keys_fn):
            norm_stats([xT[:, k, :] for k in range(8)], [XK(k) for k in range(8)], onesD[:], "onesD", True, 0, 1, [(26, 27), (28, 29)])
            ln_apply(scale_cols, bias_cols, col_keys, dst_fn, dst_keys_fn, 2)

        HB = list(range(0, 8))
        MC = list(range(8, 16))
        OPB = list(range(16, 25))
        AB = list(range(8, 30))

        def fm_proj(ws, n_, c0, tb, bank):
            for k in range(8):
                P.op("pe", lambda e, k=k: e.matmul(PB[bank][:], lhsT=wsl(ws, k * n_ + c0, k * n_ + c0 + 128),
                                                   rhs=pool[:, HB[k], tb * 512:(tb + 1) * 512], start=(k == 0), stop=(k == 7)),
                     reads=[*WK(ws), "B%d_%d" % (HB[k], tb)], writes=["PB%d" % bank])

        def fm_proj2(ws, n_, c0, tb, bank_ap, bank_key):
            for k in range(8):
                P.op("pe", lambda e, k=k: e.matmul(bank_ap, lhsT=wsl(ws, k * n_ + c0, k * n_ + c0 + 128),
                                                   rhs=pool[:, HB[k], tb * 512:(tb + 1) * 512], start=(k == 0), stop=(k == 7)),
                     reads=[*WK(ws), "B%d_%d" % (HB[k], tb)], writes=[bank_key])

        def tm_proj_tile(ws, n, c0, ncols, t, out_ap, out_key):
            for k in range(8):
                P.op("pe", lambda e, k=k: e.matmul(out_ap, lhsT=pool[:, HB[k], t * 128:(t + 1) * 128],
                                                   rhs=wsl(ws, k * n + c0, k * n + c0 + ncols), start=(k == 0), stop=(k == 7)),
                     reads=[*WK(ws), "B%d_%d" % (HB[k], t // 4)], writes=[out_key])

        def gla_pair(l, p, csz, qq, kkh, ks, masks, s0_d, os_d, mcol_off, gate_b, kind, extra_w=()):
            n = 128 // csz
            G = 8 * n
            cps = 256 // csz
            oacc = tt(0)
            cur = {}
            um = tt(1)
            for d in range(2):
                g0 = 0 if d == 0 else G - 1
                P.op("sp", lambda e, d=d: e.dma_start(out=sst[:, d, 1, :], in_=s0_d[l, d, p]), writes=["S%d_1" % d], dma="s0_%d" % d)
                P.op("act", lambda e, d=d, g0=g0: e.activation(out=sbfall[:, d, g0, :], in_=sst[:, d, 1, :], func=AF.Copy),
                     reads=["S%d_1" % d], writes=["SBA%d_%d" % (d, g0 // n)] + list(extra_w))
                cur[d] = 1
            UB = [[(PU[0], "PU0"), (PS4, "PS4")], [(PU[1], "PU1"), (PB[3], "PB3")]]
            for s in range(8):
                tiles = [s, 7 - s]
                for d in range(2):
                    t = tiles[d]
                    hk = "_%d" % (t // 4)
                    if csz == 32:
                        ub_t, ub_k = UB[d][s % 2]
                        for c_ in range(4):
                            for hh in range(2):
                                P.op("pe", lambda e, d=d, t=t, hh=hh, c_=c_, ub_t=ub_t: e.matmul(
                                        ub_t[:, c_ * 128 + hh * 64:c_ * 128 + (hh + 1) * 64],
                                        lhsT=pool[:, ks[d][hh], t * 128:(t + 1) * 128],
                                        rhs=vblk[:, t, c_, hh * 64:(hh + 1) * 64], start=True, stop=True),
                                     reads=["B%d%s" % (ks[d][hh], hk), "vblk"], writes=[ub_k])
                    else:
                        P.op("pe", lambda e, d=d, t=t: e.matmul(PU[d][:, 0:128], lhsT=pool[:, ks[d], t * 128:(t + 1) * 128],
                                                                rhs=vbuf[:, t, :], start=True, stop=True),
                             reads=["B%d%s" % (ks[d], hk), "vbuf"], writes=["PU%d" % d])
                for d in range(2):
                    if csz != 32:
                        P.op("dve", lambda e, d=d: e.tensor_tensor(out=um[:, d * 512:d * 512 + 128], in0=PU[d][:, 0:128],
                                                                   in1=ctab[:, C_BM:C_BM + 128], op=ALU.mult),
                             reads=["PU%d" % d, "ctab"], writes=["T1_%d" % d])
                for ci in range(n):
                    for d in range(2):
                        t = tiles[d]
                        c = ci if d == 0 else n - 1 - ci
                        g = t * n + c
                        si = cur[d]
                        if d == 0:
                            fin = ((g + 1) % cps == 0)
                            seq = (g + 1) // cps - 1
                            nxt_boundary = fin and (g != G - 1)
                            last = (g == G - 1)
                            gn = g + 1
                        else:
                            fin = (g % cps == 0)
                            seq = g // cps
                            nxt_boundary = fin and (g != 0)
                            last = (g == 0)
                            gn = g - 1
                        ni = (4 + seq) if fin else ((si + 1) % 4 if si < 4 else 0)
                        dcol = dmt[:, mcol_off, d, g:g + 1]
                        if csz == 32:
                            ub_t, ub_k = UB[d][s % 2]
                            ucol, ukey = ub_t[:, c * 128:(c + 1) * 128], ub_k
                        else:
                            ucol, ukey = um[:, d * 512:d * 512 + 128], "T1_%d" % d
                        P.op("dve", lambda e, d=d, si=si, ni=ni, dcol=dcol, ucol=ucol: e.scalar_tensor_tensor(
                                out=sst[:, d, ni, :], in0=sst[:, d, si, :], scalar=dcol, in1=ucol, op0=ALU.mult, op1=ALU.add),
                             reads=["S%d_%d" % (d, si), "dmt%d_%d" % (mcol_off, d), ukey], writes=["S%d_%d" % (d, ni)])
                        if fin:
                            P.op("sp", lambda e, d=d, ni=ni, seq=seq: e.dma_start(out=os_d[l, d, seq, p], in_=sst[:, d, ni, :]),
                                 reads=["S%d_%d" % (d, ni)], writes=["os_%s_%d_%d_%d_%d" % (kind, l, d, seq, p)],
                                 dma="os%d_%d" % (d, ni))
                        if not last:
                            mk = C_MB if nxt_boundary else C_MB + 1
                            P.op("act", lambda e, d=d, ni=ni, gn=gn, mk=mk: e.activation(out=sbfall[:, d, gn, :], in_=sst[:, d, ni, :],
                                                                                      func=AF.Identity, scale=ctab[:, mk:mk + 1]),
                                 reads=["S%d_%d" % (d, ni), "ctab"], writes=["SBA%d_%d" % (d, gn // n)])
                        cur[d] = ni
            SCB = [[PS4, PB[3]], [PB[2], PU[0]]]
            SCK = [["PS4", "PB3"], ["PB2", "PU0"]]
            OAB, OAK = [PB[0], PB[1]], ["PB0", "PB1"]

            def scores(t):
                tsl = slice(t * 128, (t + 1) * 128)
                hk = "_%d" % (t // 4)
                par = t % 2
                for d in range(2):
                    for h in range(2):
                        P.op("pe", lambda e, d=d, h=h, par=par, tsl=tsl: e.matmul(SCB[d][par][:, h * 128:(h + 1) * 128], lhsT=pool[:, kkh[d][h], tsl],
                                                                                 rhs=pool[:, qq[d], tsl], start=True, stop=True),
                             reads=["B%d%s" % (kkh[d][h], hk), "B%d%s" % (qq[d], hk)], writes=[SCK[d][par]])
                for d in range(2):
                    P.op("dve", lambda e, d=d, par=par: e.tensor_tensor(
                            out=pbuf[:, par, 2 * d:2 * d + 2, :], in0=SCB[d][par][:, 0:256].rearrange("p (h c) -> p h c", h=2),
                            in1=ctab[:, masks[d]:masks[d] + 128].unsqueeze(1).to_broadcast([128, 2, 128]), op=ALU.mult),
                         reads=[SCK[d][par], "ctab"], writes=["pbuf%d_%d" % (par, d)])

            def outputs(t):
                tsl = slice(t * 128, (t + 1) * 128)
                hk = "_%d" % (t // 4)
                par = t % 2
                osl = OAB[par][:, 0:128]
                first = True
                for d in range(2):
                    for h in range(2):
                        P.op("pe", lambda e, d=d, h=h, t=t, par=par, osl=osl, first=first: e.matmul(osl, lhsT=vh[:, h, t, :], rhs=pbuf[:, par, 2 * d + h, :],
                                                                                                start=first, stop=False),
                             reads=["vh", "pbuf%d_%d" % (par, d)], writes=[OAK[par]])
                        first = False
                for d in range(2):
                    for c in range(n):
                        g = t * n + c
                        csl = slice(t * 128 + c * csz, t * 128 + (c + 1) * csz)
                        lastmm = (d == 1 and c == n - 1)
                        P.op("pe", lambda e, d=d, g=g, c=c, csl=csl, par=par, lastmm=lastmm: e.matmul(
                                OAB[par][:, c * csz:(c + 1) * csz], lhsT=sbfall[:, d, g, :], rhs=pool[:, qq[d], csl],
                                start=False, stop=lastmm),
                             reads=["SBA%d_%d" % (d, t), "B%d%s" % (qq[d], hk)], writes=[OAK[par]])
                P.op("act", lambda e, osl=osl, tsl=tsl: e.activation(out=oacc[:, tsl], in_=osl, func=AF.Copy),
                     reads=[OAK[par]], writes=["T0%s" % hk])

            for t in range(9):
                if t < 8:
                    scores(t)
                if t >= 1:
                    outputs(t - 1)

        for l in range(DEPTH):
            mpar = l % 2
            modt = modt_t[:, mpar, :]
            MTK = lambda js: ["mt%d_%d" % (mpar, j) for j in js]

            def mod_block(lm, j12, bank, bank_key):
                ws_ = wuse(WI[("mod", lm, j12)])
                for jj in range(4):
                    for k in range(8):
                        P.op("pe", lambda e, ws_=ws_, jj=jj, k=k: e.matmul(bank[:, jj:jj + 1], lhsT=wsl(ws_, k * 512 + jj * 128, k * 512 + (jj + 1) * 128),
                                                                          rhs=scb[:, k:k + 1], start=(k == 0), stop=(k == 7)),
                             reads=[*WK(ws_), "scb"], writes=[bank_key])
                wdone(WI[("mod", lm, j12)])
                P.op("dve", lambda e: e.tensor_tensor(out=modt_t[:, lm % 2, 4 * j12:4 * j12 + 4], in0=bank[:, 0:4],
                                                      in1=colsA[:, lm * 48 + 4 * j12:lm * 48 + 4 * j12 + 4], op=ALU.add),
                     reads=[bank_key, "colsA"], writes=["mt%d_%d" % (lm % 2, j12)])

            def run_slot_mods(i):
                for (lm, j) in slot_mods(l, i):
                    mod_block(lm, j, PU[1], "PU1")

            if l == 0:
                for j12 in range(4):
                    mod_block(0, j12, PB[j12 % 2], "PB%d" % (j12 % 2))
            P.op("dve", lambda e, modt=modt: e.tensor_scalar(out=modt[:, 48:56], in0=modt[:, 8:16], scalar1=1.0, scalar2=None, op0=ALU.add),
                 reads=MTK([2, 3]), writes=["ma1_%d" % mpar])
            layer_norm_x(modt[:, 48:56], modt[:, 0:8], MTK([0, 1]) + ["ma1_%d" % mpar], lambda k: pbuf_(HB[k]), lambda k: K2("B%d" % HB[k]))

            ws = wuse(WI[("fnet", l)])
            ub_, pc_ = OPB[0:2], OPB[2:6]
            uview = pool[:, ub_[0]:ub_[0] + 2, :].rearrange("p a b -> p (a b)").rearrange("p (t c) -> p t c", c=256)
            UK = K2("B%d" % ub_[0]) + K2("B%d" % ub_[1])
            for t in range(8):
                bank = t % 2
                tm_proj_tile(ws, 256, 0, 256, t, PB[bank][:, 0:256], "PB%d" % bank)
                P.op("act", lambda e, t=t, bank=bank: e.activation(out=uview[:, t, :], in_=PB[bank][:, 0:256], func=AF.Copy),
                     reads=["PB%d" % bank], writes=UK)
            wdone(WI[("fnet", l)])
            run_slot_mods(0)
            for tbl in range(2):
                for ob in range(2):
                    ws = wuse(WI[("dft", l, tbl, ob)])
                    for ct in range(2):
                        bank = 2 + ct
                        for kt in range(8):
                            P.op("pe", lambda e, ws=ws, ct=ct, kt=kt, bank=bank: e.matmul(PB[bank][:], lhsT=uview[:, kt, ct * 128:(ct + 1) * 128],
                                                                                          rhs=wsl(ws, kt * 512, (kt + 1) * 512), start=(kt == 0), stop=(kt == 7)),
                                 reads=UK + [*WK(ws)], writes=["PB%d" % bank])
                        dstb = pc_[tbl * 2 + ct]
                        P.op("act", lambda e, bank=bank, dstb=dstb, ob=ob: e.activation(out=pool[:, dstb, ob * 512:(ob + 1) * 512], in_=PB[bank][:], func=AF.Copy),
                             reads=["PB%d" % bank], writes=["B%d_%d" % (dstb, ob)])
                    wdone(WI[("dft", l, tbl, ob)])
            for ct in range(2):
                for ob in range(2):
                    bank = ob
                    for tbl in range(2):
                        srcb = pc_[tbl * 2 + ct]
                        P.op("pe", lambda e, tbl=tbl, srcb=srcb, ob=ob, bank=bank: e.matmul(PB[bank][:], lhsT=dftcb[:, tbl, :], rhs=pool[:, srcb, ob * 512:(ob + 1) * 512],
                                                                                          start=(tbl == 0), stop=(tbl == 1)),
                             reads=["dftcb", "B%d_%d" % (srcb, ob)], writes=["PB%d" % bank])
                    P.op("act", lambda e, ct=ct, ob=ob, bank=bank: e.activation(out=pool[:, MC[ct], ob * 512:(ob + 1) * 512], in_=PB[bank][:], func=AF.Copy),
                         reads=["PB%d" % bank], writes=["B%d_%d" % (MC[ct], ob)])

            def v_proj(ws, n, c0, need_blk):
                for t in range(8):
                    bank = t % 2
                    tm_proj_tile(ws, n, c0, 128, t, PB[bank][:, 0:128], "PB%d" % bank)
                    P.op("act", lambda e, t=t, bank=bank: e.activation(out=vbuf[:, t, :], in_=PB[bank][:, 0:128], func=AF.Copy),
                         reads=["PB%d" % bank], writes=["vbuf"])
                for h in range(2):
                    P.op("act", lambda e, h=h: e.activation(out=vh[:, h, :, h * 64:(h + 1) * 64], in_=vbuf[:, :, h * 64:(h + 1) * 64], func=AF.Copy),
                         reads=["vbuf"], writes=["vh"])
                if need_blk:
                    for c in range(4):
                        P.op("dve", lambda e, c=c: e.tensor_copy(out=vblk[c * 32:(c + 1) * 32, :, c, :], in_=vbuf[c * 32:(c + 1) * 32, :, :]),
                             reads=["vbuf"], writes=["vblk"])

            def v_proj2(ws, n, c0, need_blk):
                for half in range(2):
                    for tq in range(4):
                        t = half * 4 + tq
                        tm_proj_tile(ws, n, c0, 128, t, PU[1][:, tq * 128:(tq + 1) * 128], "PU1")
                    P.op("act", lambda e, half=half: e.activation(out=vbuf[:, half * 4:(half + 1) * 4, :],
                                                                  in_=PU[1][:].rearrange("p (a b) -> p a b", b=128), func=AF.Copy),
                         reads=["PU1"], writes=["vbuf"])
                for h in range(2):
                    P.op("act", lambda e, h=h: e.activation(out=vh[:, h, :, h * 64:(h + 1) * 64], in_=vbuf[:, :, h * 64:(h + 1) * 64], func=AF.Copy),
                         reads=["vbuf"], writes=["vh"])
                if need_blk:
                    for c in range(4):
                        P.op("dve", lambda e, c=c: e.tensor_copy(out=vblk[c * 32:(c + 1) * 32, :, c, :], in_=vbuf[c * 32:(c + 1) * 32, :, :]),
                             reads=["vbuf"], writes=["vblk"])

            def finish_pair(kind, gate_b, mc_idx):
                oacc = tt(0)
                center = (kind == "ret")
                norm_stats([oacc], [K2("T0")], bd64[:], "bd64", center, 3, 4, [(26, 27)])
                tmp = tt(2)
                HSL = [slice(0, 512), slice(512, 1024)]
                if center:
                    for tb in range(2):
                        P.op("dve", lambda e, tb=tb: e.tensor_tensor(out=tmp[:, HSL[tb]], in0=oacc[:, HSL[tb]], in1=tt(3)[:, HSL[tb]], op=ALU.subtract),
                             reads=["T0_%d" % tb, "T3_%d" % tb], writes=["T2_%d" % tb])
                    for tb in range(2):
                        P.op("dve", lambda e, tb=tb: e.tensor_tensor(out=tmp[:, HSL[tb]], in0=tmp[:, HSL[tb]], in1=tt(4)[:, HSL[tb]], op=ALU.mult),
                             reads=["T2_%d" % tb, "T4_%d" % tb], writes=["T2_%d" % tb])
                else:
                    for tb in range(2):
                        P.op("dve", lambda e, tb=tb: e.tensor_tensor(out=tmp[:, HSL[tb]], in0=oacc[:, HSL[tb]], in1=tt(4)[:, HSL[tb]], op=ALU.mult),
                             reads=["T0_%d" % tb, "T4_%d" % tb], writes=["T2_%d" % tb])
                for tb in range(2):
                    P.op("dve", lambda e, tb=tb: e.tensor_tensor(out=pool[:, MC[mc_idx], HSL[tb]], in0=tmp[:, HSL[tb]], in1=pool[:, gate_b, HSL[tb]], op=ALU.mult),
                         reads=["T2_%d" % tb, "B%d_%d" % (gate_b, tb)], writes=["B%d_%d" % (MC[mc_idx], tb)])

            def ks_transposes(src_b, dst_b_list, evac):
                for t in range(8):
                    P.op("pe", lambda e, t=t: e.transpose(out=PT5[:, (t % 8) * 128:(t % 8 + 1) * 128], in_=pool[:, src_b, t * 128:(t + 1) * 128], identity=identb[:]),
                         reads=["B%d_%d" % (src_b, t // 4), "identb"], writes=["PT5"])
                for t in range(8):
                    evac(t, PT5[:, (t % 8) * 128:(t % 8 + 1) * 128], "PT5")

            for d in range(2):
                for h in range(2):
                    bi_ = OPB[2 + d * 2 + h]
                    P.op("dve", lambda e, h=h, bi_=bi_: e.memset(pool[(1 - h) * 64:(2 - h) * 64, bi_, :], 0.0), writes=K2("B%d" % bi_))
            def ret_tables(p_, par):
                for d in range(2):
                    cidx = l * 6 + d * 3 + p_
                    lgc = misc[:, 16 + cidx:17 + cidx]
                    nlgc = misc[:, 28 + cidx:29 + cidx]
                    pos = ctab[:, C_POSP1:C_POSP1 + 128] if d == 0 else ctab[:, C_POSREV:C_POSREV + 128]
                    P.op("act", lambda e, d=d, lgc=lgc, pos=pos: e.activation(out=dect[:, par, 2 * d, :], in_=pos, func=AF.Exp, scale=lgc),
                         reads=["ctab", "misc_lg2"], writes=["dect%d_%d" % (par, 2 * d)])
                    P.op("act", lambda e, d=d, nlgc=nlgc, pos=pos: e.activation(out=dect[:, par, 2 * d + 1, :], in_=pos, func=AF.Exp, scale=nlgc, bias=ln8c),
                         reads=["ctab", "misc_lg", "ln8c"], writes=["dect%d_%d" % (par, 2 * d + 1)])
                    pz = ctab[:, C_PZF:C_PZF + 128] if d == 0 else ctab[:, C_PZB:C_PZB + 128]
                    P.op("act", lambda e, d=d, lgc=lgc, pz=pz: e.activation(out=dect[:, par, 4 + d, :], in_=pz, func=AF.Exp, scale=lgc, bias=ln8c),
                         reads=["ctab", "misc_lg2", "ln8c"], writes=["dect%d_%d" % (par, 4 + d)])
                    P.op("pe", lambda e, d=d: e.transpose(out=PT5f[:, d * 128:(d + 1) * 128], in_=dect[:, par, 4 + d, :], identity=ident),
                         reads=["dect%d_%d" % (par, 4 + d), "ctab"], writes=["PT5"])
                for d in range(2):
                    cidx = l * 6 + d * 3 + p_
                    P.op("act", lambda e, d=d: e.activation(out=dect[:, par, 4 + d, :], in_=PT5f[:, d * 128:(d + 1) * 128], func=AF.Copy),
                         reads=["PT5"], writes=["dect%d_%d" % (par, 4 + d)])
                    P.op("dve", lambda e, d=d, cidx=cidx: e.tensor_tensor(out=dmt[:, par, d, 0:8], in0=misc[:, 40 + cidx:41 + cidx].to_broadcast([128, 8]),
                                                                         in1=ctab[:, (C_MRF if d == 0 else C_MRB):(C_MRF if d == 0 else C_MRB) + 8], op=ALU.mult),
                         reads=["misc_D", "ctab"], writes=["dmt%d_%d" % (par, d)])

            ret_tables(0, 0)
            for p in range(3):
                ws = wuse(WI[("ret", l, p)])
                qq = [OPB[0], OPB[1]]
                kkh = [[OPB[2], OPB[3]], [OPB[4], OPB[5]]]
                ks = [OPB[6], OPB[7]]
                gate_b = OPB[8]
                par = p % 2
                wv = wsl(ws, 0, 4096).rearrange("p (k g a c) -> p k g a c", k=8, g=16, a=2)
                wsw = pool[:, 28:30, :].rearrange("p a b -> p (a b)").rearrange("p (k n) -> p k n", n=256)
                sv = wsw.rearrange("p k (g a c) -> p k g a c", g=8, a=2)
                for a in range(2):
                    P.op("act", lambda e, a=a, wv=wv, sv=sv: e.activation(out=sv[:, :, :, a, :], in_=wv[:, :, 0:8, 1 - a, :], func=AF.Copy),
                         reads=[*WK(ws)], writes=K2("B28") + K2("B29"))
                rope = t32[:, 0:2, :]
                P.op("sp", lambda e: e.dma_start(out=t32[:, 0:2, :], in_=rope_d), writes=K2("T0") + K2("T1"), dma="c1")
                for which in range(2):
                    rot = tt(2 + which)
                    SLs = [slice(0, 512), slice(512, 1024)]
                    for tb in range(2):
                        ba, bb_ = 2 * tb, 2 * tb + 1
                        fm_proj(ws, 512, which * 128, tb, ba)
                        for k in range(8):
                            P.op("pe", lambda e, k=k, which=which, tb=tb, bb_=bb_: e.matmul(PB[bb_][:], lhsT=wsw[:, k, which * 128:(which + 1) * 128],
                                                                                        rhs=pool[:, HB[k], tb * 512:(tb + 1) * 512], start=(k == 0), stop=(k == 7)),
                                 reads=K2("B28") + K2("B29") + ["B%d_%d" % (HB[k], tb)], writes=["PB%d" % bb_])
                    for tb in range(2):
                        ba, bb_ = 2 * tb, 2 * tb + 1
                        sl = SLs[tb]
                        P.op("dve", lambda e, rot=rot, sl=sl, ba=ba: e.tensor_tensor(out=rot[:, sl], in0=PB[ba][:], in1=rope[:, 0, sl], op=ALU.mult),
                             reads=["PB%d" % ba, "T0_%d" % tb], writes=["T%d_%d" % (2 + which, tb)])
                        P.op("dve", lambda e, sl=sl, bb_=bb_: e.tensor_tensor(out=tt(4)[:, sl], in0=PB[bb_][:], in1=rope[:, 1, sl], op=ALU.mult),
                             reads=["PB%d" % bb_, "T1_%d" % tb], writes=["T4_%d" % tb])
                    for tb in range(2):
                        sl = SLs[tb]
                        P.op("dve", lambda e, rot=rot, sl=sl: e.tensor_tensor(out=rot[:, sl], in0=rot[:, sl], in1=tt(4)[:, sl], op=ALU.add),
                             reads=["T%d_%d" % (2 + which, tb), "T4_%d" % tb], writes=["T%d_%d" % (2 + which, tb)])
                qrot, krot = tt(2), tt(3)
                r3 = lambda ap: ap.rearrange("p (t c) -> p t c", c=128)
                for d in range(2):
                    eq = dect[:, par, 2 * d, :]
                    ek = dect[:, par, 2 * d + 1, :]
                    P.op("dve", lambda e, d=d, eq=eq: e.tensor_tensor(out=r3(pbuf_(qq[d])), in0=r3(qrot), in1=eq.unsqueeze(1).to_broadcast([128, 8, 128]), op=ALU.mult),
                         reads=K2("T2") + ["dect%d_%d" % (par, 2 * d)], writes=K2("B%d" % qq[d]))
                    for h in range(2):
                        hs = slice(h * 64, (h + 1) * 64)
                        P.op("dve", lambda e, d=d, h=h, hs=hs, ek=ek: e.tensor_tensor(out=r3(pool[hs, kkh[d][h], :]), in0=r3(krot[hs, :]),
                                                                                  in1=ek[hs, :].unsqueeze(1).to_broadcast([64, 8, 128]), op=ALU.mult),
                             reads=K2("T3") + ["dect%d_%d" % (par, 2 * d + 1)], writes=K2("B%d" % kkh[d][h]))
                kb = 25
                P.op("act", lambda e: e.activation(out=pbuf_(kb), in_=krot, func=AF.Copy), reads=K2("T3"), writes=K2("B%d" % kb))

                def evac_ret(t, src, skey, par=par, ks=ks):
                    for d in range(2):
                        zt = dect[:, par, 4 + d, :]
                        P.op("dve", lambda e, d=d, t=t, src=src, zt=zt, ks=ks: e.tensor_tensor(out=pool[:, ks[d], t * 128:(t + 1) * 128], in0=src, in1=zt, op=ALU.mult),
                             reads=[skey, "dect%d_%d" % (par, 4 + d)], writes=["B%d_%d" % (ks[d], t // 4)])
                ks_transposes(kb, ks, evac_ret)
                v_proj(ws, 512, 256, False)
                for tb in range(2):
                    fm_proj(ws, 512, 384, tb, 2 + tb)
                    P.op("act", lambda e, tb=tb: e.activation(out=pool[:, gate_b, tb * 512:(tb + 1) * 512], in_=PB[2 + tb][:], func=AF.Silu),
                         reads=["PB%d" % (2 + tb)], writes=["B%d_%d" % (gate_b, tb)])
                wdone(WI[("ret", l, p)])
                run_slot_mods(1 + p)
                if p < 2:
                    ret_tables(p + 1, (p + 1) % 2)
                gla_pair(l, p, 128, qq, kkh, ks, (C_RMF, C_RMB), s0r_d, osr_d, par, gate_b, "r")
                finish_pair("ret", gate_b, 2 + p)

            GBK = [(PS4[:], "PS4"), (PU[0][:], "PU0")]
            xf = sbfall[:, :, :, :].rearrange("p a b c -> p (a b c)").bitcast(F32)
            ALL_SBA = ["SBA%d_%d" % (d_, t_) for d_ in range(2) for t_ in range(8)]
            XKEYS = ["X%d_%d" % (i_, h_) for i_ in range(4) for h_ in range(2)]
            HS = [slice(0, 512), slice(512, 1024)]
            for p in range(3):
                ws = wuse(WI[("hg", l, p)])
                qq = [OPB[0], OPB[1]]
                kkh = [[OPB[2], OPB[3]], [OPB[4], OPB[5]]]
                ks = [[OPB[6], 28], [OPB[7], 29]]
                for d_ in range(2):
                    for hh_ in range(2):
                        bz = ks[d_][hh_]
                        P.op("dve", lambda e, bz=bz: e.memset(pool[:, bz, :], 0.0), writes=K2("B%d" % bz))
                gate_b = OPB[8]
                dpar = (p + 1) % 2
                TSET = [dict(sig=tt(0), kf=tt(1), bb=tt(3), einv=tt(4), K=("T0", "T1", "T3", "T4"), kb=25,
                             banks=[(PB[2][:], "PB2"), (PB[3][:], "PB3")]),
                        dict(sig=xf[:, 0:1024], kf=xf[:, 1024:2048], bb=xf[:, 2048:3072], einv=xf[:, 3072:4096],
                             K=("X0", "X1", "X2", "X3"), kb=26, banks=GBK)]
                for tb in range(2):
                    fm_proj2(ws, 640, 512, tb, GBK[tb][0], GBK[tb][1])
                    P.op("act", lambda e, tb=tb, gate_b=gate_b: e.activation(out=pool[:, gate_b, tb * 512:(tb + 1) * 512], in_=GBK[tb][0], func=AF.Silu),
                         reads=[GBK[tb][1]], writes=["B%d_%d" % (gate_b, tb)])
                v_proj2(ws, 640, 384, True)
                qf = tt(2)
                for tb in range(2):
                    fm_proj(ws, 640, 0, tb, tb)
                    P.op("act", lambda e, tb=tb: e.activation(out=qf[:, tb * 512:(tb + 1) * 512], in_=PB[tb][:], func=AF.Silu),
                         reads=["PB%d" % tb], writes=["T2_%d" % tb])
                for d in range(2):
                    for tb in range(2):
                        bk_ap, bk_key = TSET[d]["banks"][tb]
                        fm_proj2(ws, 640, 128 * (1 + d), tb, bk_ap, bk_key)
                cols = []
                for d in range(2):
                    lidx = d * 6 + l * 3 + p
                    cols.append(dict(oml=misc[:, 76 + lidx:77 + lidx], lb=misc[:, 64 + lidx:65 + lidx], lbm1=misc[:, 88 + lidx:89 + lidx],
                                     edge=(31 if d == 0 else 0)))
                DT = [(d, tb) for d in range(2) for tb in range(2)]
                for (d, tb) in DT:
                    T_, (bk_ap, bk_key) = TSET[d], TSET[d]["banks"][tb]
                    extra = ALL_SBA if (d == 1 and tb == 0) else []
                    P.op("act", lambda e, T_=T_, tb=tb, bk_ap=bk_ap: e.activation(out=T_["sig"][:, HS[tb]], in_=bk_ap, func=AF.Sigmoid),
                         reads=[bk_key], writes=["%s_%d" % (T_["K"][0], tb)] + extra)
                for (d, tb) in DT:
                    T_, C_ = TSET[d], cols[d]
                    P.op("dve", lambda e, T_=T_, C_=C_, tb=tb: e.tensor_scalar(out=T_["kf"][:, HS[tb]], in0=T_["sig"][:, HS[tb]], scalar1=C_["lbm1"], scalar2=C_["oml"],
                                                                               op0=ALU.mult, op1=ALU.add),
                         reads=["%s_%d" % (T_["K"][0], tb), "misc_lbm1", "misc_oml"], writes=["%s_%d" % (T_["K"][1], tb)])
                for (d, tb) in DT:
                    T_, C_ = TSET[d], cols[d]
                    P.op("act", lambda e, T_=T_, C_=C_, tb=tb: e.activation(out=T_["sig"][:, HS[tb]], in_=T_["sig"][:, HS[tb]], func=AF.Ln, scale=C_["oml"], bias=C_["lb"]),
                         reads=["%s_%d" % (T_["K"][0], tb), "misc_oml", "misc_lb"], writes=["%s_%d" % (T_["K"][0], tb)])
                for (d, tb) in DT:
                    T_ = TSET[d]
                    P.op("dve", lambda e, T_=T_, tb=tb: e.tensor_tensor_scan(out=T_["bb"][:, HS[tb]], data0=ctab[:, C_SEG:C_SEG + 512], data1=T_["sig"][:, HS[tb]],
                                                                             initial=0.0, op0=ALU.mult, op1=ALU.add),
                         reads=["%s_%d" % (T_["K"][0], tb), "ctab"], writes=["%s_%d" % (T_["K"][2], tb)])
                T1 = TSET[1]
                for tb in range(2):
                    b3 = T1["bb"][:, HS[tb]].rearrange("p (c s) -> p c s", s=32)
                    P.op("dve", lambda e, b3=b3: e.tensor_tensor(out=b3, in0=b3, in1=b3[:, :, 31:32].to_broadcast([128, 16, 32]), op=ALU.subtract),
                         reads=["X2_%d" % tb], writes=["X2_%d" % tb])
                for tb in range(2):
                    P.op("dve", lambda e, T1=T1, tb=tb: e.tensor_tensor(out=T1["bb"][:, HS[tb]], in0=T1["sig"][:, HS[tb]], in1=T1["bb"][:, HS[tb]], op=ALU.subtract),
                         reads=["X2_%d" % tb, "X0_%d" % tb], writes=["X2_%d" % tb])
                for (d, tb) in DT:
                    T_ = TSET[d]
                    P.op("act", lambda e, T_=T_, tb=tb: e.activation(out=T_["bb"][:, HS[tb]], in_=T_["bb"][:, HS[tb]], func=AF.Relu, bias=c80p, scale=1.0),
                         reads=["%s_%d" % (T_["K"][2], tb), "c80"], writes=["%s_%d" % (T_["K"][2], tb)])
                for (d, tb) in DT:
                    T_ = TSET[d]
                    P.op("act", lambda e, T_=T_, tb=tb: e.activation(out=T_["sig"][:, HS[tb]], in_=T_["bb"][:, HS[tb]], func=AF.Exp, bias=c80n, scale=1.0),
                         reads=["%s_%d" % (T_["K"][2], tb), "c80"], writes=["%s_%d" % (T_["K"][0], tb)])
                    P.op("act", lambda e, T_=T_, tb=tb: e.activation(out=T_["einv"][:, HS[tb]], in_=T_["bb"][:, HS[tb]], func=AF.Exp, bias=c80p, scale=-1.0),
                         reads=["%s_%d" % (T_["K"][2], tb), "c80"], writes=["%s_%d" % (T_["K"][3], tb)])
                for (d, tb) in DT:
                    T_, edge = TSET[d], cols[d]["edge"]
                    kE, kK, kI = "%s_%d" % (T_["K"][0], tb), "%s_%d" % (T_["K"][1], tb), "%s_%d" % (T_["K"][3], tb)
                    e3 = T_["sig"][:, HS[tb]].rearrange("p (c s) -> p c s", s=32)
                    i3 = T_["einv"][:, HS[tb]].rearrange("p (c s) -> p c s", s=32)
                    moff = (C_MHF if d == 0 else C_MHB) + tb * 16
                    P.op("dve", lambda e, d=d, tb=tb, e3=e3, moff=moff, edge=edge, dpar=dpar: e.tensor_tensor(out=dmt[:, dpar, d, tb * 16:(tb + 1) * 16], in0=e3[:, :, edge],
                                                                                                      in1=ctab[:, moff:moff + 16], op=ALU.mult),
                         reads=[kE, "ctab"], writes=["dmt%d_%d" % (dpar, d)])
                    P.op("dve", lambda e, d=d, tb=tb, T_=T_, qq=qq: e.tensor_tensor(out=pool[:, qq[d], HS[tb]], in0=qf[:, HS[tb]], in1=T_["sig"][:, HS[tb]], op=ALU.mult),
                         reads=["T2_%d" % tb, kE], writes=["B%d_%d" % (qq[d], tb)])
                    for h in range(2):
                        hs = slice(h * 64, (h + 1) * 64)
                        P.op("dve", lambda e, d=d, h=h, hs=hs, tb=tb, T_=T_, kkh=kkh: e.tensor_tensor(out=pool[hs, kkh[d][h], HS[tb]], in0=T_["kf"][hs, HS[tb]],
                                                                                                  in1=T_["einv"][hs, HS[tb]], op=ALU.mult),
                             reads=[kK, kI], writes=["B%d_%d" % (kkh[d][h], tb)])
                    P.op("dve", lambda e, i3=i3, e3=e3, edge=edge: e.tensor_tensor(out=i3, in0=i3, in1=e3[:, :, edge:edge + 1].to_broadcast([128, 16, 32]), op=ALU.mult),
                         reads=[kI, kE], writes=[kI])
                for (d, tb) in DT:
                    T_ = TSET[d]
                    P.op("dve", lambda e, T_=T_, tb=tb: e.tensor_tensor(out=pool[:, T_["kb"], HS[tb]], in0=T_["kf"][:, HS[tb]], in1=T_["einv"][:, HS[tb]], op=ALU.mult),
                         reads=["%s_%d" % (T_["K"][1], tb), "%s_%d" % (T_["K"][3], tb)], writes=["B%d_%d" % (T_["kb"], tb)])
                for d in range(2):
                    def evac_h(t, src, skey, d=d, ks=ks):
                        for hh in range(2):
                            P.op("act", lambda e, t=t, src=src, hh=hh: e.activation(out=pool[:, ks[d][hh], t * 128 + hh * 64:t * 128 + (hh + 1) * 64],
                                                                                in_=src[:, hh * 64:(hh + 1) * 64], func=AF.Copy),
                                 reads=[skey], writes=["B%d_%d" % (ks[d][hh], t // 4)])
                    ks_transposes(TSET[d]["kb"], ks, evac_h)
                wdone(WI[("hg", l, p)])
                run_slot_mods(4 + p)
                gla_pair(l, p, 32, qq, kkh, ks, (C_HMF, C_HMB), s0h_d, osh_d, dpar, gate_b, "h", extra_w=XKEYS)
                finish_pair("hg", gate_b, 5 + p)

            def resid_update(dc, tb, bank, gcol, gkeys):
                sl = slice(tb * 512, (tb + 1) * 512)
                tmp = tt(2)
                P.op("act", lambda e: e.activation(out=tmp[:, sl], in_=PB[bank][:], func=AF.Identity, scale=gcol),
                     reads=["PB%d" % bank] + gkeys, writes=["T2_%d" % tb])
                P.op("dve", lambda e: e.scalar_tensor_tensor(out=xT[:, dc, sl], in0=xT[:, dc, sl], scalar=float(ALPHA), in1=tmp[:, sl], op0=ALU.mult, op1=ALU.add),
                     reads=["xT%d_%d" % (dc, tb), "T2_%d" % tb], writes=["xT%d_%d" % (dc, tb)])

            for ob in range(2):
                ws = wuse(WI[("out", l, ob)])
                for dcc in range(4):
                    dc = ob * 4 + dcc
                    for tb in range(2):
                        bank = (dcc * 2 + tb) % 4
                        for fc in range(8):
                            P.op("pe", lambda e, ws=ws, dcc=dcc, fc=fc, tb=tb, bank=bank: e.matmul(PB[bank][:], lhsT=wsl(ws, fc * 512 + dcc * 128, fc * 512 + (dcc + 1) * 128),
                                                                                               rhs=pool[:, MC[fc], tb * 512:(tb + 1) * 512], start=(fc == 0), stop=(fc == 7)),
                                 reads=[*WK(ws), "B%d_%d" % (MC[fc], tb)], writes=["PB%d" % bank])
                        resid_update(dc, tb, bank, modt[:, 16 + dc:17 + dc], MTK([4, 5]))
                wdone(WI[("out", l, ob)])
            gcols = colsB[:, l * 16:l * 16 + 8]
            bcols = colsB[:, 32 + l * 16:32 + l * 16 + 8]
            layer_norm_x(gcols, bcols, ["colsB"], lambda k: xT[:, k, :], lambda k: XK(k))
            if stop == "mix%d" % l:
                break

            P.op("dve", lambda e, modt=modt: e.tensor_scalar(out=modt[:, 56:64], in0=modt[:, 32:40], scalar1=1.0, scalar2=None, op0=ALU.add),
                 reads=MTK([8, 9]), writes=["ma2_%d" % mpar])
            layer_norm_x(modt[:, 56:64], modt[:, 24:32], MTK([6, 7]) + ["ma2_%d" % mpar], lambda k: pbuf_(HB[k]), lambda k: K2("B%d" % HB[k]))
            for fb in range(11):
                nj = 2
                n = 256
                wsg = wuse(WI[("gate", l, fb)])
                wsu = wuse(WI[("up", l, fb)])
                for jj in range(nj):
                    j = fb * 2 + jj
                    for tb in range(2):
                        sl = slice(tb * 512, (tb + 1) * 512)
                        bg, bu = (tb * 2) % 4, (tb * 2 + 1) % 4
                        for k in range(8):
                            P.op("pe", lambda e, k=k, wsg=wsg, jj=jj, n=n, sl=sl, bg=bg: e.matmul(PB[bg][:], lhsT=wsl(wsg, k * n + jj * 128, k * n + (jj + 1) * 128),
                                                                                            rhs=pool[:, HB[k], sl], start=(k == 0), stop=(k == 7)),
                                 reads=[*WK(wsg), "B%d_%d" % (HB[k], tb)], writes=["PB%d" % bg])
                        for k in range(8):
                            P.op("pe", lambda e, k=k, wsu=wsu, jj=jj, n=n, sl=sl, bu=bu: e.matmul(PB[bu][:], lhsT=wsl(wsu, k * n + jj * 128, k * n + (jj + 1) * 128),
                                                                                            rhs=pool[:, HB[k], sl], start=(k == 0), stop=(k == 7)),
                                 reads=[*WK(wsu), "B%d_%d" % (HB[k], tb)], writes=["PB%d" % bu])
                        sgt = sgtb[:, tb, :]
                        P.op("act", lambda e, bg=bg, sgt=sgt: e.activation(out=sgt, in_=PB[bg][:], func=AF.Silu), reads=["PB%d" % bg], writes=["sgt%d" % tb])
                        P.op("dve", lambda e, bu=bu, sgt=sgt, j=j, sl=sl: e.tensor_tensor(out=pool[:, AB[j], sl], in0=PB[bu][:], in1=sgt, op=ALU.mult),
                             reads=["PB%d" % bu, "sgt%d" % tb], writes=["B%d_%d" % (AB[j], tb)])
                wdone(WI[("gate", l, fb)])
                wdone(WI[("up", l, fb)])
            for dc in range(8):
                ws = wuse(WI[("down", l, dc)])
                for tb in range(2):
                    bank = (dc * 2 + tb) % 4
                    for j in range(NFF):
                        P.op("pe", lambda e, ws=ws, j=j, tb=tb, bank=bank: e.matmul(PB[bank][:], lhsT=wsl(ws, j * 128, (j + 1) * 128),
                                                                                  rhs=pool[:, AB[j], tb * 512:(tb + 1) * 512], start=(j == 0), stop=(j == NFF - 1)),
                             reads=[*WK(ws), "B%d_%d" % (AB[j], tb)], writes=["PB%d" % bank])
                    resid_update(dc, tb, bank, modt[:, 40 + dc:41 + dc], MTK([10, 11]))
                wdone(WI[("down", l, dc)])
            gcols = colsB[:, l * 16 + 8:l * 16 + 16]
            bcols = colsB[:, 32 + l * 16 + 8:32 + l * 16 + 16]
            layer_norm_x(gcols, bcols, ["colsB"], lambda k: xT[:, k, :], lambda k: XK(k))
            if stop == "ffn%d" % l:
                break

        for t in range(8):
            b = t % 2
            for kh in range(2):
                bk = 2 * b + kh
                for kk in range(4):
                    k = kh * 4 + kk
                    P.op("pe", lambda e, bk=bk, kk=kk, k=k, t=t: e.transpose(out=PB[bk][:, kk * 128:(kk + 1) * 128], in_=xT[:, k, t * 128:(t + 1) * 128], identity=ident),
                         reads=["xT%d_%d" % (k, t // 4), "ctab"], writes=["PB%d" % bk])
                P.op("act" if kh == 0 else "dve",
                     (lambda e, kh=kh, b=b, bk=bk: e.activation(out=t32[:, b, kh * 512:(kh + 1) * 512], in_=PB[bk][:], func=AF.Copy)) if kh == 0 else
                     (lambda e, kh=kh, b=b, bk=bk: e.tensor_copy(out=t32[:, b, kh * 512:(kh + 1) * 512], in_=PB[bk][:])),
                     reads=["PB%d" % bk], writes=["T%d_%d" % (b, kh)])
            P.op("sp", lambda e, t=t, b=b: e.dma_start(out=y_d[t * 128:(t + 1) * 128, :], in_=t32[:, b, :]),
                 reads=K2("T%d" % b), writes=["y%d" % t], dma="yout%d" % b)
        P.emit()
    return nc


def _const_tables(is_sample):
    ct = np.zeros((128, NCT), np.float32)
    ct[:, C_ID:C_ID + 128] = np.eye(128, dtype=np.float32)
    tpos = np.arange(128, dtype=np.float32)
    ct[:, C_POSP1:C_POSP1 + 128] = tpos[None, :] + 1.0
    ct[:, C_POSREV:C_POSREV + 128] = 128.0 - tpos[None, :]
    bm = np.zeros((128, 128), np.float32)
    bm[:64, :64] = 1.0
    bm[64:, 64:] = 1.0
    mb = 1.0 if is_sample else 0.0
    ct[:, C_BM:C_BM + 128] = bm
    ct[:, C_BMB:C_BMB + 128] = bm * mb
    ct[:, C_PCOL] = 127.0 - tpos
    ct[:, C_PCOL + 1] = tpos
    for off_f, off_b, G, cps in ((C_MRF, C_MRB, 8, 2), (C_MHF, C_MHB, 32, 8)):
        mf = np.ones(G, np.float32)
        mbk = np.ones(G, np.float32)
        for g in range(G):
            if g % cps == 0 and g > 0:
                mf[g] = mb
            if (g + 1) % cps == 0 and g != G - 1:
                mbk[g] = mb
        ct[:, off_f:off_f + G] = mf[None, :]
        ct[:, off_b:off_b + G] = mbk[None, :]
    n = np.arange(64)
    ang = 2.0 * np.pi * np.outer(n, n) / 64.0
    c64 = np.cos(ang) / 8.0
    s64 = np.sin(ang) / 8.0
    bdc = np.zeros((128, 128))
    bds = np.zeros((128, 128))
    bdc[:64, :64] = c64
    bdc[64:, 64:] = c64
    bds[:64, :64] = -s64
    bds[64:, 64:] = -s64
    ct[:, C_DFTC:C_DFTC + 128] = bdc
    ct[:, C_DFTS:C_DFTS + 128] = bds
    j = np.arange(128)[:, None]
    i = np.arange(128)[None, :]
    ct[:, C_RMF:C_RMF + 128] = (j <= i)
    ct[:, C_RMB:C_RMB + 128] = (j >= i)
    same = (j // 32 == i // 32)
    ct[:, C_HMF:C_HMF + 128] = (j <= i) & same
    ct[:, C_HMB:C_HMB + 128] = (j >= i) & same
    seg = np.ones(1024, np.float32)
    seg[::32] = 0.0
    ct[:, C_SEG:C_SEG + 1024] = seg[None, :]
    ct[:, C_PZF:C_PZF + 128] = 127.0 - tpos[None, :]
    ct[:, C_PZB:C_PZB + 128] = tpos[None, :]
    ct[:, C_MB] = mb
    ct[:, C_MB + 1] = 1.0
    return ct


def _rope_tables(is_sample):
    r = np.zeros((128, 2, T), np.float64)
    if not is_sample:
        r[:, 0, :] = 1.0
        return r.astype(np.float32)
    tok = np.arange(T)
    rows = (tok // 64).astype(np.float64)
    cols = (tok % 64).astype(np.float64)
    half = 32
    inv = 10000.0 ** (-np.arange(0, half, 2, dtype=np.float64) / half)
    for pp in range(128):
        dd = pp % 64
        pos = rows if dd < 32 else cols
        w = dd % 32
        fi = w % 16
        ang = pos * inv[fi]
        r[pp, 0, :] = np.cos(ang)
        r[pp, 1, :] = -np.sin(ang) if w < 16 else np.sin(ang)
    return r.astype(np.float32)


def _dft_tables(is_sample):
    L = 1024 if is_sample else 256
    n = np.arange(L)
    ang = 2.0 * np.pi * np.outer(n, n) / L
    c = np.cos(ang) / np.sqrt(L)
    s = np.sin(ang) / np.sqrt(L)
    out = np.zeros((2, T, T), np.float32)
    for b in range(T // L):
        out[0, b * L:(b + 1) * L, b * L:(b + 1) * L] = c
        out[1, b * L:(b + 1) * L, b * L:(b + 1) * L] = s
    return out


def _bd_state(s):
    out = np.zeros((DEPTH, 2, 3, 128, 128), np.float32)
    for p in range(3):
        out[:, :, p, :64, :64] = s[:, :, 2 * p]
        out[:, :, p, 64:, 64:] = s[:, :, 2 * p + 1]
    return out


_NC_CACHE = {}


def kernel(x_prompt, x_sample, c, state_ret, state_hgrn, c_ctx, w_mod, b_mod, w_in, w_out,
           ret_log_decay, hg_lower_bound, ln_g, ln_b, w_gate, w_up, w_down):
    f = lambda a: np.ascontiguousarray(np.asarray(a, dtype=np.float32))
    x_prompt, x_sample, c, state_ret, state_hgrn, c_ctx = map(f, (x_prompt, x_sample, c, state_ret, state_hgrn, c_ctx))
    w_mod, b_mod, w_in, w_out, w_gate, w_up, w_down = map(f, (w_mod, b_mod, w_in, w_out, w_gate, w_up, w_down))
    ret_log_decay, hg_lower_bound, ln_g, ln_b = map(f, (ret_log_decay, hg_lower_bound, ln_g, ln_b))

    if "nc" not in _NC_CACHE:
        import os
        _NC_CACHE["nc"] = build_program(os.environ.get("KSTOP"))
    nc = _NC_CACHE["nc"]

    smB = np.zeros((128, 128), np.float32)
    smB[0:32] = ln_g.reshape(32, 128)
    smB[32:64] = ln_b.reshape(32, 128)
    smB[64:76] = hg_lower_bound.reshape(12, 128)
    dec = np.repeat(ret_log_decay.reshape(DEPTH, 2, 6), 64, axis=-1).reshape(12, 128)
    smB[76:88] = dec
    tabs = {s: (_const_tables(s), _rope_tables(s), _dft_tables(s)) for s in (False, True)}
    zs = np.zeros((DEPTH, 2, 3, 128, 128), np.float32)
    in_maps = []
    for core in range(NCORES):
        is_sample = core >= 4
        if is_sample:
            b = core - 4
            xin = x_sample[b]
            cvec = c[b]
            s0r = _bd_state(state_ret[b])
            s0h = _bd_state(state_hgrn[b])
        else:
            xin = x_prompt[core * 4:(core + 1) * 4].reshape(T, D)
            cvec = c_ctx
            s0r, s0h = zs, zs
        smA = np.zeros((128, 128), np.float32)
        smA[0:96] = b_mod.reshape(96, 128)
        smA[96:104] = cvec.reshape(8, 128)
        ct, rp, dl = tabs[is_sample]
        in_maps.append(dict(x=np.ascontiguousarray(xin), smA=smA, smB=smB, ctab=ct, rope=rp, dftL=dl,
                            s0r=s0r, s0h=s0h, w_mod=w_mod, w_in=w_in, w_out=w_out, w_gate=w_gate, w_up=w_up, w_down=w_down))
    res = run_bass_kernel_spmd(nc, in_maps, core_ids=list(range(NCORES)))
    R = res.results
    y_prompt = np.stack([R[i]["y"] for i in range(4)]).reshape(16, 256, D)
    y_sample = np.stack([R[i]["y"] for i in range(4, 8)])

    def unpack(name):
        out = np.zeros((16, DEPTH, 2, 6, 64, 64), np.float32)
        for core in range(4):
            o = R[core][name]
            for p in range(3):
                out[core * 4:(core + 1) * 4, :, :, 2 * p] = o[:, :, :, p, :64, :64].transpose(2, 0, 1, 3, 4)
                out[core * 4:(core + 1) * 4, :, :, 2 * p + 1] = o[:, :, :, p, 64:, 64:].transpose(2, 0, 1, 3, 4)
        return out

    return (y_prompt.astype(np.float32), y_sample.astype(np.float32), unpack("osr"), unpack("osh"))
```

```python
import contextlib
import math
import numpy as np
import concourse.bass as bass
import concourse.mybir as mybir
from concourse.bass_utils import run_bass_kernel_spmd

F32 = mybir.dt.float32
BF16 = mybir.dt.bfloat16
AF = mybir.ActivationFunctionType
ALU = mybir.AluOpType

D = 1024
T = 1024
NCORES = 8
DEPTH = 2
DFF = 2816
NFF = 22
ALPHA = (2 * DEPTH) ** 0.25
LN_EPS = 1e-5
LN8 = math.log(0.125)

C_ID, C_POSP1, C_POSREV, C_BM, C_BMB, C_PCOL = 0, 128, 256, 384, 512, 640
C_MRF, C_MRB, C_MHF, C_MHB = 642, 650, 658, 690
C_DFTC, C_DFTS = 722, 850
C_RMF, C_RMB, C_HMF, C_HMB = 978, 1106, 1234, 1362
C_SEG = 1490
C_PZF, C_PZB = 1490 + 1024, 1490 + 1024 + 128
C_MB = 1490 + 1024 + 256
NCT = 1490 + 1024 + 256 + 2
WUNIT = 2560
NU = 6
NPOOL = 30


class Prog:
    ENG = ("pe", "act", "dve", "pool", "sp")

    def __init__(self, nc):
        self.nc = nc
        self.ops = []

    def op(self, eng, fn, reads=(), writes=(), dma=None):
        self.ops.append(dict(eng=eng, fn=fn, reads=tuple(reads), writes=tuple(writes), dma=dma))

    def emit(self, final_waits=()):
        nc = self.nc
        ops = self.ops
        last_w, readers = {}, {}
        eng_idx = {e: 0 for e in self.ENG}
        dma_gen = {}
        signaling = set()
        for o in ops:
            e = o["eng"]
            idx = eng_idx[e]
            eng_idx[e] += 1
            o["idx"] = idx
            deps = set()
            for r in o["reads"]:
                if r in last_w:
                    deps.add(last_w[r])
            for w in o["writes"]:
                if w in last_w:
                    deps.add(last_w[w])
                for rd in readers.get(w, ()):
                    deps.add(rd)
            if o["dma"] is not None:
                g = dma_gen.get(o["dma"], 0) + 1
                dma_gen[o["dma"]] = g
                ev = ("dma", o["dma"], g)
            else:
                ev = ("eng", e, idx)
            deps.discard(ev)
            o["deps"] = deps
            for d in deps:
                if d[0] == "eng":
                    signaling.add((d[1], d[2]))
            for r in o["reads"]:
                readers.setdefault(r, []).append(ev)
            for w in o["writes"]:
                last_w[w] = ev
                readers[w] = []
        final_events = [last_w[k] for k in final_waits]
        count = {}
        run = {e: 0 for e in self.ENG}
        per_eng = {e: [] for e in self.ENG}
        for o in ops:
            per_eng[o["eng"]].append(o)
            key = (o["eng"], o["idx"])
            if o["dma"] is None and key in signaling:
                run[o["eng"]] += 1
                count[key] = run[o["eng"]]
                o["signal"] = True
            else:
                o["signal"] = False
        dma_keys = sorted(dma_gen.keys())
        with contextlib.ExitStack() as st:
            sem_e = {e: st.enter_context(nc.semaphore("sem_" + e)) for e in self.ENG}
            sem_d = {k: st.enter_context(nc.semaphore("semd_" + k)) for k in dma_keys}
            block = st.enter_context(nc.Block())

            def run_engine(e, eo):
                wm = {}

                def do_waits(deps):
                    need = {}
                    for d in deps:
                        if d[0] == "eng":
                            if d[1] == "pe" and e == "pe":
                                continue
                            k = ("eng", d[1])
                            v = count[(d[1], d[2])]
                        else:
                            k = ("dma", d[1])
                            v = 16 * d[2]
                        if v > need.get(k, 0):
                            need[k] = v
                    for k, v in need.items():
                        if wm.get(k, 0) >= v:
                            continue
                        wm[k] = v
                        s = sem_e[k[1]] if k[0] == "eng" else sem_d[k[1]]
                        eo.wait_ge(s, v)

                for o in per_eng[e]:
                    do_waits(o["deps"])
                    ins = o["fn"](eo)
                    if o["dma"] is not None:
                        ins.then_inc(sem_d[o["dma"]], 16)
                    elif o["signal"]:
                        ins.then_inc(sem_e[e], 1)
                if e == "sp":
                    do_waits(list(final_events) + [("dma", k, g) for k, g in dma_gen.items()])

            @block.tensor
            def _(eng):
                run_engine("pe", eng)

            @block.scalar
            def _(eng):
                run_engine("act", eng)

            @block.vector
            def _(eng):
                run_engine("dve", eng)

            @block.gpsimd
            def _(eng):
                run_engine("pool", eng)

            @block.sync
            def _(eng):
                run_engine("sp", eng)


def K2(name):
    return [name + "_0", name + "_1"]


def build_program(stop=None):
    nc = bass.Bass("TRN2", target_bir_lowering=False)

    def din(name, shape):
        return nc.dram_tensor(name, list(shape), F32, kind="ExternalInput").ap()

    def dout(name, shape):
        return nc.dram_tensor(name, list(shape), F32, kind="ExternalOutput").ap()

    x_d = din("x", [T, D])
    smA_d = din("smA", [128, 128])
    smB_d = din("smB", [128, 128])
    ctab_d = din("ctab", [128, NCT])
    rope_d = din("rope", [128, 2, T])
    dftL_d = din("dftL", [2, T, T])
    s0r_d = din("s0r", [DEPTH, 2, 3, 128, 128])
    s0h_d = din("s0h", [DEPTH, 2, 3, 128, 128])
    wmod_d = din("w_mod", [DEPTH, D, 6 * D])
    win_d = din("w_in", [DEPTH, D, 3712])
    wout_d = din("w_out", [DEPTH, D, D])
    wg_d = din("w_gate", [DEPTH, D, DFF])
    wu_d = din("w_up", [DEPTH, D, DFF])
    wd_d = din("w_down", [DEPTH, DFF, D])
    y_d = dout("y", [T, D])
    osr_d = dout("osr", [DEPTH, 2, 4, 3, 128, 128])
    osh_d = dout("osh", [DEPTH, 2, 4, 3, 128, 128])

    P = Prog(nc)
    st = contextlib.ExitStack()
    with st:
        def sb(name, shape, dt):
            return st.enter_context(nc.sbuf_tensor(name, list(shape), dt))

        def ps(name, shape, dt):
            return st.enter_context(nc.psum_tensor(name, list(shape), dt))

        xT = sb("xT", [128, 8, T], F32)
        ctab = sb("ctab_sb", [128, NCT], F32)
        colsA = sb("colsA", [128, 128], F32)
        colsB = sb("colsB", [128, 128], F32)
        wring = sb("wring", [128, NU, WUNIT], BF16)
        pool = sb("pool", [128, NPOOL, T], BF16)
        t32 = sb("t32", [128, 5, T], F32)
        smt = t32[:, 4, 0:256]
        vbuf = sb("vbuf", [128, 8, 128], BF16)
        vh = sb("vh", [128, 2, 8, 128], BF16)
        vblk = sb("vblk", [128, 8, 4, 128], BF16)
        sst = sb("sst", [128, 2, 8, 128], F32)
        sbfall = sb("sbfall", [128, 2, 32, 128], BF16)
        pbuf = sb("pbuf", [128, 2, 4, 128], BF16)
        modt_t = sb("modt", [128, 2, 64], F32)
        scb = sb("scb", [128, 8], BF16)
        misc = sb("misc", [128, 160], F32)
        dect = sb("dect", [128, 2, 6, 128], F32)
        dmt = sb("dmt", [128, 2, 2, 32], F32)
        onesD = sb("onesD", [128, 128], BF16)
        bd64 = sb("bd64", [128, 128], BF16)
        identb = sb("identb", [128, 128], BF16)
        dftcb = sb("dftcb", [128, 2, 128], BF16)
        sgtb = sb("sgtb", [128, 2, 512], BF16)
        PB = [ps("PB%d" % i, [128, 512], F32) for i in range(4)]
        PS4 = ps("PS4", [128, 512], F32)
        PT5 = ps("PT5", [128, 1024], BF16)
        PU = [ps("PU%d" % i, [128, 512], F32) for i in range(2)]

        ident = ctab[:, C_ID:C_ID + 128]
        PT5f = PT5[:, :].bitcast(F32)

        def pbuf_(i):
            return pool[:, i, :]

        def tt(i):
            return t32[:, i, :]

        wloads = []
        wstate = dict(issued=0, pos=0)
        wflat = wring[:, :, :].rearrange("p a b -> p (a b)")
        unit_occ = [None] * NU
        slot_of = {}
        wdone_flags = {}
        cur_units = {}

        def wsl(ws, a_, b_):
            return wflat[:, ws * WUNIT + a_:ws * WUNIT + b_]

        def WK(ws):
            return ["W%d" % (ws + i) for i in range(cur_units[ws])]

        def wreq(make, nu=2):
            wloads.append((nu, make))
            return len(wloads) - 1

        def _pump():
            while wstate["issued"] < len(wloads):
                j = wstate["issued"]
                nu, make = wloads[j]
                pos = wstate["pos"]
                if pos + nu > NU:
                    pos = 0
                ok = all(unit_occ[u] is None or wdone_flags.get(unit_occ[u], False) for u in range(pos, pos + nu))
                if not ok:
                    return
                for u in range(pos, pos + nu):
                    unit_occ[u] = j
                slot_of[j] = pos
                for (o_ap, i_ap) in make(pos):
                    P.op("pool", (lambda e, o_ap=o_ap, i_ap=i_ap: e.dma_start(out=o_ap, in_=i_ap)),
                         writes=["W%d" % u for u in range(pos, pos + nu)], dma="W%d" % pos)
                wstate["pos"] = (pos + nu) % NU
                wstate["issued"] += 1

        def wuse(i):
            _pump()
            assert wstate["issued"] > i, (i, wstate["issued"])
            cur_units[slot_of[i]] = wloads[i][0]
            return slot_of[i]

        def wdone(i):
            wdone_flags[i] = True
            _pump()

        def wview(s, k, n):
            return wsl(s, 0, k * n).rearrange("p (k n) -> p k n", n=n)

        def mk_std(dram2d, c0, n):
            def make(s):
                return [(wview(s, 8, n), dram2d[:, c0:c0 + n].rearrange("(k p) n -> p k n", p=128))]
            return make

        def mk_grp(dram2d, c0, ng, stride):
            def make(s):
                res = []
                v = wview(s, 8, ng * 128)
                for g in range(ng):
                    res.append((v[:, :, g * 128:(g + 1) * 128],
                                dram2d[:, c0 + g * stride:c0 + g * stride + 128].rearrange("(k p) n -> p k n", p=128)))
                return res
            return make

        def mk_down(dram2d, c0):
            def make(s):
                return [(wview(s, NFF, 128), dram2d[:, c0:c0 + 128].rearrange("(k p) n -> p k n", p=128))]
            return make

        WI = {}

        def reg_mods(lm, js):
            for j in js:
                WI[("mod", lm, j)] = wreq(mk_std(wmod_d[lm], j * 512, 512))

        def slot_mods(l, i):
            if i <= 3:
                return [(l, 4 + 2 * i), (l, 5 + 2 * i)]
            if i <= 5 and l + 1 < DEPTH:
                return [(l + 1, 2 * (i - 4)), (l + 1, 2 * (i - 4) + 1)]
            return []

        reg_mods(0, range(4))
        for l in range(DEPTH):
            WI[("fnet", l)] = wreq(mk_std(win_d[l], 0, 256), 1)
            for (lm, j) in slot_mods(l, 0):
                reg_mods(lm, [j])
            for tbl in range(2):
                for ob in range(2):
                    WI[("dft", l, tbl, ob)] = wreq(mk_std(dftL_d[tbl], ob * 512, 512))
            for p in range(3):
                WI[("ret", l, p)] = wreq(mk_grp(win_d[l], 256 + 128 * p, 4, 384))
                for (lm, j) in slot_mods(l, 1 + p):
                    reg_mods(lm, [j])
            for p in range(3):
                WI[("hg", l, p)] = wreq(mk_grp(win_d[l], 1792 + 128 * p, 5, 384))
                for (lm, j) in slot_mods(l, 4 + p):
                    reg_mods(lm, [j])
            for ob in range(2):
                WI[("out", l, ob)] = wreq(mk_std(wout_d[l], ob * 512, 512))
            for fb in range(11):
                WI[("gate", l, fb)] = wreq(mk_std(wg_d[l], fb * 256, 256), 1)
                WI[("up", l, fb)] = wreq(mk_std(wu_d[l], fb * 256, 256), 1)
            for dc in range(8):
                WI[("down", l, dc)] = wreq(mk_down(wd_d[l], dc * 128))

        P.op("sp", lambda e: e.dma_start(out=ctab[:], in_=ctab_d), writes=["ctab"], dma="c0")
        P.op("sp", lambda e: e.dma_start(out=smt[:, 0:128], in_=smA_d), writes=["T4_0"], dma="c2")
        P.op("sp", lambda e: e.dma_start(out=smt[:, 128:256], in_=smB_d), writes=["T4_0"], dma="c3")
        P.op("dve", lambda e: e.memset(onesD[:], 1.0 / 1024.0), writes=["onesD"])
        P.op("dve", lambda e: e.tensor_scalar(out=bd64[:], in0=ctab[:, C_BM:C_BM + 128], scalar1=1.0 / 64.0, scalar2=None, op0=ALU.mult),
             reads=["ctab"], writes=["bd64"])
        P.op("dve", lambda e: e.tensor_copy(out=identb[:], in_=ident), reads=["ctab"], writes=["identb"])
        P.op("dve", lambda e: e.tensor_copy(out=dftcb[:].rearrange("p a b -> p (a b)"), in_=ctab[:, C_DFTC:C_DFTC + 256]),
             reads=["ctab"], writes=["dftcb"])
        P.op("dve", lambda e: e.memset(sbfall[:].rearrange("p a b c -> p (a b c)"), 0.0), writes=["SBA%d_%d" % (d_, t_) for d_ in range(2) for t_ in range(8)])
        P.op("pool", lambda e: e.memset(vh[:].rearrange("p a b c -> p (a b c)"), 0.0), writes=["vh"])
        P.op("pool", lambda e: e.memset(vblk[:].rearrange("p a b c -> p (a b c)"), 0.0), writes=["vblk"])
        P.op("pe", lambda e: e.transpose(out=PB[0][:, 0:128], in_=smt[:, 0:128], identity=ident), reads=["T4_0", "ctab"], writes=["PB0"])
        P.op("pe", lambda e: e.transpose(out=PB[0][:, 128:256], in_=smt[:, 128:256], identity=ident), reads=["T4_0", "ctab"], writes=["PB0"])
        P.op("act", lambda e: e.activation(out=colsA[:], in_=PB[0][:, 0:128], func=AF.Copy), reads=["PB0"], writes=["colsA"])
        P.op("act", lambda e: e.activation(out=colsB[:], in_=PB[0][:, 128:256], func=AF.Copy), reads=["PB0"], writes=["colsB"])
        P.op("act", lambda e: e.activation(out=scb[:], in_=colsA[:, 96:104], func=AF.Silu), reads=["colsA"], writes=["scb"])
        P.op("act", lambda e: e.activation(out=misc[:, 28:40], in_=colsB[:, 76:88], func=AF.Exp), reads=["colsB"], writes=["misc_lg"])
        P.op("dve", lambda e: e.tensor_scalar(out=misc[:, 16:28], in0=misc[:, 28:40], scalar1=-1.0, scalar2=None, op0=ALU.mult),
             reads=["misc_lg"], writes=["misc_lg2"])
        P.op("act", lambda e: e.activation(out=misc[:, 40:52], in_=misc[:, 16:28], func=AF.Exp, scale=128.0), reads=["misc_lg2"], writes=["misc_D"])
        P.op("act", lambda e: e.activation(out=misc[:, 52:64], in_=colsB[:, 64:76], func=AF.Exp), reads=["colsB"], writes=["misc_e"])
        P.op("dve", lambda e: e.memset(misc[:, 64:76], 0.0), writes=["misc_lb"])
        for d_ in range(2):
            e0 = misc[:, 52 + d_ * 6:52 + d_ * 6 + 3]
            e1 = misc[:, 52 + d_ * 6 + 3:52 + d_ * 6 + 6]
            dst = misc[:, 64 + d_ * 6 + 3:64 + d_ * 6 + 6]
            tmpc = misc[:, 100 + d_ * 3:103 + d_ * 3]
            P.op("dve", lambda e, e0=e0, e1=e1, tmpc=tmpc: e.tensor_tensor(out=tmpc, in0=e0, in1=e1, op=ALU.add), reads=["misc_e"], writes=["misc_t%d" % d_])
            P.op("dve", lambda e, tmpc=tmpc: e.reciprocal(out=tmpc, in_=tmpc), reads=["misc_t%d" % d_], writes=["misc_t%d" % d_])
            P.op("dve", lambda e, e1=e1, tmpc=tmpc, dst=dst: e.tensor_tensor(out=dst, in0=e1, in1=tmpc, op=ALU.mult),
                 reads=["misc_t%d" % d_, "misc_e", "misc_lb"], writes=["misc_lb"])
        P.op("dve", lambda e: e.tensor_scalar(out=misc[:, 76:88], in0=misc[:, 64:76], scalar1=-1.0, scalar2=1.0, op0=ALU.mult, op1=ALU.add),
             reads=["misc_lb"], writes=["misc_oml"])
        P.op("dve", lambda e: e.tensor_scalar(out=misc[:, 88:100], in0=misc[:, 64:76], scalar1=-1.0, scalar2=None, op0=ALU.add),
             reads=["misc_lb"], writes=["misc_lbm1"])

        for t in range(8):
            b = t % 2
            P.op("sp", lambda e, t=t, b=b: e.dma_start(out=t32[:, b, :], in_=x_d[t * 128:(t + 1) * 128, :]),
                 writes=K2("T%d" % b), dma="xin%d" % b)
            for kh in range(2):
                bk = 2 * b + kh
                for kk in range(4):
                    k = kh * 4 + kk
                    src = t32[:, b, k * 128:(k + 1) * 128]
                    P.op("pe", lambda e, bk=bk, kk=kk, src=src: e.transpose(out=PB[bk][:, kk * 128:(kk + 1) * 128], in_=src, identity=ident),
                         reads=K2("T%d" % b) + ["ctab"], writes=["PB%d" % bk])
                P.op("act" if kh == 0 else "dve",
                     (lambda e, kh=kh, t=t, bk=bk: e.activation(out=xT[:, kh * 4:kh * 4 + 4, t * 128:(t + 1) * 128],
                                                                 in_=PB[bk][:].rearrange("p (a b) -> p a b", b=128), func=AF.Copy)) if kh == 0 else
                     (lambda e, kh=kh, t=t, bk=bk: e.tensor_copy(out=xT[:, kh * 4:kh * 4 + 4, t * 128:(t + 1) * 128],
                                                                  in_=PB[bk][:].rearrange("p (a b) -> p a b", b=128))),
                     reads=["PB%d" % bk], writes=["xT%d_%d" % (k_, t // 4) for k_ in range(kh * 4, kh * 4 + 4)])

        XK = lambda k: ["xT%d_0" % k, "xT%d_1" % k]

        def norm_stats(chunks, chunk_keys, ones_ap, ones_key, center, mean_t, rstd_t, tmpb_all):
            n = len(chunks)
            for ci, (ch, ck) in enumerate(zip(chunks, chunk_keys)):
                tmpb = tmpb_all[ci % len(tmpb_all)]
                xb, xq = pbuf_(tmpb[0]), pbuf_(tmpb[1])
                if n == 1:
                    for tb in range(2):
                        hsl = slice(tb * 512, (tb + 1) * 512)
                        if center:
                            P.op("act", lambda e, ch=ch, xb=xb, hsl=hsl: e.activation(out=xb[:, hsl], in_=ch[:, hsl], func=AF.Copy),
                                 reads=[ck[tb]], writes=["B%d_%d" % (tmpb[0], tb)])
                        P.op("dve", lambda e, ch=ch, xq=xq, hsl=hsl: e.tensor_tensor(out=xq[:, hsl], in0=ch[:, hsl], in1=ch[:, hsl], op=ALU.mult),
                             reads=[ck[tb]], writes=["B%d_%d" % (tmpb[1], tb)])
                else:
                    if center:
                        P.op("act", lambda e, ch=ch, xb=xb: e.activation(out=xb, in_=ch, func=AF.Copy), reads=ck, writes=K2("B%d" % tmpb[0]))
                    P.op("dve", lambda e, ch=ch, xq=xq: e.tensor_tensor(out=xq, in0=ch, in1=ch, op=ALU.mult), reads=ck, writes=K2("B%d" % tmpb[1]))
                for tb in range(2):
                    if center:
                        P.op("pe", lambda e, tb=tb, xb=xb, ci=ci: e.matmul(PB[tb][:], lhsT=ones_ap, rhs=xb[:, tb * 512:(tb + 1) * 512],
                                                                         start=(ci == 0), stop=(ci == n - 1)),
                             reads=["B%d_%d" % (tmpb[0], tb), ones_key], writes=["PB%d" % tb])
                    P.op("pe", lambda e, tb=tb, xq=xq, ci=ci: e.matmul(PB[2 + tb][:], lhsT=ones_ap, rhs=xq[:, tb * 512:(tb + 1) * 512],
                                                                     start=(ci == 0), stop=(ci == n - 1)),
                         reads=["B%d_%d" % (tmpb[1], tb), ones_key], writes=["PB%d" % (2 + tb)])
            mean, rstd = tt(mean_t), tt(rstd_t)
            SL = [slice(0, 512), slice(512, 1024)]
            MKk = ["T%d_%d" % (mean_t, tb) for tb in range(2)]
            RKk = ["T%d_%d" % (rstd_t, tb) for tb in range(2)]
            if center:
                for tb in range(2):
                    P.op("act", lambda e, tb=tb: e.activation(out=mean[:, SL[tb]], in_=PB[tb][:], func=AF.Copy),
                         reads=["PB%d" % tb], writes=[MKk[tb]])
                for tb in range(2):
                    P.op("dve", lambda e, tb=tb: e.tensor_tensor(out=rstd[:, SL[tb]], in0=mean[:, SL[tb]], in1=mean[:, SL[tb]], op=ALU.mult),
                         reads=[MKk[tb]], writes=[RKk[tb]])
                for tb in range(2):
                    P.op("dve", lambda e, tb=tb: e.tensor_tensor(out=rstd[:, SL[tb]], in0=PB[2 + tb][:], in1=rstd[:, SL[tb]], op=ALU.subtract),
                         reads=["PB%d" % (2 + tb), RKk[tb]], writes=[RKk[tb]])
                for tb in range(2):
                    P.op("act", lambda e, tb=tb: e.activation(out=rstd[:, SL[tb]], in_=rstd[:, SL[tb]], func=AF.Ln, bias=epsc, scale=1.0),
                         reads=[RKk[tb], "epsc"], writes=[RKk[tb]])
            else:
                for tb in range(2):
                    P.op("act", lambda e, tb=tb: e.activation(out=rstd[:, SL[tb]], in_=PB[2 + tb][:], func=AF.Ln, bias=epsc, scale=1.0),
                         reads=["PB%d" % (2 + tb), "epsc"], writes=[RKk[tb]])
            for tb in range(2):
                P.op("act", lambda e, tb=tb: e.activation(out=rstd[:, SL[tb]], in_=rstd[:, SL[tb]], func=AF.Exp, scale=-0.5),
                     reads=[RKk[tb]], writes=[RKk[tb]])

        epsc = misc[:, 110:111]
        P.op("dve", lambda e: e.memset(epsc, LN_EPS), writes=["epsc"])
        c80p = misc[:, 112:113]
        c80n = misc[:, 113:114]
        P.op("dve", lambda e: e.memset(c80p, 80.0), writes=["c80"])
        P.op("dve", lambda e: e.memset(c80n, -80.0), writes=["c80"])
        ln8c = misc[:, 111:112]
        P.op("dve", lambda e: e.memset(ln8c, LN8), writes=["ln8c"])

        def ln_apply(scale_cols, bias_cols, col_keys, dst_fn, dst_keys_fn, tmp_t):
            mean, rstd = tt(0), tt(1)
            for kp in range(4):
                ks_ = (2 * kp, 2 * kp + 1)
                tqs = {k: tmp_t + (k % 2) for k in ks_}
                for k in ks_:
                    tq = tqs[k]
                    tmp = tt(tq)
                    P.op("dve", lambda e, k=k, tmp=tmp: e.tensor_tensor(out=tmp, in0=xT[:, k, :], in1=mean, op=ALU.subtract),
                         reads=XK(k) + K2("T0"), writes=K2("T%d" % tq))
                for k in ks_:
                    tq = tqs[k]
                    tmp = tt(tq)
                    P.op("dve", lambda e, tmp=tmp: e.tensor_tensor(out=tmp, in0=tmp, in1=rstd, op=ALU.mult),
                         reads=K2("T%d" % tq) + K2("T1"), writes=K2("T%d" % tq))
                for k in ks_:
                    tq = tqs[k]
                    tmp = tt(tq)
                    P.op("act", lambda e, k=k, tmp=tmp: e.activation(out=dst_fn(k), in_=tmp, func=AF.Identity, scale=scale_cols[:, k:k + 1], bias=bias_cols[:, k:k + 1]),
                         reads=K2("T%d" % tq) + col_keys, writes=dst_keys_fn(k))

        def layer_norm_x(scale_cols, bias_cols, col_keys, dst_fn, dst_keys_fn):
            norm_stats([xT[:, k, :] for k in range(8)], [XK(k) for k in range(8)], onesD[:], "onesD", True, 0, 1, [(26, 27), (28, 29)])
            ln_apply(scale_cols, bias_cols, col_keys, dst_fn, dst_keys_fn, 2)

        HB = list(range(0, 8))
        MC = list(range(8, 16))
        OPB = list(range(16, 25))
        AB = list(range(8, 30))

        def fm_proj(ws, n_, c0, tb, bank):
            for k in range(8):
                P.op("pe", lambda e, k=k: e.matmul(PB[bank][:], lhsT=wsl(ws, k * n_ + c0, k * n_ + c0 + 128),
                                                   rhs=pool[:, HB[k], tb * 512:(tb + 1) * 512], start=(k == 0), stop=(k == 7)),
                     reads=[*WK(ws), "B%d_%d" % (HB[k], tb)], writes=["PB%d" % bank])

        def fm_proj2(ws, n_, c0, tb, bank_ap, bank_key):
            for k in range(8):
                P.op("pe", lambda e, k=k: e.matmul(bank_ap, lhsT=wsl(ws, k * n_ + c0, k * n_ + c0 + 128),
                                                   rhs=pool[:, HB[k], tb * 512:(tb + 1) * 512], start=(k == 0), stop=(k == 7)),
                     reads=[*WK(ws), "B%d_%d" % (HB[k], tb)], writes=[bank_key])

        def tm_proj_tile(ws, n, c0, ncols, t, out_ap, out_key):
            for k in range(8):
                P.op("pe", lambda e, k=k: e.matmul(out_ap, lhsT=pool[:, HB[k], t * 128:(t + 1) * 128],
                                                   rhs=wsl(ws, k * n + c0, k * n + c0 + ncols), start=(k == 0), stop=(k == 7)),
                     reads=[*WK(ws), "B%d_%d" % (HB[k], t // 4)], writes=[out_key])

        def gla_pair(l, p, csz, qq, kkh, ks, masks, s0_d, os_d, mcol_off, gate_b, kind, extra_w=()):
            n = 128 // csz
            G = 8 * n
            cps = 256 // csz
            oacc = tt(0)
            cur = {}
            um = tt(1)
            for d in range(2):
                g0 = 0 if d == 0 else G - 1
                P.op("sp", lambda e, d=d: e.dma_start(out=sst[:, d, 1, :], in_=s0_d[l, d, p]), writes=["S%d_1" % d], dma="s0_%d" % d)
                P.op("act", lambda e, d=d, g0=g0: e.activation(out=sbfall[:, d, g0, :], in_=sst[:, d, 1, :], func=AF.Copy),
                     reads=["S%d_1" % d], writes=["SBA%d_%d" % (d, g0 // n)] + list(extra_w))
                cur[d] = 1
            UB = [[(PU[0], "PU0"), (PS4, "PS4")], [(PU[1], "PU1"), (PB[3], "PB3")]]
            for s in range(8):
                tiles = [s, 7 - s]
                for d in range(2):
                    t = tiles[d]
                    hk = "_%d" % (t // 4)
                    if csz == 32:
                        ub_t, ub_k = UB[d][s % 2]
                        for c_ in range(4):
                            for hh in range(2):
                                P.op("pe", lambda e, d=d, t=t, hh=hh, c_=c_, ub_t=ub_t: e.matmul(
                                        ub_t[:, c_ * 128 + hh * 64:c_ * 128 + (hh + 1) * 64],
                                        lhsT=pool[:, ks[d][hh], t * 128:(t + 1) * 128],
                                        rhs=vblk[:, t, c_, hh * 64:(hh + 1) * 64], start=True, stop=True),
                                     reads=["B%d%s" % (ks[d][hh], hk), "vblk"], writes=[ub_k])
                    else:
                        P.op("pe", lambda e, d=d, t=t: e.matmul(PU[d][:, 0:128], lhsT=pool[:, ks[d], t * 128:(t + 1) * 128],
                                                                rhs=vbuf[:, t, :], start=True, stop=True),
                             reads=["B%d%s" % (ks[d], hk), "vbuf"], writes=["PU%d" % d])
                for d in range(2):
                    if csz != 32:
                        P.op("dve", lambda e, d=d: e.tensor_tensor(out=um[:, d * 512:d * 512 + 128], in0=PU[d][:, 0:128],
                                                                   in1=ctab[:, C_BM:C_BM + 128], op=ALU.mult),
                             reads=["PU%d" % d, "ctab"], writes=["T1_%d" % d])
                for ci in range(n):
                    for d in range(2):
                        t = tiles[d]
                        c = ci if d == 0 else n - 1 - ci
                        g = t * n + c
                        si = cur[d]
                        if d == 0:
                            fin = ((g + 1) % cps == 0)
                            seq = (g + 1) // cps - 1
                            nxt_boundary = fin and (g != G - 1)
                            last = (g == G - 1)
                            gn = g + 1
                        else:
                            fin = (g % cps == 0)
                            seq = g // cps
                            nxt_boundary = fin and (g != 0)
                            last = (g == 0)
                            gn = g - 1
                        ni = (4 + seq) if fin else ((si + 1) % 4 if si < 4 else 0)
                        dcol = dmt[:, mcol_off, d, g:g + 1]
                        if csz == 32:
                            ub_t, ub_k = UB[d][s % 2]
                            ucol, ukey = ub_t[:, c * 128:(c + 1) * 128], ub_k
                        else:
                            ucol, ukey = um[:, d * 512:d * 512 + 128], "T1_%d" % d
                        P.op("dve", lambda e, d=d, si=si, ni=ni, dcol=dcol, ucol=ucol: e.scalar_tensor_tensor(
                                out=sst[:, d, ni, :], in0=sst[:, d, si, :], scalar=dcol, in1=ucol, op0=ALU.mult, op1=ALU.add),
                             reads=["S%d_%d" % (d, si), "dmt%d_%d" % (mcol_off, d), ukey], writes=["S%d_%d" % (d, ni)])
                        if fin:
                            P.op("sp", lambda e, d=d, ni=ni, seq=seq: e.dma_start(out=os_d[l, d, seq, p], in_=sst[:, d, ni, :]),
                                 reads=["S%d_%d" % (d, ni)], writes=["os_%s_%d_%d_%d_%d" % (kind, l, d, seq, p)],
                                 dma="os%d_%d" % (d, ni))
                        if not last:
                            mk = C_MB if nxt_boundary else C_MB + 1
                            P.op("act", lambda e, d=d, ni=ni, gn=gn, mk=mk: e.activation(out=sbfall[:, d, gn, :], in_=sst[:, d, ni, :],
                                                                                      func=AF.Identity, scale=ctab[:, mk:mk + 1]),
                                 reads=["S%d_%d" % (d, ni), "ctab"], writes=["SBA%d_%d" % (d, gn // n)])
                        cur[d] = ni
            SCB = [[PS4, PB[3]], [PB[2], PU[0]]]
            SCK = [["PS4", "PB3"], ["PB2", "PU0"]]
            OAB, OAK = [PB[0], PB[1]], ["PB0", "PB1"]

            def scores(t):
                tsl = slice(t * 128, (t + 1) * 128)
                hk = "_%d" % (t // 4)
                par = t % 2
                for d in range(2):
                    for h in range(2):
                        P.op("pe", lambda e, d=d, h=h, par=par, tsl=tsl: e.matmul(SCB[d][par][:, h * 128:(h + 1) * 128], lhsT=pool[:, kkh[d][h], tsl],
                                                                                 rhs=pool[:, qq[d], tsl], start=True, stop=True),
                             reads=["B%d%s" % (kkh[d][h], hk), "B%d%s" % (qq[d], hk)], writes=[SCK[d][par]])
                for d in range(2):
                    P.op("dve", lambda e, d=d, par=par: e.tensor_tensor(
                            out=pbuf[:, par, 2 * d:2 * d + 2, :], in0=SCB[d][par][:, 0:256].rearrange("p (h c) -> p h c", h=2),
                            in1=ctab[:, masks[d]:masks[d] + 128].unsqueeze(1).to_broadcast([128, 2, 128]), op=ALU.mult),
                         reads=[SCK[d][par], "ctab"], writes=["pbuf%d_%d" % (par, d)])

            def outputs(t):
                tsl = slice(t * 128, (t + 1) * 128)
                hk = "_%d" % (t // 4)
                par = t % 2
                osl = OAB[par][:, 0:128]
                first = True
                for d in range(2):
                    for h in range(2):
                        P.op("pe", lambda e, d=d, h=h, t=t, par=par, osl=osl, first=first: e.matmul(osl, lhsT=vh[:, h, t, :], rhs=pbuf[:, par, 2 * d + h, :],
                                                                                                start=first, stop=False),
                             reads=["vh", "pbuf%d_%d" % (par, d)], writes=[OAK[par]])
                        first = False
                for d in range(2):
                    for c in range(n):
                        g = t * n + c
                        csl = slice(t * 128 + c * csz, t * 128 + (c + 1) * csz)
                        lastmm = (d == 1 and c == n - 1)
                        P.op("pe", lambda e, d=d, g=g, c=c, csl=csl, par=par, lastmm=lastmm: e.matmul(
                                OAB[par][:, c * csz:(c + 1) * csz], lhsT=sbfall[:, d, g, :], rhs=pool[:, qq[d], csl],
                                start=False, stop=lastmm),
                             reads=["SBA%d_%d" % (d, t), "B%d%s" % (qq[d], hk)], writes=[OAK[par]])
                P.op("act", lambda e, osl=osl, tsl=tsl: e.activation(out=oacc[:, tsl], in_=osl, func=AF.Copy),
                     reads=[OAK[par]], writes=["T0%s" % hk])

            for t in range(9):
                if t < 8:
                    scores(t)
                if t >= 1:
                    outputs(t - 1)

        for l in range(DEPTH):
            mpar = l % 2
            modt = modt_t[:, mpar, :]
            MTK = lambda js: ["mt%d_%d" % (mpar, j) for j in js]

            def mod_block(lm, j12, bank, bank_key):
                ws_ = wuse(WI[("mod", lm, j12)])
                for jj in range(4):
                    for k in range(8):
                        P.op("pe", lambda e, ws_=ws_, jj=jj, k=k: e.matmul(bank[:, jj:jj + 1], lhsT=wsl(ws_, k * 512 + jj * 128, k * 512 + (jj + 1) * 128),
                                                                          rhs=scb[:, k:k + 1], start=(k == 0), stop=(k == 7)),
                             reads=[*WK(ws_), "scb"], writes=[bank_key])
                wdone(WI[("mod", lm, j12)])
                P.op("dve", lambda e: e.tensor_tensor(out=modt_t[:, lm % 2, 4 * j12:4 * j12 + 4], in0=bank[:, 0:4],
                                                      in1=colsA[:, lm * 48 + 4 * j12:lm * 48 + 4 * j12 + 4], op=ALU.add),
                     reads=[bank_key, "colsA"], writes=["mt%d_%d" % (lm % 2, j12)])

            def run_slot_mods(i):
                for (lm, j) in slot_mods(l, i):
                    mod_block(lm, j, PU[1], "PU1")

            if l == 0:
                for j12 in range(4):
                    mod_block(0, j12, PB[j12 % 2], "PB%d" % (j12 % 2))
            P.op("dve", lambda e, modt=modt: e.tensor_scalar(out=modt[:, 48:56], in0=modt[:, 8:16], scalar1=1.0, scalar2=None, op0=ALU.add),
                 reads=MTK([2, 3]), writes=["ma1_%d" % mpar])
            layer_norm_x(modt[:, 48:56], modt[:, 0:8], MTK([0, 1]) + ["ma1_%d" % mpar], lambda k: pbuf_(HB[k]), lambda k: K2("B%d" % HB[k]))

            ws = wuse(WI[("fnet", l)])
            ub_, pc_ = OPB[0:2], OPB[2:6]
            uview = pool[:, ub_[0]:ub_[0] + 2, :].rearrange("p a b -> p (a b)").rearrange("p (t c) -> p t c", c=256)
            UK = K2("B%d" % ub_[0]) + K2("B%d" % ub_[1])
            for t in range(8):
                bank = t % 2
                tm_proj_tile(ws, 256, 0, 256, t, PB[bank][:, 0:256], "PB%d" % bank)
                P.op("act", lambda e, t=t, bank=bank: e.activation(out=uview[:, t, :], in_=PB[bank][:, 0:256], func=AF.Copy),
                     reads=["PB%d" % bank], writes=UK)
            wdone(WI[("fnet", l)])
            run_slot_mods(0)
            for tbl in range(2):
                for ob in range(2):
                    ws = wuse(WI[("dft", l, tbl, ob)])
                    for ct in range(2):
                        bank = 2 + ct
                        for kt in range(8):
                            P.op("pe", lambda e, ws=ws, ct=ct, kt=kt, bank=bank: e.matmul(PB[bank][:], lhsT=uview[:, kt, ct * 128:(ct + 1) * 128],
                                                                                          rhs=wsl(ws, kt * 512, (kt + 1) * 512), start=(kt == 0), stop=(kt == 7)),
                                 reads=UK + [*WK(ws)], writes=["PB%d" % bank])
                        dstb = pc_[tbl * 2 + ct]
                        P.op("act", lambda e, bank=bank, dstb=dstb, ob=ob: e.activation(out=pool[:, dstb, ob * 512:(ob + 1) * 512], in_=PB[bank][:], func=AF.Copy),
                             reads=["PB%d" % bank], writes=["B%d_%d" % (dstb, ob)])
                    wdone(WI[("dft", l, tbl, ob)])
            for ct in range(2):
                for ob in range(2):
                    bank = ob
                    for tbl in range(2):
                        srcb = pc_[tbl * 2 + ct]
                        P.op("pe", lambda e, tbl=tbl, srcb=srcb, ob=ob, bank=bank: e.matmul(PB[bank][:], lhsT=dftcb[:, tbl, :], rhs=pool[:, srcb, ob * 512:(ob + 1) * 512],
                                                                                          start=(tbl == 0), stop=(tbl == 1)),
                             reads=["dftcb", "B%d_%d" % (srcb, ob)], writes=["PB%d" % bank])
                    P.op("act", lambda e, ct=ct, ob=ob, bank=bank: e.activation(out=pool[:, MC[ct], ob * 512:(ob + 1) * 512], in_=PB[bank][:], func=AF.Copy),
                         reads=["PB%d" % bank], writes=["B%d_%d" % (MC[ct], ob)])

            def v_proj(ws, n, c0, need_blk):
                for t in range(8):
                    bank = t % 2
                    tm_proj_tile(ws, n, c0, 128, t, PB[bank][:, 0:128], "PB%d" % bank)
                    P.op("act", lambda e, t=t, bank=bank: e.activation(out=vbuf[:, t, :], in_=PB[bank][:, 0:128], func=AF.Copy),
                         reads=["PB%d" % bank], writes=["vbuf"])
                for h in range(2):
                    P.op("act", lambda e, h=h: e.activation(out=vh[:, h, :, h * 64:(h + 1) * 64], in_=vbuf[:, :, h * 64:(h + 1) * 64], func=AF.Copy),
                         reads=["vbuf"], writes=["vh"])
                if need_blk:
                    for c in range(4):
                        P.op("dve", lambda e, c=c: e.tensor_copy(out=vblk[c * 32:(c + 1) * 32, :, c, :], in_=vbuf[c * 32:(c + 1) * 32, :, :]),
                             reads=["vbuf"], writes=["vblk"])

            def v_proj2(ws, n, c0, need_blk):
                for half in range(2):
                    for tq in range(4):
                        t = half * 4 + tq
                        tm_proj_tile(ws, n, c0, 128, t, PU[1][:, tq * 128:(tq + 1) * 128], "PU1")
                    P.op("act", lambda e, half=half: e.activation(out=vbuf[:, half * 4:(half + 1) * 4, :],
                                                                  in_=PU[1][:].rearrange("p (a b) -> p a b", b=128), func=AF.Copy),
                         reads=["PU1"], writes=["vbuf"])
                for h in range(2):
                    P.op("act", lambda e, h=h: e.activation(out=vh[:, h, :, h * 64:(h + 1) * 64], in_=vbuf[:, :, h * 64:(h + 1) * 64], func=AF.Copy),
                         reads=["vbuf"], writes=["vh"])
                if need_blk:
                    for c in range(4):
                        P.op("dve", lambda e, c=c: e.tensor_copy(out=vblk[c * 32:(c + 1) * 32, :, c, :], in_=vbuf[c * 32:(c + 1) * 32, :, :]),
                             reads=["vbuf"], writes=["vblk"])

            def finish_pair(kind, gate_b, mc_idx):
                oacc = tt(0)
                center = (kind == "ret")
                norm_stats([oacc], [K2("T0")], bd64[:], "bd64", center, 3, 4, [(26, 27)])
                tmp = tt(2)
                HSL = [slice(0, 512), slice(512, 1024)]
                if center:
                    for tb in range(2):
                        P.op("dve", lambda e, tb=tb: e.tensor_tensor(out=tmp[:, HSL[tb]], in0=oacc[:, HSL[tb]], in1=tt(3)[:, HSL[tb]], op=ALU.subtract),
                             reads=["T0_%d" % tb, "T3_%d" % tb], writes=["T2_%d" % tb])
                    for tb in range(2):
                        P.op("dve", lambda e, tb=tb: e.tensor_tensor(out=tmp[:, HSL[tb]], in0=tmp[:, HSL[tb]], in1=tt(4)[:, HSL[tb]], op=ALU.mult),
                             reads=["T2_%d" % tb, "T4_%d" % tb], writes=["T2_%d" % tb])
                else:
                    for tb in range(2):
                        P.op("dve", lambda e, tb=tb: e.tensor_tensor(out=tmp[:, HSL[tb]], in0=oacc[:, HSL[tb]], in1=tt(4)[:, HSL[tb]], op=ALU.mult),
                             reads=["T0_%d" % tb, "T4_%d" % tb], writes=["T2_%d" % tb])
                for tb in range(2):
                    P.op("dve", lambda e, tb=tb: e.tensor_tensor(out=pool[:, MC[mc_idx], HSL[tb]], in0=tmp[:, HSL[tb]], in1=pool[:, gate_b, HSL[tb]], op=ALU.mult),
                         reads=["T2_%d" % tb, "B%d_%d" % (gate_b, tb)], writes=["B%d_%d" % (MC[mc_idx], tb)])

            def ks_transposes(src_b, dst_b_list, evac):
                for t in range(8):
                    P.op("pe", lambda e, t=t: e.transpose(out=PT5[:, (t % 8) * 128:(t % 8 + 1) * 128], in_=pool[:, src_b, t * 128:(t + 1) * 128], identity=identb[:]),
                         reads=["B%d_%d" % (src_b, t // 4), "identb"], writes=["PT5"])
                for t in range(8):
                    evac(t, PT5[:, (t % 8) * 128:(t % 8 + 1) * 128], "PT5")

            for d in range(2):
                for h in range(2):
                    bi_ = OPB[2 + d * 2 + h]
                    P.op("dve", lambda e, h=h, bi_=bi_: e.memset(pool[(1 - h) * 64:(2 - h) * 64, bi_, :], 0.0), writes=K2("B%d" % bi_))
            def ret_tables(p_, par):
                for d in range(2):
                    cidx = l * 6 + d * 3 + p_
                    lgc = misc[:, 16 + cidx:17 + cidx]
                    nlgc = misc[:, 28 + cidx:29 + cidx]
                    pos = ctab[:, C_POSP1:C_POSP1 + 128] if d == 0 else ctab[:, C_POSREV:C_POSREV + 128]
                    P.op("act", lambda e, d=d, lgc=lgc, pos=pos: e.activation(out=dect[:, par, 2 * d, :], in_=pos, func=AF.Exp, scale=lgc),
                         reads=["ctab", "misc_lg2"], writes=["dect%d_%d" % (par, 2 * d)])
                    P.op("act", lambda e, d=d, nlgc=nlgc, pos=pos: e.activation(out=dect[:, par, 2 * d + 1, :], in_=pos, func=AF.Exp, scale=nlgc, bias=ln8c),
                         reads=["ctab", "misc_lg", "ln8c"], writes=["dect%d_%d" % (par, 2 * d + 1)])
                    pz = ctab[:, C_PZF:C_PZF + 128] if d == 0 else ctab[:, C_PZB:C_PZB + 128]
                    P.op("act", lambda e, d=d, lgc=lgc, pz=pz: e.activation(out=dect[:, par, 4 + d, :], in_=pz, func=AF.Exp, scale=lgc, bias=ln8c),
                         reads=["ctab", "misc_lg2", "ln8c"], writes=["dect%d_%d" % (par, 4 + d)])
                    P.op("pe", lambda e, d=d: e.transpose(out=PT5f[:, d * 128:(d + 1) * 128], in_=dect[:, par, 4 + d, :], identity=ident),
                         reads=["dect%d_%d" % (par, 4 + d), "ctab"], writes=["PT5"])
                for d in range(2):
                    cidx = l * 6 + d * 3 + p_
                    P.op("act", lambda e, d=d: e.activation(out=dect[:, par, 4 + d, :], in_=PT5f[:, d * 128:(d + 1) * 128], func=AF.Copy),
                         reads=["PT5"], writes=["dect%d_%d" % (par, 4 + d)])
                    P.op("dve", lambda e, d=d, cidx=cidx: e.tensor_tensor(out=dmt[:, par, d, 0:8], in0=misc[:, 40 + cidx:41 + cidx].to_broadcast([128, 8]),
                                                                         in1=ctab[:, (C_MRF if d == 0 else C_MRB):(C_MRF if d == 0 else C_MRB) + 8], op=ALU.mult),
                         reads=["misc_D", "ctab"], writes=["dmt%d_%d" % (par, d)])

            ret_tables(0, 0)
            for p in range(3):
                ws = wuse(WI[("ret", l, p)])
                qq = [OPB[0], OPB[1]]
                kkh = [[OPB[2], OPB[3]], [OPB[4], OPB[5]]]
                ks = [OPB[6], OPB[7]]
                gate_b = OPB[8]
                par = p % 2
                wv = wsl(ws, 0, 4096).rearrange("p (k g a c) -> p k g a c", k=8, g=16, a=2)
                wsw = pool[:, 28:30, :].rearrange("p a b -> p (a b)").rearrange("p (k n) -> p k n", n=256)
                sv = wsw.rearrange("p k (g a c) -> p k g a c", g=8, a=2)
                for a in range(2):
                    P.op("act", lambda e, a=a, wv=wv, sv=sv: e.activation(out=sv[:, :, :, a, :], in_=wv[:, :, 0:8, 1 - a, :], func=AF.Copy),
                         reads=[*WK(ws)], writes=K2("B28") + K2("B29"))
                rope = t32[:, 0:2, :]
                P.op("sp", lambda e: e.dma_start(out=t32[:, 0:2, :], in_=rope_d), writes=K2("T0") + K2("T1"), dma="c1")
                for which in range(2):
                    rot = tt(2 + which)
                    SLs = [slice(0, 512), slice(512, 1024)]
                    for tb in range(2):
                        ba, bb_ = 2 * tb, 2 * tb + 1
                        fm_proj(ws, 512, which * 128, tb, ba)
                        for k in range(8):
                            P.op("pe", lambda e, k=k, which=which, tb=tb, bb_=bb_: e.matmul(PB[bb_][:], lhsT=wsw[:, k, which * 128:(which + 1) * 128],
                                                                                        rhs=pool[:, HB[k], tb * 512:(tb + 1) * 512], start=(k == 0), stop=(k == 7)),
                                 reads=K2("B28") + K2("B29") + ["B%d_%d" % (HB[k], tb)], writes=["PB%d" % bb_])
                    for tb in range(2):
                        ba, bb_ = 2 * tb, 2 * tb + 1
                        sl = SLs[tb]
                        P.op("dve", lambda e, rot=rot, sl=sl, ba=ba: e.tensor_tensor(out=rot[:, sl], in0=PB[ba][:], in1=rope[:, 0, sl], op=ALU.mult),
                             reads=["PB%d" % ba, "T0_%d" % tb], writes=["T%d_%d" % (2 + which, tb)])
                        P.op("dve", lambda e, sl=sl, bb_=bb_: e.tensor_tensor(out=tt(4)[:, sl], in0=PB[bb_][:], in1=rope[:, 1, sl], op=ALU.mult),
                             reads=["PB%d" % bb_, "T1_%d" % tb], writes=["T4_%d" % tb])
                    for tb in range(2):
                        sl = SLs[tb]
                        P.op("dve", lambda e, rot=rot, sl=sl: e.tensor_tensor(out=rot[:, sl], in0=rot[:, sl], in1=tt(4)[:, sl], op=ALU.add),
                             reads=["T%d_%d" % (2 + which, tb), "T4_%d" % tb], writes=["T%d_%d" % (2 + which, tb)])
                qrot, krot = tt(2), tt(3)
                r3 = lambda ap: ap.rearrange("p (t c) -> p t c", c=128)
                for d in range(2):
                    eq = dect[:, par, 2 * d, :]
                    ek = dect[:, par, 2 * d + 1, :]
                    P.op("dve", lambda e, d=d, eq=eq: e.tensor_tensor(out=r3(pbuf_(qq[d])), in0=r3(qrot), in1=eq.unsqueeze(1).to_broadcast([128, 8, 128]), op=ALU.mult),
                         reads=K2("T2") + ["dect%d_%d" % (par, 2 * d)], writes=K2("B%d" % qq[d]))
                    for h in range(2):
                        hs = slice(h * 64, (h + 1) * 64)
                        P.op("dve", lambda e, d=d, h=h, hs=hs, ek=ek: e.tensor_tensor(out=r3(pool[hs, kkh[d][h], :]), in0=r3(krot[hs, :]),
                                                                                  in1=ek[hs, :].unsqueeze(1).to_broadcast([64, 8, 128]), op=ALU.mult),
                             reads=K2("T3") + ["dect%d_%d" % (par, 2 * d + 1)], writes=K2("B%d" % kkh[d][h]))
                kb = 25
                P.op("act", lambda e: e.activation(out=pbuf_(kb), in_=krot, func=AF.Copy), reads=K2("T3"), writes=K2("B%d" % kb))

                def evac_ret(t, src, skey, par=par, ks=ks):
                    for d in range(2):
                        zt = dect[:, par, 4 + d, :]
                        P.op("dve", lambda e, d=d, t=t, src=src, zt=zt, ks=ks: e.tensor_tensor(out=pool[:, ks[d], t * 128:(t + 1) * 128], in0=src, in1=zt, op=ALU.mult),
                             reads=[skey, "dect%d_%d" % (par, 4 + d)], writes=["B%d_%d" % (ks[d], t // 4)])
                ks_transposes(kb, ks, evac_ret)
                v_proj2(ws, 512, 256, False)
                for tb in range(2):
                    fm_proj(ws, 512, 384, tb, 2 + tb)
                    P.op("act", lambda e, tb=tb: e.activation(out=pool[:, gate_b, tb * 512:(tb + 1) * 512], in_=PB[2 + tb][:], func=AF.Silu),
                         reads=["PB%d" % (2 + tb)], writes=["B%d_%d" % (gate_b, tb)])
                wdone(WI[("ret", l, p)])
                run_slot_mods(1 + p)
                if p < 2:
                    ret_tables(p + 1, (p + 1) % 2)
                gla_pair(l, p, 128, qq, kkh, ks, (C_RMF, C_RMB), s0r_d, osr_d, par, gate_b, "r")
                finish_pair("ret", gate_b, 2 + p)

            GBK = [(PS4[:], "PS4"), (PU[0][:], "PU0")]
            xf = sbfall[:, :, :, :].rearrange("p a b c -> p (a b c)").bitcast(F32)
            ALL_SBA = ["SBA%d_%d" % (d_, t_) for d_ in range(2) for t_ in range(8)]
            XKEYS = ["X%d_%d" % (i_, h_) for i_ in range(4) for h_ in range(2)]
            HS = [slice(0, 512), slice(512, 1024)]
            for p in range(3):
                ws = wuse(WI[("hg", l, p)])
                qq = [OPB[0], OPB[1]]
                kkh = [[OPB[2], OPB[3]], [OPB[4], OPB[5]]]
                ks = [[OPB[6], 28], [OPB[7], 29]]
                if p == 0:
                    for d_ in range(2):
                        for hh_ in range(2):
                            bz = ks[d_][hh_]
                            P.op("dve", lambda e, bz=bz: e.memset(pool[:, bz, :], 0.0), writes=K2("B%d" % bz))
                gate_b = OPB[8]
                dpar = (p + 1) % 2
                TSET = [dict(sig=tt(0), kf=tt(1), bb=tt(3), einv=tt(4), K=("T0", "T1", "T3", "T4"), kb=25,
                             banks=[(PB[2][:], "PB2"), (PB[3][:], "PB3")]),
                        dict(sig=xf[:, 0:1024], kf=xf[:, 1024:2048], bb=xf[:, 2048:3072], einv=xf[:, 3072:4096],
                             K=("X0", "X1", "X2", "X3"), kb=26, banks=GBK)]
                for tb in range(2):
                    fm_proj2(ws, 640, 512, tb, GBK[tb][0], GBK[tb][1])
                    P.op("act", lambda e, tb=tb, gate_b=gate_b: e.activation(out=pool[:, gate_b, tb * 512:(tb + 1) * 512], in_=GBK[tb][0], func=AF.Silu),
                         reads=[GBK[tb][1]], writes=["B%d_%d" % (gate_b, tb)])
                v_proj2(ws, 640, 384, True)
                qf = tt(2)
                for tb in range(2):
                    fm_proj(ws, 640, 0, tb, tb)
                    P.op("act", lambda e, tb=tb: e.activation(out=qf[:, tb * 512:(tb + 1) * 512], in_=PB[tb][:], func=AF.Silu),
                         reads=["PB%d" % tb], writes=["T2_%d" % tb])
                for d in range(2):
                    for tb in range(2):
                        bk_ap, bk_key = TSET[d]["banks"][tb]
                        fm_proj2(ws, 640, 128 * (1 + d), tb, bk_ap, bk_key)
                cols = []
                for d in range(2):
                    lidx = d * 6 + l * 3 + p
                    cols.append(dict(oml=misc[:, 76 + lidx:77 + lidx], lb=misc[:, 64 + lidx:65 + lidx], lbm1=misc[:, 88 + lidx:89 + lidx],
                                     edge=(31 if d == 0 else 0)))
                DT = [(d, tb) for d in range(2) for tb in range(2)]
                for (d, tb) in DT:
                    T_, (bk_ap, bk_key) = TSET[d], TSET[d]["banks"][tb]
                    extra = ALL_SBA if (d == 1 and tb == 0) else []
                    P.op("act", lambda e, T_=T_, tb=tb, bk_ap=bk_ap: e.activation(out=T_["sig"][:, HS[tb]], in_=bk_ap, func=AF.Sigmoid),
                         reads=[bk_key], writes=["%s_%d" % (T_["K"][0], tb)] + extra)
                for (d, tb) in DT:
                    T_, C_ = TSET[d], cols[d]
                    P.op("dve", lambda e, T_=T_, C_=C_, tb=tb: e.tensor_scalar(out=T_["kf"][:, HS[tb]], in0=T_["sig"][:, HS[tb]], scalar1=C_["lbm1"], scalar2=C_["oml"],
                                                                               op0=ALU.mult, op1=ALU.add),
                         reads=["%s_%d" % (T_["K"][0], tb), "misc_lbm1", "misc_oml"], writes=["%s_%d" % (T_["K"][1], tb)])
                for (d, tb) in DT:
                    T_, C_ = TSET[d], cols[d]
                    P.op("act", lambda e, T_=T_, C_=C_, tb=tb: e.activation(out=T_["sig"][:, HS[tb]], in_=T_["sig"][:, HS[tb]], func=AF.Ln, scale=C_["oml"], bias=C_["lb"]),
                         reads=["%s_%d" % (T_["K"][0], tb), "misc_oml", "misc_lb"], writes=["%s_%d" % (T_["K"][0], tb)])
                for (d, tb) in DT:
                    T_ = TSET[d]
                    P.op("dve", lambda e, T_=T_, tb=tb: e.tensor_tensor_scan(out=T_["bb"][:, HS[tb]], data0=ctab[:, C_SEG:C_SEG + 512], data1=T_["sig"][:, HS[tb]],
                                                                             initial=0.0, op0=ALU.mult, op1=ALU.add),
                         reads=["%s_%d" % (T_["K"][0], tb), "ctab"], writes=["%s_%d" % (T_["K"][2], tb)])
                T1 = TSET[1]
                for tb in range(2):
                    b3 = T1["bb"][:, HS[tb]].rearrange("p (c s) -> p c s", s=32)
                    P.op("dve", lambda e, b3=b3: e.tensor_tensor(out=b3, in0=b3, in1=b3[:, :, 31:32].to_broadcast([128, 16, 32]), op=ALU.subtract),
                         reads=["X2_%d" % tb], writes=["X2_%d" % tb])
                for tb in range(2):
                    P.op("dve", lambda e, T1=T1, tb=tb: e.tensor_tensor(out=T1["bb"][:, HS[tb]], in0=T1["sig"][:, HS[tb]], in1=T1["bb"][:, HS[tb]], op=ALU.subtract),
                         reads=["X2_%d" % tb, "X0_%d" % tb], writes=["X2_%d" % tb])
                for (d, tb) in DT:
                    T_ = TSET[d]
                    P.op("act", lambda e, T_=T_, tb=tb: e.activation(out=T_["bb"][:, HS[tb]], in_=T_["bb"][:, HS[tb]], func=AF.Relu, bias=c80p, scale=1.0),
                         reads=["%s_%d" % (T_["K"][2], tb), "c80"], writes=["%s_%d" % (T_["K"][2], tb)])
                for (d, tb) in DT:
                    T_ = TSET[d]
                    P.op("act", lambda e, T_=T_, tb=tb: e.activation(out=T_["sig"][:, HS[tb]], in_=T_["bb"][:, HS[tb]], func=AF.Exp, bias=c80n, scale=1.0),
                         reads=["%s_%d" % (T_["K"][2], tb), "c80"], writes=["%s_%d" % (T_["K"][0], tb)])
                    P.op("act", lambda e, T_=T_, tb=tb: e.activation(out=T_["einv"][:, HS[tb]], in_=T_["bb"][:, HS[tb]], func=AF.Exp, bias=c80p, scale=-1.0),
                         reads=["%s_%d" % (T_["K"][2], tb), "c80"], writes=["%s_%d" % (T_["K"][3], tb)])
                for (d, tb) in DT:
                    T_, edge = TSET[d], cols[d]["edge"]
                    kE, kK, kI = "%s_%d" % (T_["K"][0], tb), "%s_%d" % (T_["K"][1], tb), "%s_%d" % (T_["K"][3], tb)
                    e3 = T_["sig"][:, HS[tb]].rearrange("p (c s) -> p c s", s=32)
                    i3 = T_["einv"][:, HS[tb]].rearrange("p (c s) -> p c s", s=32)
                    moff = (C_MHF if d == 0 else C_MHB) + tb * 16
                    P.op("dve", lambda e, d=d, tb=tb, e3=e3, moff=moff, edge=edge, dpar=dpar: e.tensor_tensor(out=dmt[:, dpar, d, tb * 16:(tb + 1) * 16], in0=e3[:, :, edge],
                                                                                                      in1=ctab[:, moff:moff + 16], op=ALU.mult),
                         reads=[kE, "ctab"], writes=["dmt%d_%d" % (dpar, d)])
                    P.op("dve", lambda e, d=d, tb=tb, T_=T_, qq=qq: e.tensor_tensor(out=pool[:, qq[d], HS[tb]], in0=qf[:, HS[tb]], in1=T_["sig"][:, HS[tb]], op=ALU.mult),
                         reads=["T2_%d" % tb, kE], writes=["B%d_%d" % (qq[d], tb)])
                    for h in range(2):
                        hs = slice(h * 64, (h + 1) * 64)
                        P.op("dve", lambda e, d=d, h=h, hs=hs, tb=tb, T_=T_, kkh=kkh: e.tensor_tensor(out=pool[hs, kkh[d][h], HS[tb]], in0=T_["kf"][hs, HS[tb]],
                                                                                                  in1=T_["einv"][hs, HS[tb]], op=ALU.mult),
                             reads=[kK, kI], writes=["B%d_%d" % (kkh[d][h], tb)])
                    P.op("dve", lambda e, i3=i3, e3=e3, edge=edge: e.tensor_tensor(out=i3, in0=i3, in1=e3[:, :, edge:edge + 1].to_broadcast([128, 16, 32]), op=ALU.mult),
                         reads=[kI, kE], writes=[kI])
                for (d, tb) in DT:
                    T_ = TSET[d]
                    P.op("dve", lambda e, T_=T_, tb=tb: e.tensor_tensor(out=pool[:, T_["kb"], HS[tb]], in0=T_["kf"][:, HS[tb]], in1=T_["einv"][:, HS[tb]], op=ALU.mult),
                         reads=["%s_%d" % (T_["K"][1], tb), "%s_%d" % (T_["K"][3], tb)], writes=["B%d_%d" % (T_["kb"], tb)])
                for d in range(2):
                    def evac_h(t, src, skey, d=d, ks=ks):
                        for hh in range(2):
                            P.op("act", lambda e, t=t, src=src, hh=hh: e.activation(out=pool[:, ks[d][hh], t * 128 + hh * 64:t * 128 + (hh + 1) * 64],
                                                                                in_=src[:, hh * 64:(hh + 1) * 64], func=AF.Copy),
                                 reads=[skey], writes=["B%d_%d" % (ks[d][hh], t // 4)])
                    ks_transposes(TSET[d]["kb"], ks, evac_h)
                wdone(WI[("hg", l, p)])
                run_slot_mods(4 + p)
                gla_pair(l, p, 32, qq, kkh, ks, (C_HMF, C_HMB), s0h_d, osh_d, dpar, gate_b, "h", extra_w=XKEYS)
                finish_pair("hg", gate_b, 5 + p)

            def resid_update(dc, tb, bank, gcol, gkeys):
                sl = slice(tb * 512, (tb + 1) * 512)
                tmp = tt(2)
                P.op("act", lambda e: e.activation(out=tmp[:, sl], in_=PB[bank][:], func=AF.Identity, scale=gcol),
                     reads=["PB%d" % bank] + gkeys, writes=["T2_%d" % tb])
                P.op("dve", lambda e: e.scalar_tensor_tensor(out=xT[:, dc, sl], in0=xT[:, dc, sl], scalar=float(ALPHA), in1=tmp[:, sl], op0=ALU.mult, op1=ALU.add),
                     reads=["xT%d_%d" % (dc, tb), "T2_%d" % tb], writes=["xT%d_%d" % (dc, tb)])

            for ob in range(2):
                ws = wuse(WI[("out", l, ob)])
                for dcc in range(4):
                    dc = ob * 4 + dcc
                    for tb in range(2):
                        bank = (dcc * 2 + tb) % 4
                        for fc in range(8):
                            P.op("pe", lambda e, ws=ws, dcc=dcc, fc=fc, tb=tb, bank=bank: e.matmul(PB[bank][:], lhsT=wsl(ws, fc * 512 + dcc * 128, fc * 512 + (dcc + 1) * 128),
                                                                                               rhs=pool[:, MC[fc], tb * 512:(tb + 1) * 512], start=(fc == 0), stop=(fc == 7)),
                                 reads=[*WK(ws), "B%d_%d" % (MC[fc], tb)], writes=["PB%d" % bank])
                        resid_update(dc, tb, bank, modt[:, 16 + dc:17 + dc], MTK([4, 5]))
                wdone(WI[("out", l, ob)])
            gcols = colsB[:, l * 16:l * 16 + 8]
            bcols = colsB[:, 32 + l * 16:32 + l * 16 + 8]
            layer_norm_x(gcols, bcols, ["colsB"], lambda k: xT[:, k, :], lambda k: XK(k))
            if stop == "mix%d" % l:
                break

            P.op("dve", lambda e, modt=modt: e.tensor_scalar(out=modt[:, 56:64], in0=modt[:, 32:40], scalar1=1.0, scalar2=None, op0=ALU.add),
                 reads=MTK([8, 9]), writes=["ma2_%d" % mpar])
            layer_norm_x(modt[:, 56:64], modt[:, 24:32], MTK([6, 7]) + ["ma2_%d" % mpar], lambda k: pbuf_(HB[k]), lambda k: K2("B%d" % HB[k]))
            for fb in range(11):
                nj = 2
                n = 256
                wsg = wuse(WI[("gate", l, fb)])
                wsu = wuse(WI[("up", l, fb)])
                for jj in range(nj):
                    j = fb * 2 + jj
                    for tb in range(2):
                        sl = slice(tb * 512, (tb + 1) * 512)
                        bg, bu = (tb * 2) % 4, (tb * 2 + 1) % 4
                        for k in range(8):
                            P.op("pe", lambda e, k=k, wsg=wsg, jj=jj, n=n, sl=sl, bg=bg: e.matmul(PB[bg][:], lhsT=wsl(wsg, k * n + jj * 128, k * n + (jj + 1) * 128),
                                                                                            rhs=pool[:, HB[k], sl], start=(k == 0), stop=(k == 7)),
                                 reads=[*WK(wsg), "B%d_%d" % (HB[k], tb)], writes=["PB%d" % bg])
                        for k in range(8):
                            P.op("pe", lambda e, k=k, wsu=wsu, jj=jj, n=n, sl=sl, bu=bu: e.matmul(PB[bu][:], lhsT=wsl(wsu, k * n + jj * 128, k * n + (jj + 1) * 128),
                                                                                            rhs=pool[:, HB[k], sl], start=(k == 0), stop=(k == 7)),
                                 reads=[*WK(wsu), "B%d_%d" % (HB[k], tb)], writes=["PB%d" % bu])
                        sgt = sgtb[:, tb, :]
                        P.op("act", lambda e, bg=bg, sgt=sgt: e.activation(out=sgt, in_=PB[bg][:], func=AF.Silu), reads=["PB%d" % bg], writes=["sgt%d" % tb])
                        P.op("dve", lambda e, bu=bu, sgt=sgt, j=j, sl=sl: e.tensor_tensor(out=pool[:, AB[j], sl], in0=PB[bu][:], in1=sgt, op=ALU.mult),
                             reads=["PB%d" % bu, "sgt%d" % tb], writes=["B%d_%d" % (AB[j], tb)])
                wdone(WI[("gate", l, fb)])
                wdone(WI[("up", l, fb)])
            for dc in range(8):
                ws = wuse(WI[("down", l, dc)])
                for tb in range(2):
                    bank = (dc * 2 + tb) % 4
                    for j in range(NFF):
                        P.op("pe", lambda e, ws=ws, j=j, tb=tb, bank=bank: e.matmul(PB[bank][:], lhsT=wsl(ws, j * 128, (j + 1) * 128),
                                                                                  rhs=pool[:, AB[j], tb * 512:(tb + 1) * 512], start=(j == 0), stop=(j == NFF - 1)),
                             reads=[*WK(ws), "B%d_%d" % (AB[j], tb)], writes=["PB%d" % bank])
                    resid_update(dc, tb, bank, modt[:, 40 + dc:41 + dc], MTK([10, 11]))
                wdone(WI[("down", l, dc)])
            gcols = colsB[:, l * 16 + 8:l * 16 + 16]
            bcols = colsB[:, 32 + l * 16 + 8:32 + l * 16 + 16]
            layer_norm_x(gcols, bcols, ["colsB"], lambda k: xT[:, k, :], lambda k: XK(k))
            if stop == "ffn%d" % l:
                break

        for t in range(8):
            b = t % 2
            for kh in range(2):
                bk = 2 * b + kh
                for kk in range(4):
                    k = kh * 4 + kk
                    P.op("pe", lambda e, bk=bk, kk=kk, k=k, t=t: e.transpose(out=PB[bk][:, kk * 128:(kk + 1) * 128], in_=xT[:, k, t * 128:(t + 1) * 128], identity=ident),
                         reads=["xT%d_%d" % (k, t // 4), "ctab"], writes=["PB%d" % bk])
                P.op("act" if kh == 0 else "dve",
                     (lambda e, kh=kh, b=b, bk=bk: e.activation(out=t32[:, b, kh * 512:(kh + 1) * 512], in_=PB[bk][:], func=AF.Copy)) if kh == 0 else
                     (lambda e, kh=kh, b=b, bk=bk: e.tensor_copy(out=t32[:, b, kh * 512:(kh + 1) * 512], in_=PB[bk][:])),
                     reads=["PB%d" % bk], writes=["T%d_%d" % (b, kh)])
            P.op("sp", lambda e, t=t, b=b: e.dma_start(out=y_d[t * 128:(t + 1) * 128, :], in_=t32[:, b, :]),
                 reads=K2("T%d" % b), writes=["y%d" % t], dma="yout%d" % b)
        P.emit()
    return nc


def _const_tables(is_sample):
    ct = np.zeros((128, NCT), np.float32)
    ct[:, C_ID:C_ID + 128] = np.eye(128, dtype=np.float32)
    tpos = np.arange(128, dtype=np.float32)
    ct[:, C_POSP1:C_POSP1 + 128] = tpos[None, :] + 1.0
    ct[:, C_POSREV:C_POSREV + 128] = 128.0 - tpos[None, :]
    bm = np.zeros((128, 128), np.float32)
    bm[:64, :64] = 1.0
    bm[64:, 64:] = 1.0
    mb = 1.0 if is_sample else 0.0
    ct[:, C_BM:C_BM + 128] = bm
    ct[:, C_BMB:C_BMB + 128] = bm * mb
    ct[:, C_PCOL] = 127.0 - tpos
    ct[:, C_PCOL + 1] = tpos
    for off_f, off_b, G, cps in ((C_MRF, C_MRB, 8, 2), (C_MHF, C_MHB, 32, 8)):
        mf = np.ones(G, np.float32)
        mbk = np.ones(G, np.float32)
        for g in range(G):
            if g % cps == 0 and g > 0:
                mf[g] = mb
            if (g + 1) % cps == 0 and g != G - 1:
                mbk[g] = mb
        ct[:, off_f:off_f + G] = mf[None, :]
        ct[:, off_b:off_b + G] = mbk[None, :]
    n = np.arange(64)
    ang = 2.0 * np.pi * np.outer(n, n) / 64.0
    c64 = np.cos(ang) / 8.0
    s64 = np.sin(ang) / 8.0
    bdc = np.zeros((128, 128))
    bds = np.zeros((128, 128))
    bdc[:64, :64] = c64
    bdc[64:, 64:] = c64
    bds[:64, :64] = -s64
    bds[64:, 64:] = -s64
    ct[:, C_DFTC:C_DFTC + 128] = bdc
    ct[:, C_DFTS:C_DFTS + 128] = bds
    j = np.arange(128)[:, None]
    i = np.arange(128)[None, :]
    ct[:, C_RMF:C_RMF + 128] = (j <= i)
    ct[:, C_RMB:C_RMB + 128] = (j >= i)
    same = (j // 32 == i // 32)
    ct[:, C_HMF:C_HMF + 128] = (j <= i) & same
    ct[:, C_HMB:C_HMB + 128] = (j >= i) & same
    seg = np.ones(1024, np.float32)
    seg[::32] = 0.0
    ct[:, C_SEG:C_SEG + 1024] = seg[None, :]
    ct[:, C_PZF:C_PZF + 128] = 127.0 - tpos[None, :]
    ct[:, C_PZB:C_PZB + 128] = tpos[None, :]
    ct[:, C_MB] = mb
    ct[:, C_MB + 1] = 1.0
    return ct


def _rope_tables(is_sample):
    r = np.zeros((128, 2, T), np.float64)
    if not is_sample:
        r[:, 0, :] = 1.0
        return r.astype(np.float32)
    tok = np.arange(T)
    rows = (tok // 64).astype(np.float64)
    cols = (tok % 64).astype(np.float64)
    half = 32
    inv = 10000.0 ** (-np.arange(0, half, 2, dtype=np.float64) / half)
    for pp in range(128):
        dd = pp % 64
        pos = rows if dd < 32 else cols
        w = dd % 32
        fi = w % 16
        ang = pos * inv[fi]
        r[pp, 0, :] = np.cos(ang)
        r[pp, 1, :] = -np.sin(ang) if w < 16 else np.sin(ang)
    return r.astype(np.float32)


def _dft_tables(is_sample):
    L = 1024 if is_sample else 256
    n = np.arange(L)
    ang = 2.0 * np.pi * np.outer(n, n) / L
    c = np.cos(ang) / np.sqrt(L)
    s = np.sin(ang) / np.sqrt(L)
    out = np.zeros((2, T, T), np.float32)
    for b in range(T // L):
        out[0, b * L:(b + 1) * L, b * L:(b + 1) * L] = c
        out[1, b * L:(b + 1) * L, b * L:(b + 1) * L] = s
    return out


def _bd_state(s):
    out = np.zeros((DEPTH, 2, 3, 128, 128), np.float32)
    for p in range(3):
        out[:, :, p, :64, :64] = s[:, :, 2 * p]
        out[:, :, p, 64:, 64:] = s[:, :, 2 * p + 1]
    return out


_NC_CACHE = {}


def kernel(x_prompt, x_sample, c, state_ret, state_hgrn, c_ctx, w_mod, b_mod, w_in, w_out,
           ret_log_decay, hg_lower_bound, ln_g, ln_b, w_gate, w_up, w_down):
    f = lambda a: np.ascontiguousarray(np.asarray(a, dtype=np.float32))
    x_prompt, x_sample, c, state_ret, state_hgrn, c_ctx = map(f, (x_prompt, x_sample, c, state_ret, state_hgrn, c_ctx))
    w_mod, b_mod, w_in, w_out, w_gate, w_up, w_down = map(f, (w_mod, b_mod, w_in, w_out, w_gate, w_up, w_down))
    ret_log_decay, hg_lower_bound, ln_g, ln_b = map(f, (ret_log_decay, hg_lower_bound, ln_g, ln_b))

    if "nc" not in _NC_CACHE:
        import os
        _NC_CACHE["nc"] = build_program(os.environ.get("KSTOP"))
    nc = _NC_CACHE["nc"]

    smB = np.zeros((128, 128), np.float32)
    smB[0:32] = ln_g.reshape(32, 128)
    smB[32:64] = ln_b.reshape(32, 128)
    smB[64:76] = hg_lower_bound.reshape(12, 128)
    dec = np.repeat(ret_log_decay.reshape(DEPTH, 2, 6), 64, axis=-1).reshape(12, 128)
    smB[76:88] = dec
    tabs = {s: (_const_tables(s), _rope_tables(s), _dft_tables(s)) for s in (False, True)}
    zs = np.zeros((DEPTH, 2, 3, 128, 128), np.float32)
    in_maps = []
    for core in range(NCORES):
        is_sample = core >= 4
        if is_sample:
            b = core - 4
            xin = x_sample[b]
            cvec = c[b]
            s0r = _bd_state(state_ret[b])
            s0h = _bd_state(state_hgrn[b])
        else:
            xin = x_prompt[core * 4:(core + 1) * 4].reshape(T, D)
            cvec = c_ctx
            s0r, s0h = zs, zs
        smA = np.zeros((128, 128), np.float32)
        smA[0:96] = b_mod.reshape(96, 128)
        smA[96:104] = cvec.reshape(8, 128)
        ct, rp, dl = tabs[is_sample]
        in_maps.append(dict(x=np.ascontiguousarray(xin), smA=smA, smB=smB, ctab=ct, rope=rp, dftL=dl,
                            s0r=s0r, s0h=s0h, w_mod=w_mod, w_in=w_in, w_out=w_out, w_gate=w_gate, w_up=w_up, w_down=w_down))
    res = run_bass_kernel_spmd(nc, in_maps, core_ids=list(range(NCORES)))
    R = res.results
    y_prompt = np.stack([R[i]["y"] for i in range(4)]).reshape(16, 256, D)
    y_sample = np.stack([R[i]["y"] for i in range(4, 8)])

    def unpack(name):
        out = np.zeros((16, DEPTH, 2, 6, 64, 64), np.float32)
        for core in range(4):
            o = R[core][name]
            for p in range(3):
                out[core * 4:(core + 1) * 4, :, :, 2 * p] = o[:, :, :, p, :64, :64].transpose(2, 0, 1, 3, 4)
                out[core * 4:(core + 1) * 4, :, :, 2 * p + 1] = o[:, :, :, p, 64:, 64:].transpose(2, 0, 1, 3, 4)
        return out

    return (y_prompt.astype(np.float32), y_sample.astype(np.float32), unpack("osr"), unpack("osh"))
```

```python
import contextlib
import math
import numpy as np
import concourse.bass as bass
import concourse.mybir as mybir
from concourse.bass_utils import run_bass_kernel_spmd

F32 = mybir.dt.float32
BF16 = mybir.dt.bfloat16
AF = mybir.ActivationFunctionType
ALU = mybir.AluOpType

D = 1024
T = 1024
NCORES = 8
DEPTH = 2
DFF = 2816
NFF = 22
ALPHA = (2 * DEPTH) ** 0.25
LN_EPS = 1e-5
LN8 = math.log(0.125)

C_ID, C_POSP1, C_POSREV, C_BM, C_BMB, C_PCOL = 0, 128, 256, 384, 512, 640
C_MRF, C_MRB, C_MHF, C_MHB = 642, 650, 658, 690
C_DFTC, C_DFTS = 722, 850
C_RMF, C_RMB, C_HMF, C_HMB = 978, 1106, 1234, 1362
C_SEG = 1490
C_PZF, C_PZB = 1490 + 1024, 1490 + 1024 + 128
C_MB = 1490 + 1024 + 256
NCT = 1490 + 1024 + 256 + 2
WUNIT = 2560
NU = 6
NPOOL = 30


class Prog:
    ENG = ("pe", "act", "dve", "pool", "sp")

    def __init__(self, nc):
        self.nc = nc
        self.ops = []

    def op(self, eng, fn, reads=(), writes=(), dma=None):
        self.ops.append(dict(eng=eng, fn=fn, reads=tuple(reads), writes=tuple(writes), dma=dma))

    def emit(self, final_waits=()):
        nc = self.nc
        ops = self.ops
        last_w, readers = {}, {}
        eng_idx = {e: 0 for e in self.ENG}
        dma_gen = {}
        signaling = set()
        for o in ops:
            e = o["eng"]
            idx = eng_idx[e]
            eng_idx[e] += 1
            o["idx"] = idx
            deps = set()
            for r in o["reads"]:
                if r in last_w:
                    deps.add(last_w[r])
            for w in o["writes"]:
                if w in last_w:
                    deps.add(last_w[w])
                for rd in readers.get(w, ()):
                    deps.add(rd)
            if o["dma"] is not None:
                g = dma_gen.get(o["dma"], 0) + 1
                dma_gen[o["dma"]] = g
                ev = ("dma", o["dma"], g)
            else:
                ev = ("eng", e, idx)
            deps.discard(ev)
            o["deps"] = deps
            for d in deps:
                if d[0] == "eng":
                    signaling.add((d[1], d[2]))
            for r in o["reads"]:
                readers.setdefault(r, []).append(ev)
            for w in o["writes"]:
                last_w[w] = ev
                readers[w] = []
        final_events = [last_w[k] for k in final_waits]
        count = {}
        run = {e: 0 for e in self.ENG}
        per_eng = {e: [] for e in self.ENG}
        for o in ops:
            per_eng[o["eng"]].append(o)
            key = (o["eng"], o["idx"])
            if o["dma"] is None and key in signaling:
                run[o["eng"]] += 1
                count[key] = run[o["eng"]]
                o["signal"] = True
            else:
                o["signal"] = False
        dma_keys = sorted(dma_gen.keys())
        with contextlib.ExitStack() as st:
            sem_e = {e: st.enter_context(nc.semaphore("sem_" + e)) for e in self.ENG}
            sem_d = {k: st.enter_context(nc.semaphore("semd_" + k)) for k in dma_keys}
            block = st.enter_context(nc.Block())

            def run_engine(e, eo):
                wm = {}

                def do_waits(deps):
                    need = {}
                    for d in deps:
                        if d[0] == "eng":
                            if d[1] == "pe" and e == "pe":
                                continue
                            k = ("eng", d[1])
                            v = count[(d[1], d[2])]
                        else:
                            k = ("dma", d[1])
                            v = 16 * d[2]
                        if v > need.get(k, 0):
                            need[k] = v
                    for k, v in need.items():
                        if wm.get(k, 0) >= v:
                            continue
                        wm[k] = v
                        s = sem_e[k[1]] if k[0] == "eng" else sem_d[k[1]]
                        eo.wait_ge(s, v)

                for o in per_eng[e]:
                    do_waits(o["deps"])
                    ins = o["fn"](eo)
                    if o["dma"] is not None:
                        ins.then_inc(sem_d[o["dma"]], 16)
                    elif o["signal"]:
                        ins.then_inc(sem_e[e], 1)
                if e == "sp":
                    do_waits(list(final_events) + [("dma", k, g) for k, g in dma_gen.items()])

            @block.tensor
            def _(eng):
                run_engine("pe", eng)

            @block.scalar
            def _(eng):
                run_engine("act", eng)

            @block.vector
            def _(eng):
                run_engine("dve", eng)

            @block.gpsimd
            def _(eng):
                run_engine("pool", eng)

            @block.sync
            def _(eng):
                run_engine("sp", eng)


def K2(name):
    return [name + "_0", name + "_1"]


def build_program(stop=None):
    nc = bass.Bass("TRN2", target_bir_lowering=False)

    def din(name, shape):
        return nc.dram_tensor(name, list(shape), F32, kind="ExternalInput").ap()

    def dout(name, shape):
        return nc.dram_tensor(name, list(shape), F32, kind="ExternalOutput").ap()

    x_d = din("x", [T, D])
    smA_d = din("smA", [128, 128])
    smB_d = din("smB", [128, 128])
    ctab_d = din("ctab", [128, NCT])
    rope_d = din("rope", [128, 2, T])
    dftL_d = din("dftL", [2, T, T])
    s0r_d = din("s0r", [DEPTH, 2, 3, 128, 128])
    s0h_d = din("s0h", [DEPTH, 2, 3, 128, 128])
    wmod_d = din("w_mod", [DEPTH, D, 6 * D])
    win_d = din("w_in", [DEPTH, D, 3712])
    wout_d = din("w_out", [DEPTH, D, D])
    wg_d = din("w_gate", [DEPTH, D, DFF])
    wu_d = din("w_up", [DEPTH, D, DFF])
    wd_d = din("w_down", [DEPTH, DFF, D])
    y_d = dout("y", [T, D])
    osr_d = dout("osr", [DEPTH, 2, 4, 3, 128, 128])
    osh_d = dout("osh", [DEPTH, 2, 4, 3, 128, 128])

    P = Prog(nc)
    st = contextlib.ExitStack()
    with st:
        def sb(name, shape, dt):
            return st.enter_context(nc.sbuf_tensor(name, list(shape), dt))

        def ps(name, shape, dt):
            return st.enter_context(nc.psum_tensor(name, list(shape), dt))

        xT = sb("xT", [128, 8, T], F32)
        ctab = sb("ctab_sb", [128, NCT], F32)
        colsA = sb("colsA", [128, 128], F32)
        colsB = sb("colsB", [128, 128], F32)
        wring = sb("wring", [128, NU, WUNIT], BF16)
        pool = sb("pool", [128, NPOOL, T], BF16)
        t32 = sb("t32", [128, 5, T], F32)
        smt = t32[:, 4, 0:256]
        vbuf = sb("vbuf", [128, 8, 128], BF16)
        vh = sb("vh", [128, 2, 8, 128], BF16)
        vblk = sb("vblk", [128, 8, 4, 128], BF16)
        sst = sb("sst", [128, 2, 8, 128], F32)
        sbfall = sb("sbfall", [128, 2, 32, 128], BF16)
        pbuf = sb("pbuf", [128, 2, 4, 128], BF16)
        modt_t = sb("modt", [128, 2, 64], F32)
        scb = sb("scb", [128, 8], BF16)
        misc = sb("misc", [128, 160], F32)
        dect = sb("dect", [128, 2, 6, 128], F32)
        dmt = sb("dmt", [128, 2, 2, 32], F32)
        onesD = sb("onesD", [128, 128], BF16)
        bd64 = sb("bd64", [128, 128], BF16)
        identb = sb("identb", [128, 128], BF16)
        dftcb = sb("dftcb", [128, 2, 128], BF16)
        sgtb = sb("sgtb", [128, 2, 512], BF16)
        PB = [ps("PB%d" % i, [128, 512], F32) for i in range(4)]
        PS4 = ps("PS4", [128, 512], F32)
        PT5 = ps("PT5", [128, 1024], BF16)
        PU = [ps("PU%d" % i, [128, 512], F32) for i in range(2)]

        ident = ctab[:, C_ID:C_ID + 128]
        PT5f = PT5[:, :].bitcast(F32)

        def pbuf_(i):
            return pool[:, i, :]

        def tt(i):
            return t32[:, i, :]

        wloads = []
        wstate = dict(issued=0, pos=0)
        wflat = wring[:, :, :].rearrange("p a b -> p (a b)")
        unit_occ = [None] * NU
        slot_of = {}
        wdone_flags = {}
        cur_units = {}

        def wsl(ws, a_, b_):
            return wflat[:, ws * WUNIT + a_:ws * WUNIT + b_]

        def WK(ws):
            return ["W%d" % (ws + i) for i in range(cur_units[ws])]

        def wreq(make, nu=2):
            wloads.append((nu, make))
            return len(wloads) - 1

        def _pump():
            while wstate["issued"] < len(wloads):
                j = wstate["issued"]
                nu, make = wloads[j]
                pos = wstate["pos"]
                if pos + nu > NU:
                    pos = 0
                ok = all(unit_occ[u] is None or wdone_flags.get(unit_occ[u], False) for u in range(pos, pos + nu))
                if not ok:
                    return
                for u in range(pos, pos + nu):
                    unit_occ[u] = j
                slot_of[j] = pos
                for (o_ap, i_ap) in make(pos):
                    P.op("pool", (lambda e, o_ap=o_ap, i_ap=i_ap: e.dma_start(out=o_ap, in_=i_ap)),
                         writes=["W%d" % u for u in range(pos, pos + nu)], dma="W%d" % pos)
                wstate["pos"] = (pos + nu) % NU
                wstate["issued"] += 1

        def wuse(i):
            _pump()
            assert wstate["issued"] > i, (i, wstate["issued"])
            cur_units[slot_of[i]] = wloads[i][0]
            return slot_of[i]

        def wdone(i):
            wdone_flags[i] = True
            _pump()

        def wview(s, k, n):
            return wsl(s, 0, k * n).rearrange("p (k n) -> p k n", n=n)

        def mk_std(dram2d, c0, n):
            def make(s):
                return [(wview(s, 8, n), dram2d[:, c0:c0 + n].rearrange("(k p) n -> p k n", p=128))]
            return make

        def mk_grp(dram2d, c0, ng, stride):
            def make(s):
                res = []
                v = wview(s, 8, ng * 128)
                for g in range(ng):
                    res.append((v[:, :, g * 128:(g + 1) * 128],
                                dram2d[:, c0 + g * stride:c0 + g * stride + 128].rearrange("(k p) n -> p k n", p=128)))
                return res
            return make

        def mk_down(dram2d, c0):
            def make(s):
                return [(wview(s, NFF, 128), dram2d[:, c0:c0 + 128].rearrange("(k p) n -> p k n", p=128))]
            return make

        WI = {}

        def reg_mods(lm, js):
            for j in js:
                WI[("mod", lm, j)] = wreq(mk_std(wmod_d[lm], j * 512, 512))

        def slot_mods(l, i):
            if i <= 3:
                return [(l, 4 + 2 * i), (l, 5 + 2 * i)]
            if i <= 5 and l + 1 < DEPTH:
                return [(l + 1, 2 * (i - 4)), (l + 1, 2 * (i - 4) + 1)]
            return []

        reg_mods(0, range(4))
        for l in range(DEPTH):
            WI[("fnet", l)] = wreq(mk_std(win_d[l], 0, 256), 1)
            for (lm, j) in slot_mods(l, 0):
                reg_mods(lm, [j])
            for tbl in range(2):
                for ob in range(2):
                    WI[("dft", l, tbl, ob)] = wreq(mk_std(dftL_d[tbl], ob * 512, 512))
            for p in range(3):
                WI[("ret", l, p)] = wreq(mk_grp(win_d[l], 256 + 128 * p, 4, 384))
                for (lm, j) in slot_mods(l, 1 + p):
                    reg_mods(lm, [j])
            for p in range(3):
                WI[("hg", l, p)] = wreq(mk_grp(win_d[l], 1792 + 128 * p, 5, 384))
                for (lm, j) in slot_mods(l, 4 + p):
                    reg_mods(lm, [j])
            for ob in range(2):
                WI[("out", l, ob)] = wreq(mk_std(wout_d[l], ob * 512, 512))
            for fb in range(11):
                WI[("gate", l, fb)] = wreq(mk_std(wg_d[l], fb * 256, 256), 1)
                WI[("up", l, fb)] = wreq(mk_std(wu_d[l], fb * 256, 256), 1)
            for dc in range(8):
                WI[("down", l, dc)] = wreq(mk_down(wd_d[l], dc * 128))

        P.op("sp", lambda e: e.dma_start(out=ctab[:], in_=ctab_d), writes=["ctab"], dma="c0")
        P.op("sp", lambda e: e.dma_start(out=smt[:, 0:128], in_=smA_d), writes=["T4_0"], dma="c2")
        P.op("sp", lambda e: e.dma_start(out=smt[:, 128:256], in_=smB_d), writes=["T4_0"], dma="c3")
        P.op("dve", lambda e: e.memset(onesD[:], 1.0 / 1024.0), writes=["onesD"])
        P.op("dve", lambda e: e.tensor_scalar(out=bd64[:], in0=ctab[:, C_BM:C_BM + 128], scalar1=1.0 / 64.0, scalar2=None, op0=ALU.mult),
             reads=["ctab"], writes=["bd64"])
        P.op("dve", lambda e: e.tensor_copy(out=identb[:], in_=ident), reads=["ctab"], writes=["identb"])
        P.op("dve", lambda e: e.tensor_copy(out=dftcb[:].rearrange("p a b -> p (a b)"), in_=ctab[:, C_DFTC:C_DFTC + 256]),
             reads=["ctab"], writes=["dftcb"])
        P.op("dve", lambda e: e.memset(sbfall[:].rearrange("p a b c -> p (a b c)"), 0.0), writes=["SBA%d_%d" % (d_, t_) for d_ in range(2) for t_ in range(8)])
        P.op("pool", lambda e: e.memset(vh[:].rearrange("p a b c -> p (a b c)"), 0.0), writes=["vh"])
        P.op("pool", lambda e: e.memset(vblk[:].rearrange("p a b c -> p (a b c)"), 0.0), writes=["vblk"])
        P.op("pe", lambda e: e.transpose(out=PB[0][:, 0:128], in_=smt[:, 0:128], identity=ident), reads=["T4_0", "ctab"], writes=["PB0"])
        P.op("pe", lambda e: e.transpose(out=PB[0][:, 128:256], in_=smt[:, 128:256], identity=ident), reads=["T4_0", "ctab"], writes=["PB0"])
        P.op("act", lambda e: e.activation(out=colsA[:], in_=PB[0][:, 0:128], func=AF.Copy), reads=["PB0"], writes=["colsA"])
        P.op("act", lambda e: e.activation(out=colsB[:], in_=PB[0][:, 128:256], func=AF.Copy), reads=["PB0"], writes=["colsB"])
        P.op("act", lambda e: e.activation(out=scb[:], in_=colsA[:, 96:104], func=AF.Silu), reads=["colsA"], writes=["scb"])
        P.op("act", lambda e: e.activation(out=misc[:, 28:40], in_=colsB[:, 76:88], func=AF.Exp), reads=["colsB"], writes=["misc_lg"])
        P.op("dve", lambda e: e.tensor_scalar(out=misc[:, 16:28], in0=misc[:, 28:40], scalar1=-1.0, scalar2=None, op0=ALU.mult),
             reads=["misc_lg"], writes=["misc_lg2"])
        P.op("act", lambda e: e.activation(out=misc[:, 40:52], in_=misc[:, 16:28], func=AF.Exp, scale=128.0), reads=["misc_lg2"], writes=["misc_D"])
        P.op("act", lambda e: e.activation(out=misc[:, 52:64], in_=colsB[:, 64:76], func=AF.Exp), reads=["colsB"], writes=["misc_e"])
        P.op("dve", lambda e: e.memset(misc[:, 64:76], 0.0), writes=["misc_lb"])
        for d_ in range(2):
            e0 = misc[:, 52 + d_ * 6:52 + d_ * 6 + 3]
            e1 = misc[:, 52 + d_ * 6 + 3:52 + d_ * 6 + 6]
            dst = misc[:, 64 + d_ * 6 + 3:64 + d_ * 6 + 6]
            tmpc = misc[:, 100 + d_ * 3:103 + d_ * 3]
            P.op("dve", lambda e, e0=e0, e1=e1, tmpc=tmpc: e.tensor_tensor(out=tmpc, in0=e0, in1=e1, op=ALU.add), reads=["misc_e"], writes=["misc_t%d" % d_])
            P.op("dve", lambda e, tmpc=tmpc: e.reciprocal(out=tmpc, in_=tmpc), reads=["misc_t%d" % d_], writes=["misc_t%d" % d_])
            P.op("dve", lambda e, e1=e1, tmpc=tmpc, dst=dst: e.tensor_tensor(out=dst, in0=e1, in1=tmpc, op=ALU.mult),
                 reads=["misc_t%d" % d_, "misc_e", "misc_lb"], writes=["misc_lb"])
        P.op("dve", lambda e: e.tensor_scalar(out=misc[:, 76:88], in0=misc[:, 64:76], scalar1=-1.0, scalar2=1.0, op0=ALU.mult, op1=ALU.add),
             reads=["misc_lb"], writes=["misc_oml"])
        P.op("dve", lambda e: e.tensor_scalar(out=misc[:, 88:100], in0=misc[:, 64:76], scalar1=-1.0, scalar2=None, op0=ALU.add),
             reads=["misc_lb"], writes=["misc_lbm1"])

        for t in range(8):
            b = t % 2
            P.op("sp", lambda e, t=t, b=b: e.dma_start(out=t32[:, b, :], in_=x_d[t * 128:(t + 1) * 128, :]),
                 writes=K2("T%d" % b), dma="xin%d" % b)
            for kh in range(2):
                bk = 2 * b + kh
                for kk in range(4):
                    k = kh * 4 + kk
                    src = t32[:, b, k * 128:(k + 1) * 128]
                    P.op("pe", lambda e, bk=bk, kk=kk, src=src: e.transpose(out=PB[bk][:, kk * 128:(kk + 1) * 128], in_=src, identity=ident),
                         reads=K2("T%d" % b) + ["ctab"], writes=["PB%d" % bk])
                P.op("act" if kh == 0 else "dve",
                     (lambda e, kh=kh, t=t, bk=bk: e.activation(out=xT[:, kh * 4:kh * 4 + 4, t * 128:(t + 1) * 128],
                                                                 in_=PB[bk][:].rearrange("p (a b) -> p a b", b=128), func=AF.Copy)) if kh == 0 else
                     (lambda e, kh=kh, t=t, bk=bk: e.tensor_copy(out=xT[:, kh * 4:kh * 4 + 4, t * 128:(t + 1) * 128],
                                                                  in_=PB[bk][:].rearrange("p (a b) -> p a b", b=128))),
                     reads=["PB%d" % bk], writes=["xT%d_%d" % (k_, t // 4) for k_ in range(kh * 4, kh * 4 + 4)])

        XK = lambda k: ["xT%d_0" % k, "xT%d_1" % k]

        def norm_stats(chunks, chunk_keys, ones_ap, ones_key, center, mean_t, rstd_t, tmpb_all):
            n = len(chunks)
            for ci, (ch, ck) in enumerate(zip(chunks, chunk_keys)):
                tmpb = tmpb_all[ci % len(tmpb_all)]
                xb, xq = pbuf_(tmpb[0]), pbuf_(tmpb[1])
                if n == 1:
                    for tb in range(2):
                        hsl = slice(tb * 512, (tb + 1) * 512)
                        if center:
                            P.op("act", lambda e, ch=ch, xb=xb, hsl=hsl: e.activation(out=xb[:, hsl], in_=ch[:, hsl], func=AF.Copy),
                                 reads=[ck[tb]], writes=["B%d_%d" % (tmpb[0], tb)])
                        P.op("dve", lambda e, ch=ch, xq=xq, hsl=hsl: e.tensor_tensor(out=xq[:, hsl], in0=ch[:, hsl], in1=ch[:, hsl], op=ALU.mult),
                             reads=[ck[tb]], writes=["B%d_%d" % (tmpb[1], tb)])
                else:
                    if center:
                        P.op("act", lambda e, ch=ch, xb=xb: e.activation(out=xb, in_=ch, func=AF.Copy), reads=ck, writes=K2("B%d" % tmpb[0]))
                    P.op("dve", lambda e, ch=ch, xq=xq: e.tensor_tensor(out=xq, in0=ch, in1=ch, op=ALU.mult), reads=ck, writes=K2("B%d" % tmpb[1]))
                for tb in range(2):
                    if center:
                        P.op("pe", lambda e, tb=tb, xb=xb, ci=ci: e.matmul(PB[tb][:], lhsT=ones_ap, rhs=xb[:, tb * 512:(tb + 1) * 512],
                                                                         start=(ci == 0), stop=(ci == n - 1)),
                             reads=["B%d_%d" % (tmpb[0], tb), ones_key], writes=["PB%d" % tb])
                    P.op("pe", lambda e, tb=tb, xq=xq, ci=ci: e.matmul(PB[2 + tb][:], lhsT=ones_ap, rhs=xq[:, tb * 512:(tb + 1) * 512],
                                                                     start=(ci == 0), stop=(ci == n - 1)),
                         reads=["B%d_%d" % (tmpb[1], tb), ones_key], writes=["PB%d" % (2 + tb)])
            mean, rstd = tt(mean_t), tt(rstd_t)
            SL = [slice(0, 512), slice(512, 1024)]
            MKk = ["T%d_%d" % (mean_t, tb) for tb in range(2)]
            RKk = ["T%d_%d" % (rstd_t, tb) for tb in range(2)]
            if center:
                for tb in range(2):
                    P.op("act", lambda e, tb=tb: e.activation(out=mean[:, SL[tb]], in_=PB[tb][:], func=AF.Copy),
                         reads=["PB%d" % tb], writes=[MKk[tb]])
                for tb in range(2):
                    P.op("dve", lambda e, tb=tb: e.tensor_tensor(out=rstd[:, SL[tb]], in0=mean[:, SL[tb]], in1=mean[:, SL[tb]], op=ALU.mult),
                         reads=[MKk[tb]], writes=[RKk[tb]])
                for tb in range(2):
                    P.op("dve", lambda e, tb=tb: e.tensor_tensor(out=rstd[:, SL[tb]], in0=PB[2 + tb][:], in1=rstd[:, SL[tb]], op=ALU.subtract),
                         reads=["PB%d" % (2 + tb), RKk[tb]], writes=[RKk[tb]])
                for tb in range(2):
                    P.op("act", lambda e, tb=tb: e.activation(out=rstd[:, SL[tb]], in_=rstd[:, SL[tb]], func=AF.Ln, bias=epsc, scale=1.0),
                         reads=[RKk[tb], "epsc"], writes=[RKk[tb]])
            else:
                for tb in range(2):
                    P.op("act", lambda e, tb=tb: e.activation(out=rstd[:, SL[tb]], in_=PB[2 + tb][:], func=AF.Ln, bias=epsc, scale=1.0),
                         reads=["PB%d" % (2 + tb), "epsc"], writes=[RKk[tb]])
            for tb in range(2):
                P.op("act", lambda e, tb=tb: e.activation(out=rstd[:, SL[tb]], in_=rstd[:, SL[tb]], func=AF.Exp, scale=-0.5),
                     reads=[RKk[tb]], writes=[RKk[tb]])

        epsc = misc[:, 110:111]
        P.op("dve", lambda e: e.memset(epsc, LN_EPS), writes=["epsc"])
        c80p = misc[:, 112:113]
        c80n = misc[:, 113:114]
        P.op("dve", lambda e: e.memset(c80p, 80.0), writes=["c80"])
        P.op("dve", lambda e: e.memset(c80n, -80.0), writes=["c80"])
        ln8c = misc[:, 111:112]
        P.op("dve", lambda e: e.memset(ln8c, LN8), writes=["ln8c"])

        def ln_apply(scale_cols, bias_cols, col_keys, dst_fn, dst_keys_fn, tmp_t):
            mean, rstd = tt(0), tt(1)
            for kp in range(4):
                ks_ = (2 * kp, 2 * kp + 1)
                tqs = {k: tmp_t + (k % 2) for k in ks_}
                for k in ks_:
                    tq = tqs[k]
                    tmp = tt(tq)
                    P.op("dve", lambda e, k=k, tmp=tmp: e.tensor_tensor(out=tmp, in0=xT[:, k, :], in1=mean, op=ALU.subtract),
                         reads=XK(k) + K2("T0"), writes=K2("T%d" % tq))
                for k in ks_:
                    tq = tqs[k]
                    tmp = tt(tq)
                    P.op("dve", lambda e, tmp=tmp: e.tensor_tensor(out=tmp, in0=tmp, in1=rstd, op=ALU.mult),
                         reads=K2("T%d" % tq) + K2("T1"), writes=K2("T%d" % tq))
                for k in ks_:
                    tq = tqs[k]
                    tmp = tt(tq)
                    P.op("act", lambda e, k=k, tmp=tmp: e.activation(out=dst_fn(k), in_=tmp, func=AF.Identity, scale=scale_cols[:, k:k + 1], bias=bias_cols[:, k:k + 1]),
                         reads=K2("T%d" % tq) + col_keys, writes=dst_keys_fn(k))

        def layer_norm_x(scale_cols, bias_cols, col_keys, dst_fn, dst_keys_fn):
            norm_stats([xT[:, k, :] for k in range(8)], [XK(k) for k in range(8)], onesD[:], "onesD", True, 0, 1, [(26, 27), (28, 29)])
            ln_apply(scale_cols, bias_cols, col_keys, dst_fn, dst_keys_fn, 2)

        HB = list(range(0, 8))
        MC = list(range(8, 16))
        OPB = list(range(16, 25))
        AB = list(range(8, 30))

        def fm_proj(ws, n_, c0, tb, bank):
            for k in range(8):
                P.op("pe", lambda e, k=k: e.matmul(PB[bank][:], lhsT=wsl(ws, k * n_ + c0, k * n_ + c0 + 128),
                                                   rhs=pool[:, HB[k], tb * 512:(tb + 1) * 512], start=(k == 0), stop=(k == 7)),
                     reads=[*WK(ws), "B%d_%d" % (HB[k], tb)], writes=["PB%d" % bank])

        def fm_proj2(ws, n_, c0, tb, bank_ap, bank_key):
            for k in range(8):
                P.op("pe", lambda e, k=k: e.matmul(bank_ap, lhsT=wsl(ws, k * n_ + c0, k * n_ + c0 + 128),
                                                   rhs=pool[:, HB[k], tb * 512:(tb + 1) * 512], start=(k == 0), stop=(k == 7)),
                     reads=[*WK(ws), "B%d_%d" % (HB[k], tb)], writes=[bank_key])

        def tm_proj_tile(ws, n, c0, ncols, t, out_ap, out_key):
            for k in range(8):
                P.op("pe", lambda e, k=k: e.matmul(out_ap, lhsT=pool[:, HB[k], t * 128:(t + 1) * 128],
                                                   rhs=wsl(ws, k * n + c0, k * n + c0 + ncols), start=(k == 0), stop=(k == 7)),
                     reads=[*WK(ws), "B%d_%d" % (HB[k], t // 4)], writes=[out_key])

        def gla_pair(l, p, csz, qq, kkh, ks, masks, s0_d, os_d, mcol_off, gate_b, kind, extra_w=()):
            n = 128 // csz
            G = 8 * n
            cps = 256 // csz
            oacc = tt(0)
            cur = {}
            um = tt(1)
            for d in range(2):
                g0 = 0 if d == 0 else G - 1
                P.op("sp", lambda e, d=d: e.dma_start(out=sst[:, d, 1, :], in_=s0_d[l, d, p]), writes=["S%d_1" % d], dma="s0_%d" % d)
                P.op("act", lambda e, d=d, g0=g0: e.activation(out=sbfall[:, d, g0, :], in_=sst[:, d, 1, :], func=AF.Copy),
                     reads=["S%d_1" % d], writes=["SBA%d_%d" % (d, g0 // n)] + list(extra_w))
                cur[d] = 1
            UB = [[(PU[0], "PU0"), (PS4, "PS4")], [(PU[1], "PU1"), (PB[3], "PB3")]]
            for s in range(8):
                tiles = [s, 7 - s]
                for d in range(2):
                    t = tiles[d]
                    hk = "_%d" % (t // 4)
                    ub_t, ub_k = UB[d][s % 2]
                    for c_ in range(n):
                        for hh in range(2):
                            if csz == 32:
                                rhs_ap, rkey = vblk[:, t, c_, hh * 64:(hh + 1) * 64], "vblk"
                            else:
                                rhs_ap, rkey = vbuf[:, t, hh * 64:(hh + 1) * 64], "vbuf"
                            P.op("pe", lambda e, d=d, t=t, hh=hh, c_=c_, ub_t=ub_t, rhs_ap=rhs_ap: e.matmul(
                                    ub_t[:, c_ * 128 + hh * 64:c_ * 128 + (hh + 1) * 64],
                                    lhsT=pool[:, ks[d][hh], t * 128:(t + 1) * 128],
                                    rhs=rhs_ap, start=True, stop=True),
                                 reads=["B%d%s" % (ks[d][hh], hk), rkey], writes=[ub_k])
                for ci in range(n):
                    for d in range(2):
                        t = tiles[d]
                        c = ci if d == 0 else n - 1 - ci
                        g = t * n + c
                        si = cur[d]
                        if d == 0:
                            fin = ((g + 1) % cps == 0)
                            seq = (g + 1) // cps - 1
                            nxt_boundary = fin and (g != G - 1)
                            last = (g == G - 1)
                            gn = g + 1
                        else:
                            fin = (g % cps == 0)
                            seq = g // cps
                            nxt_boundary = fin and (g != 0)
                            last = (g == 0)
                            gn = g - 1
                        ni = (4 + seq) if fin else ((si + 1) % 4 if si < 4 else 0)
                        dcol = dmt[:, mcol_off, d, g:g + 1]
                        ub_t, ub_k = UB[d][s % 2]
                        ucol, ukey = ub_t[:, c * 128:(c + 1) * 128], ub_k
                        P.op("dve", lambda e, d=d, si=si, ni=ni, dcol=dcol, ucol=ucol: e.scalar_tensor_tensor(
                                out=sst[:, d, ni, :], in0=sst[:, d, si, :], scalar=dcol, in1=ucol, op0=ALU.mult, op1=ALU.add),
                             reads=["S%d_%d" % (d, si), "dmt%d_%d" % (mcol_off, d), ukey], writes=["S%d_%d" % (d, ni)])
                        if fin:
                            P.op("sp", lambda e, d=d, ni=ni, seq=seq: e.dma_start(out=os_d[l, d, seq, p], in_=sst[:, d, ni, :]),
                                 reads=["S%d_%d" % (d, ni)], writes=["os_%s_%d_%d_%d_%d" % (kind, l, d, seq, p)],
                                 dma="os%d_%d" % (d, ni))
                        if not last:
                            mk = C_MB if nxt_boundary else C_MB + 1
                            P.op("act", lambda e, d=d, ni=ni, gn=gn, mk=mk: e.activation(out=sbfall[:, d, gn, :], in_=sst[:, d, ni, :],
                                                                                      func=AF.Identity, scale=ctab[:, mk:mk + 1]),
                                 reads=["S%d_%d" % (d, ni), "ctab"], writes=["SBA%d_%d" % (d, gn // n)])
                        cur[d] = ni
            SCB = [[PS4, PB[3]], [PB[2], PU[0]]]
            SCK = [["PS4", "PB3"], ["PB2", "PU0"]]
            OAB, OAK = [PB[0], PB[1]], ["PB0", "PB1"]

            def scores(t):
                tsl = slice(t * 128, (t + 1) * 128)
                hk = "_%d" % (t // 4)
                par = t % 2
                for d in range(2):
                    for h in range(2):
                        P.op("pe", lambda e, d=d, h=h, par=par, tsl=tsl: e.matmul(SCB[d][par][:, h * 128:(h + 1) * 128], lhsT=pool[:, kkh[d][h], tsl],
                                                                                 rhs=pool[:, qq[d], tsl], start=True, stop=True),
                             reads=["B%d%s" % (kkh[d][h], hk), "B%d%s" % (qq[d], hk)], writes=[SCK[d][par]])
                for d in range(2):
                    P.op("dve", lambda e, d=d, par=par: e.tensor_tensor(
                            out=pbuf[:, par, 2 * d:2 * d + 2, :], in0=SCB[d][par][:, 0:256].rearrange("p (h c) -> p h c", h=2),
                            in1=ctab[:, masks[d]:masks[d] + 128].unsqueeze(1).to_broadcast([128, 2, 128]), op=ALU.mult),
                         reads=[SCK[d][par], "ctab"], writes=["pbuf%d_%d" % (par, d)])

            def outputs(t):
                tsl = slice(t * 128, (t + 1) * 128)
                hk = "_%d" % (t // 4)
                par = t % 2
                osl = OAB[par][:, 0:128]
                first = True
                for d in range(2):
                    for h in range(2):
                        P.op("pe", lambda e, d=d, h=h, t=t, par=par, osl=osl, first=first: e.matmul(osl, lhsT=vh[:, h, t, :], rhs=pbuf[:, par, 2 * d + h, :],
                                                                                                start=first, stop=False),
                             reads=["vh", "pbuf%d_%d" % (par, d)], writes=[OAK[par]])
                        first = False
                for d in range(2):
                    for c in range(n):
                        g = t * n + c
                        csl = slice(t * 128 + c * csz, t * 128 + (c + 1) * csz)
                        lastmm = (d == 1 and c == n - 1)
                        P.op("pe", lambda e, d=d, g=g, c=c, csl=csl, par=par, lastmm=lastmm: e.matmul(
                                OAB[par][:, c * csz:(c + 1) * csz], lhsT=sbfall[:, d, g, :], rhs=pool[:, qq[d], csl],
                                start=False, stop=lastmm),
                             reads=["SBA%d_%d" % (d, t), "B%d%s" % (qq[d], hk)], writes=[OAK[par]])
                P.op("act", lambda e, osl=osl, tsl=tsl: e.activation(out=oacc[:, tsl], in_=osl, func=AF.Copy),
                     reads=[OAK[par]], writes=["T0%s" % hk])

            for t in range(9):
                if t < 8:
                    scores(t)
                if t >= 1:
                    outputs(t - 1)

        for l in range(DEPTH):
            mpar = l % 2
            modt = modt_t[:, mpar, :]
            MTK = lambda js: ["mt%d_%d" % (mpar, j) for j in js]

            def mod_block(lm, j12, bank, bank_key):
                ws_ = wuse(WI[("mod", lm, j12)])
                for jj in range(4):
                    for k in range(8):
                        P.op("pe", lambda e, ws_=ws_, jj=jj, k=k: e.matmul(bank[:, jj:jj + 1], lhsT=wsl(ws_, k * 512 + jj * 128, k * 512 + (jj + 1) * 128),
                                                                          rhs=scb[:, k:k + 1], start=(k == 0), stop=(k == 7)),
                             reads=[*WK(ws_), "scb"], writes=[bank_key])
                wdone(WI[("mod", lm, j12)])
                P.op("dve", lambda e: e.tensor_tensor(out=modt_t[:, lm % 2, 4 * j12:4 * j12 + 4], in0=bank[:, 0:4],
                                                      in1=colsA[:, lm * 48 + 4 * j12:lm * 48 + 4 * j12 + 4], op=ALU.add),
                     reads=[bank_key, "colsA"], writes=["mt%d_%d" % (lm % 2, j12)])

            def run_slot_mods(i):
                for (lm, j) in slot_mods(l, i):
                    mod_block(lm, j, PU[1], "PU1")

            if l == 0:
                for j12 in range(4):
                    mod_block(0, j12, PB[j12 % 2], "PB%d" % (j12 % 2))
            P.op("dve", lambda e, modt=modt: e.tensor_scalar(out=modt[:, 48:56], in0=modt[:, 8:16], scalar1=1.0, scalar2=None, op0=ALU.add),
                 reads=MTK([2, 3]), writes=["ma1_%d" % mpar])
            layer_norm_x(modt[:, 48:56], modt[:, 0:8], MTK([0, 1]) + ["ma1_%d" % mpar], lambda k: pbuf_(HB[k]), lambda k: K2("B%d" % HB[k]))

            ws = wuse(WI[("fnet", l)])
            ub_, pc_ = OPB[0:2], OPB[2:6]
            uview = pool[:, ub_[0]:ub_[0] + 2, :].rearrange("p a b -> p (a b)").rearrange("p (t c) -> p t c", c=256)
            UK = K2("B%d" % ub_[0]) + K2("B%d" % ub_[1])
            for t in range(8):
                bank = t % 2
                tm_proj_tile(ws, 256, 0, 256, t, PB[bank][:, 0:256], "PB%d" % bank)
                P.op("act", lambda e, t=t, bank=bank: e.activation(out=uview[:, t, :], in_=PB[bank][:, 0:256], func=AF.Copy),
                     reads=["PB%d" % bank], writes=UK)
            wdone(WI[("fnet", l)])
            run_slot_mods(0)
            for tbl in range(2):
                for ob in range(2):
                    ws = wuse(WI[("dft", l, tbl, ob)])
                    for ct in range(2):
                        bank = 2 + ct
                        for kt in range(8):
                            P.op("pe", lambda e, ws=ws, ct=ct, kt=kt, bank=bank: e.matmul(PB[bank][:], lhsT=uview[:, kt, ct * 128:(ct + 1) * 128],
                                                                                          rhs=wsl(ws, kt * 512, (kt + 1) * 512), start=(kt == 0), stop=(kt == 7)),
                                 reads=UK + [*WK(ws)], writes=["PB%d" % bank])
                        dstb = pc_[tbl * 2 + ct]
                        P.op("act", lambda e, bank=bank, dstb=dstb, ob=ob: e.activation(out=pool[:, dstb, ob * 512:(ob + 1) * 512], in_=PB[bank][:], func=AF.Copy),
                             reads=["PB%d" % bank], writes=["B%d_%d" % (dstb, ob)])
                    wdone(WI[("dft", l, tbl, ob)])
            for ct in range(2):
                for ob in range(2):
                    bank = ob
                    for tbl in range(2):
                        srcb = pc_[tbl * 2 + ct]
                        P.op("pe", lambda e, tbl=tbl, srcb=srcb, ob=ob, bank=bank: e.matmul(PB[bank][:], lhsT=dftcb[:, tbl, :], rhs=pool[:, srcb, ob * 512:(ob + 1) * 512],
                                                                                          start=(tbl == 0), stop=(tbl == 1)),
                             reads=["dftcb", "B%d_%d" % (srcb, ob)], writes=["PB%d" % bank])
                    P.op("act", lambda e, ct=ct, ob=ob, bank=bank: e.activation(out=pool[:, MC[ct], ob * 512:(ob + 1) * 512], in_=PB[bank][:], func=AF.Copy),
                         reads=["PB%d" % bank], writes=["B%d_%d" % (MC[ct], ob)])

            def v_proj(ws, n, c0, need_blk):
                for t in range(8):
                    bank = t % 2
                    tm_proj_tile(ws, n, c0, 128, t, PB[bank][:, 0:128], "PB%d" % bank)
                    P.op("act", lambda e, t=t, bank=bank: e.activation(out=vbuf[:, t, :], in_=PB[bank][:, 0:128], func=AF.Copy),
                         reads=["PB%d" % bank], writes=["vbuf"])
                for h in range(2):
                    P.op("act", lambda e, h=h: e.activation(out=vh[:, h, :, h * 64:(h + 1) * 64], in_=vbuf[:, :, h * 64:(h + 1) * 64], func=AF.Copy),
                         reads=["vbuf"], writes=["vh"])
                if need_blk:
                    for c in range(4):
                        P.op("dve", lambda e, c=c: e.tensor_copy(out=vblk[c * 32:(c + 1) * 32, :, c, :], in_=vbuf[c * 32:(c + 1) * 32, :, :]),
                             reads=["vbuf"], writes=["vblk"])

            def v_proj2(ws, n, c0, need_blk):
                for half in range(2):
                    for tq in range(4):
                        t = half * 4 + tq
                        tm_proj_tile(ws, n, c0, 128, t, PU[1][:, tq * 128:(tq + 1) * 128], "PU1")
                    P.op("act", lambda e, half=half: e.activation(out=vbuf[:, half * 4:(half + 1) * 4, :],
                                                                  in_=PU[1][:].rearrange("p (a b) -> p a b", b=128), func=AF.Copy),
                         reads=["PU1"], writes=["vbuf"])
                for h in range(2):
                    P.op("act", lambda e, h=h: e.activation(out=vh[:, h, :, h * 64:(h + 1) * 64], in_=vbuf[:, :, h * 64:(h + 1) * 64], func=AF.Copy),
                         reads=["vbuf"], writes=["vh"])
                if need_blk:
                    for c in range(4):
                        P.op("dve", lambda e, c=c: e.tensor_copy(out=vblk[c * 32:(c + 1) * 32, :, c, :], in_=vbuf[c * 32:(c + 1) * 32, :, :]),
                             reads=["vbuf"], writes=["vblk"])

            def finish_pair(kind, gate_b, mc_idx):
                oacc = tt(0)
                center = (kind == "ret")
                norm_stats([oacc], [K2("T0")], bd64[:], "bd64", center, 3, 4, [(26, 27)])
                tmp = tt(2)
                HSL = [slice(0, 512), slice(512, 1024)]
                if center:
                    for tb in range(2):
                        P.op("dve", lambda e, tb=tb: e.tensor_tensor(out=tmp[:, HSL[tb]], in0=oacc[:, HSL[tb]], in1=tt(3)[:, HSL[tb]], op=ALU.subtract),
                             reads=["T0_%d" % tb, "T3_%d" % tb], writes=["T2_%d" % tb])
                    for tb in range(2):
                        P.op("dve", lambda e, tb=tb: e.tensor_tensor(out=tmp[:, HSL[tb]], in0=tmp[:, HSL[tb]], in1=tt(4)[:, HSL[tb]], op=ALU.mult),
                             reads=["T2_%d" % tb, "T4_%d" % tb], writes=["T2_%d" % tb])
                else:
                    for tb in range(2):
                        P.op("dve", lambda e, tb=tb: e.tensor_tensor(out=tmp[:, HSL[tb]], in0=oacc[:, HSL[tb]], in1=tt(4)[:, HSL[tb]], op=ALU.mult),
                             reads=["T0_%d" % tb, "T4_%d" % tb], writes=["T2_%d" % tb])
                for tb in range(2):
                    P.op("dve", lambda e, tb=tb: e.tensor_tensor(out=pool[:, MC[mc_idx], HSL[tb]], in0=tmp[:, HSL[tb]], in1=pool[:, gate_b, HSL[tb]], op=ALU.mult),
                         reads=["T2_%d" % tb, "B%d_%d" % (gate_b, tb)], writes=["B%d_%d" % (MC[mc_idx], tb)])

            def ks_transposes(src_b, dst_b_list, evac):
                for t in range(8):
                    P.op("pe", lambda e, t=t: e.transpose(out=PT5[:, (t % 8) * 128:(t % 8 + 1) * 128], in_=pool[:, src_b, t * 128:(t + 1) * 128], identity=identb[:]),
                         reads=["B%d_%d" % (src_b, t // 4), "identb"], writes=["PT5"])
                for t in range(8):
                    evac(t, PT5[:, (t % 8) * 128:(t % 8 + 1) * 128], "PT5")

            for d in range(2):
                for h in range(2):
                    bi_ = OPB[2 + d * 2 + h]
                    P.op("dve", lambda e, h=h, bi_=bi_: e.memset(pool[(1 - h) * 64:(2 - h) * 64, bi_, :], 0.0), writes=K2("B%d" % bi_))
            def ret_tables(p_, par):
                for d in range(2):
                    cidx = l * 6 + d * 3 + p_
                    lgc = misc[:, 16 + cidx:17 + cidx]
                    nlgc = misc[:, 28 + cidx:29 + cidx]
                    pos = ctab[:, C_POSP1:C_POSP1 + 128] if d == 0 else ctab[:, C_POSREV:C_POSREV + 128]
                    P.op("act", lambda e, d=d, lgc=lgc, pos=pos: e.activation(out=dect[:, par, 2 * d, :], in_=pos, func=AF.Exp, scale=lgc),
                         reads=["ctab", "misc_lg2"], writes=["dect%d_%d" % (par, 2 * d)])
                    P.op("act", lambda e, d=d, nlgc=nlgc, pos=pos: e.activation(out=dect[:, par, 2 * d + 1, :], in_=pos, func=AF.Exp, scale=nlgc, bias=ln8c),
                         reads=["ctab", "misc_lg", "ln8c"], writes=["dect%d_%d" % (par, 2 * d + 1)])
                    pz = ctab[:, C_PZF:C_PZF + 128] if d == 0 else ctab[:, C_PZB:C_PZB + 128]
                    P.op("act", lambda e, d=d, lgc=lgc, pz=pz: e.activation(out=dect[:, par, 4 + d, :], in_=pz, func=AF.Exp, scale=lgc, bias=ln8c),
                         reads=["ctab", "misc_lg2", "ln8c"], writes=["dect%d_%d" % (par, 4 + d)])
                    P.op("pe", lambda e, d=d: e.transpose(out=PT5f[:, d * 128:(d + 1) * 128], in_=dect[:, par, 4 + d, :], identity=ident),
                         reads=["dect%d_%d" % (par, 4 + d), "ctab"], writes=["PT5"])
                for d in range(2):
                    cidx = l * 6 + d * 3 + p_
                    P.op("act", lambda e, d=d: e.activation(out=dect[:, par, 4 + d, :], in_=PT5f[:, d * 128:(d + 1) * 128], func=AF.Copy),
                         reads=["PT5"], writes=["dect%d_%d" % (par, 4 + d)])
                    P.op("dve", lambda e, d=d, cidx=cidx: e.tensor_tensor(out=dmt[:, par, d, 0:8], in0=misc[:, 40 + cidx:41 + cidx].to_broadcast([128, 8]),
                                                                         in1=ctab[:, (C_MRF if d == 0 else C_MRB):(C_MRF if d == 0 else C_MRB) + 8], op=ALU.mult),
                         reads=["misc_D", "ctab"], writes=["dmt%d_%d" % (par, d)])

            ret_tables(0, 0)
            for p in range(3):
                ws = wuse(WI[("ret", l, p)])
                qq = [OPB[0], OPB[1]]
                kkh = [[OPB[2], OPB[3]], [OPB[4], OPB[5]]]
                ks = [[OPB[6], 26], [OPB[7], 27]]
                for d_ in range(2):
                    for hh_ in range(2):
                        bz = ks[d_][hh_]
                        P.op("dve", lambda e, bz=bz: e.memset(pool[:, bz, :], 0.0), writes=K2("B%d" % bz))
                gate_b = OPB[8]
                par = p % 2
                wv = wsl(ws, 0, 4096).rearrange("p (k g a c) -> p k g a c", k=8, g=16, a=2)
                wsw = pool[:, 28:30, :].rearrange("p a b -> p (a b)").rearrange("p (k n) -> p k n", n=256)
                sv = wsw.rearrange("p k (g a c) -> p k g a c", g=8, a=2)
                for a in range(2):
                    P.op("act", lambda e, a=a, wv=wv, sv=sv: e.activation(out=sv[:, :, :, a, :], in_=wv[:, :, 0:8, 1 - a, :], func=AF.Copy),
                         reads=[*WK(ws)], writes=K2("B28") + K2("B29"))
                rope = t32[:, 0:2, :]
                P.op("sp", lambda e: e.dma_start(out=t32[:, 0:2, :], in_=rope_d), writes=K2("T0") + K2("T1"), dma="c1")
                for which in range(2):
                    rot = tt(2 + which)
                    SLs = [slice(0, 512), slice(512, 1024)]
                    for tb in range(2):
                        ba, bb_ = 2 * tb, 2 * tb + 1
                        fm_proj(ws, 512, which * 128, tb, ba)
                        for k in range(8):
                            P.op("pe", lambda e, k=k, which=which, tb=tb, bb_=bb_: e.matmul(PB[bb_][:], lhsT=wsw[:, k, which * 128:(which + 1) * 128],
                                                                                        rhs=pool[:, HB[k], tb * 512:(tb + 1) * 512], start=(k == 0), stop=(k == 7)),
                                 reads=K2("B28") + K2("B29") + ["B%d_%d" % (HB[k], tb)], writes=["PB%d" % bb_])
                    for tb in range(2):
                        ba, bb_ = 2 * tb, 2 * tb + 1
                        sl = SLs[tb]
                        P.op("dve", lambda e, rot=rot, sl=sl, ba=ba: e.tensor_tensor(out=rot[:, sl], in0=PB[ba][:], in1=rope[:, 0, sl], op=ALU.mult),
                             reads=["PB%d" % ba, "T0_%d" % tb], writes=["T%d_%d" % (2 + which, tb)])
                        P.op("dve", lambda e, sl=sl, bb_=bb_: e.tensor_tensor(out=tt(4)[:, sl], in0=PB[bb_][:], in1=rope[:, 1, sl], op=ALU.mult),
                             reads=["PB%d" % bb_, "T1_%d" % tb], writes=["T4_%d" % tb])
                    for tb in range(2):
                        sl = SLs[tb]
                        P.op("dve", lambda e, rot=rot, sl=sl: e.tensor_tensor(out=rot[:, sl], in0=rot[:, sl], in1=tt(4)[:, sl], op=ALU.add),
                             reads=["T%d_%d" % (2 + which, tb), "T4_%d" % tb], writes=["T%d_%d" % (2 + which, tb)])
                qrot, krot = tt(2), tt(3)
                r3 = lambda ap: ap.rearrange("p (t c) -> p t c", c=128)
                for d in range(2):
                    eq = dect[:, par, 2 * d, :]
                    ek = dect[:, par, 2 * d + 1, :]
                    P.op("dve", lambda e, d=d, eq=eq: e.tensor_tensor(out=r3(pbuf_(qq[d])), in0=r3(qrot), in1=eq.unsqueeze(1).to_broadcast([128, 8, 128]), op=ALU.mult),
                         reads=K2("T2") + ["dect%d_%d" % (par, 2 * d)], writes=K2("B%d" % qq[d]))
                    for h in range(2):
                        hs = slice(h * 64, (h + 1) * 64)
                        P.op("dve", lambda e, d=d, h=h, hs=hs, ek=ek: e.tensor_tensor(out=r3(pool[hs, kkh[d][h], :]), in0=r3(krot[hs, :]),
                                                                                  in1=ek[hs, :].unsqueeze(1).to_broadcast([64, 8, 128]), op=ALU.mult),
                             reads=K2("T3") + ["dect%d_%d" % (par, 2 * d + 1)], writes=K2("B%d" % kkh[d][h]))
                kb = 25
                P.op("act", lambda e: e.activation(out=pbuf_(kb), in_=krot, func=AF.Copy), reads=K2("T3"), writes=K2("B%d" % kb))

                def evac_ret(t, src, skey, par=par, ks=ks):
                    for d in range(2):
                        zt = dect[:, par, 4 + d, :]
                        for hh in range(2):
                            hcs = slice(hh * 64, (hh + 1) * 64)
                            P.op("dve", lambda e, d=d, t=t, src=src, zt=zt, ks=ks, hh=hh, hcs=hcs: e.tensor_tensor(
                                    out=pool[:, ks[d][hh], t * 128 + hh * 64:t * 128 + (hh + 1) * 64], in0=src[:, hcs], in1=zt[:, hcs], op=ALU.mult),
                                 reads=[skey, "dect%d_%d" % (par, 4 + d)], writes=["B%d_%d" % (ks[d][hh], t // 4)])
                ks_transposes(kb, ks, evac_ret)
                v_proj(ws, 512, 256, False)
                for tb in range(2):
                    fm_proj(ws, 512, 384, tb, 2 + tb)
                    P.op("act", lambda e, tb=tb: e.activation(out=pool[:, gate_b, tb * 512:(tb + 1) * 512], in_=PB[2 + tb][:], func=AF.Silu),
                         reads=["PB%d" % (2 + tb)], writes=["B%d_%d" % (gate_b, tb)])
                wdone(WI[("ret", l, p)])
                run_slot_mods(1 + p)
                if p < 2:
                    ret_tables(p + 1, (p + 1) % 2)
                gla_pair(l, p, 128, qq, kkh, ks, (C_RMF, C_RMB), s0r_d, osr_d, par, gate_b, "r")
                finish_pair("ret", gate_b, 2 + p)

            GBK = [(PS4[:], "PS4"), (PU[0][:], "PU0")]
            xf = sbfall[:, :, :, :].rearrange("p a b c -> p (a b c)").bitcast(F32)
            ALL_SBA = ["SBA%d_%d" % (d_, t_) for d_ in range(2) for t_ in range(8)]
            XKEYS = ["X%d_%d" % (i_, h_) for i_ in range(4) for h_ in range(2)]
            HS = [slice(0, 512), slice(512, 1024)]
            for p in range(3):
                ws = wuse(WI[("hg", l, p)])
                qq = [OPB[0], OPB[1]]
                kkh = [[OPB[2], OPB[3]], [OPB[4], OPB[5]]]
                ks = [[OPB[6], 28], [OPB[7], 29]]
                for d_ in range(2):
                    for hh_ in range(2):
                        bz = ks[d_][hh_]
                        P.op("dve", lambda e, bz=bz: e.memset(pool[:, bz, :], 0.0), writes=K2("B%d" % bz))
                gate_b = OPB[8]
                dpar = (p + 1) % 2
                TSET = [dict(sig=tt(0), kf=tt(1), bb=tt(3), einv=tt(4), K=("T0", "T1", "T3", "T4"), kb=25,
                             banks=[(PB[2][:], "PB2"), (PB[3][:], "PB3")]),
                        dict(sig=xf[:, 0:1024], kf=xf[:, 1024:2048], bb=xf[:, 2048:3072], einv=xf[:, 3072:4096],
                             K=("X0", "X1", "X2", "X3"), kb=26, banks=GBK)]
                for tb in range(2):
                    fm_proj2(ws, 640, 512, tb, GBK[tb][0], GBK[tb][1])
                    P.op("act", lambda e, tb=tb, gate_b=gate_b: e.activation(out=pool[:, gate_b, tb * 512:(tb + 1) * 512], in_=GBK[tb][0], func=AF.Silu),
                         reads=[GBK[tb][1]], writes=["B%d_%d" % (gate_b, tb)])
                v_proj2(ws, 640, 384, True)
                qf = tt(2)
                for tb in range(2):
                    fm_proj(ws, 640, 0, tb, tb)
                    P.op("act", lambda e, tb=tb: e.activation(out=qf[:, tb * 512:(tb + 1) * 512], in_=PB[tb][:], func=AF.Silu),
                         reads=["PB%d" % tb], writes=["T2_%d" % tb])
                for d in range(2):
                    for tb in range(2):
                        bk_ap, bk_key = TSET[d]["banks"][tb]
                        fm_proj2(ws, 640, 128 * (1 + d), tb, bk_ap, bk_key)
                cols = []
                for d in range(2):
                    lidx = d * 6 + l * 3 + p
                    cols.append(dict(oml=misc[:, 76 + lidx:77 + lidx], lb=misc[:, 64 + lidx:65 + lidx], lbm1=misc[:, 88 + lidx:89 + lidx],
                                     edge=(31 if d == 0 else 0)))
                DT = [(d, tb) for d in range(2) for tb in range(2)]
                for (d, tb) in DT:
                    T_, (bk_ap, bk_key) = TSET[d], TSET[d]["banks"][tb]
                    extra = ALL_SBA if (d == 1 and tb == 0) else []
                    P.op("act", lambda e, T_=T_, tb=tb, bk_ap=bk_ap: e.activation(out=T_["sig"][:, HS[tb]], in_=bk_ap, func=AF.Sigmoid),
                         reads=[bk_key], writes=["%s_%d" % (T_["K"][0], tb)] + extra)
                for (d, tb) in DT:
                    T_, C_ = TSET[d], cols[d]
                    P.op("dve", lambda e, T_=T_, C_=C_, tb=tb: e.tensor_scalar(out=T_["kf"][:, HS[tb]], in0=T_["sig"][:, HS[tb]], scalar1=C_["lbm1"], scalar2=C_["oml"],
                                                                               op0=ALU.mult, op1=ALU.add),
                         reads=["%s_%d" % (T_["K"][0], tb), "misc_lbm1", "misc_oml"], writes=["%s_%d" % (T_["K"][1], tb)])
                for (d, tb) in DT:
                    T_, C_ = TSET[d], cols[d]
                    P.op("act", lambda e, T_=T_, C_=C_, tb=tb: e.activation(out=T_["sig"][:, HS[tb]], in_=T_["sig"][:, HS[tb]], func=AF.Ln, scale=C_["oml"], bias=C_["lb"]),
                         reads=["%s_%d" % (T_["K"][0], tb), "misc_oml", "misc_lb"], writes=["%s_%d" % (T_["K"][0], tb)])
                for (d, tb) in DT:
                    T_ = TSET[d]
                    P.op("dve", lambda e, T_=T_, tb=tb: e.tensor_tensor_scan(out=T_["bb"][:, HS[tb]], data0=ctab[:, C_SEG:C_SEG + 512], data1=T_["sig"][:, HS[tb]],
                                                                             initial=0.0, op0=ALU.mult, op1=ALU.add),
                         reads=["%s_%d" % (T_["K"][0], tb), "ctab"], writes=["%s_%d" % (T_["K"][2], tb)])
                T1 = TSET[1]
                for tb in range(2):
                    b3 = T1["bb"][:, HS[tb]].rearrange("p (c s) -> p c s", s=32)
                    P.op("dve", lambda e, b3=b3: e.tensor_tensor(out=b3, in0=b3, in1=b3[:, :, 31:32].to_broadcast([128, 16, 32]), op=ALU.subtract),
                         reads=["X2_%d" % tb], writes=["X2_%d" % tb])
                for tb in range(2):
                    P.op("dve", lambda e, T1=T1, tb=tb: e.tensor_tensor(out=T1["bb"][:, HS[tb]], in0=T1["sig"][:, HS[tb]], in1=T1["bb"][:, HS[tb]], op=ALU.subtract),
                         reads=["X2_%d" % tb, "X0_%d" % tb], writes=["X2_%d" % tb])
                for (d, tb) in DT:
                    T_ = TSET[d]
                    P.op("act", lambda e, T_=T_, tb=tb: e.activation(out=T_["bb"][:, HS[tb]], in_=T_["bb"][:, HS[tb]], func=AF.Relu, bias=c80p, scale=1.0),
                         reads=["%s_%d" % (T_["K"][2], tb), "c80"], writes=["%s_%d" % (T_["K"][2], tb)])
                for (d, tb) in DT:
                    T_ = TSET[d]
                    P.op("act", lambda e, T_=T_, tb=tb: e.activation(out=T_["sig"][:, HS[tb]], in_=T_["bb"][:, HS[tb]], func=AF.Exp, bias=c80n, scale=1.0),
                         reads=["%s_%d" % (T_["K"][2], tb), "c80"], writes=["%s_%d" % (T_["K"][0], tb)])
                    P.op("act", lambda e, T_=T_, tb=tb: e.activation(out=T_["einv"][:, HS[tb]], in_=T_["bb"][:, HS[tb]], func=AF.Exp, bias=c80p, scale=-1.0),
                         reads=["%s_%d" % (T_["K"][2], tb), "c80"], writes=["%s_%d" % (T_["K"][3], tb)])
                for (d, tb) in DT:
                    T_, edge = TSET[d], cols[d]["edge"]
                    kE, kK, kI = "%s_%d" % (T_["K"][0], tb), "%s_%d" % (T_["K"][1], tb), "%s_%d" % (T_["K"][3], tb)
                    e3 = T_["sig"][:, HS[tb]].rearrange("p (c s) -> p c s", s=32)
                    i3 = T_["einv"][:, HS[tb]].rearrange("p (c s) -> p c s", s=32)
                    moff = (C_MHF if d == 0 else C_MHB) + tb * 16
                    P.op("dve", lambda e, d=d, tb=tb, e3=e3, moff=moff, edge=edge, dpar=dpar: e.tensor_tensor(out=dmt[:, dpar, d, tb * 16:(tb + 1) * 16], in0=e3[:, :, edge],
                                                                                                      in1=ctab[:, moff:moff + 16], op=ALU.mult),
                         reads=[kE, "ctab"], writes=["dmt%d_%d" % (dpar, d)])
                    P.op("dve", lambda e, d=d, tb=tb, T_=T_, qq=qq: e.tensor_tensor(out=pool[:, qq[d], HS[tb]], in0=qf[:, HS[tb]], in1=T_["sig"][:, HS[tb]], op=ALU.mult),
                         reads=["T2_%d" % tb, kE], writes=["B%d_%d" % (qq[d], tb)])
                    for h in range(2):
                        hs = slice(h * 64, (h + 1) * 64)
                        P.op("dve", lambda e, d=d, h=h, hs=hs, tb=tb, T_=T_, kkh=kkh: e.tensor_tensor(out=pool[hs, kkh[d][h], HS[tb]], in0=T_["kf"][hs, HS[tb]],
                                                                                                  in1=T_["einv"][hs, HS[tb]], op=ALU.mult),
                             reads=[kK, kI], writes=["B%d_%d" % (kkh[d][h], tb)])
                    P.op("dve", lambda e, i3=i3, e3=e3, edge=edge: e.tensor_tensor(out=i3, in0=i3, in1=e3[:, :, edge:edge + 1].to_broadcast([128, 16, 32]), op=ALU.mult),
                         reads=[kI, kE], writes=[kI])
                for (d, tb) in DT:
                    T_ = TSET[d]
                    P.op("dve", lambda e, T_=T_, tb=tb: e.tensor_tensor(out=pool[:, T_["kb"], HS[tb]], in0=T_["kf"][:, HS[tb]], in1=T_["einv"][:, HS[tb]], op=ALU.mult),
                         reads=["%s_%d" % (T_["K"][1], tb), "%s_%d" % (T_["K"][3], tb)], writes=["B%d_%d" % (T_["kb"], tb)])
                for d in range(2):
                    def evac_h(t, src, skey, d=d, ks=ks):
                        for hh in range(2):
                            P.op("act", lambda e, t=t, src=src, hh=hh: e.activation(out=pool[:, ks[d][hh], t * 128 + hh * 64:t * 128 + (hh + 1) * 64],
                                                                                in_=src[:, hh * 64:(hh + 1) * 64], func=AF.Copy),
                                 reads=[skey], writes=["B%d_%d" % (ks[d][hh], t // 4)])
                    ks_transposes(TSET[d]["kb"], ks, evac_h)
                wdone(WI[("hg", l, p)])
                run_slot_mods(4 + p)
                gla_pair(l, p, 32, qq, kkh, ks, (C_HMF, C_HMB), s0h_d, osh_d, dpar, gate_b, "h", extra_w=XKEYS)
                finish_pair("hg", gate_b, 5 + p)

            def resid_update(dc, tb, bank, gcol, gkeys):
                sl = slice(tb * 512, (tb + 1) * 512)
                tmp = tt(2)
                P.op("act", lambda e: e.activation(out=tmp[:, sl], in_=PB[bank][:], func=AF.Identity, scale=gcol),
                     reads=["PB%d" % bank] + gkeys, writes=["T2_%d" % tb])
                P.op("dve", lambda e: e.scalar_tensor_tensor(out=xT[:, dc, sl], in0=xT[:, dc, sl], scalar=float(ALPHA), in1=tmp[:, sl], op0=ALU.mult, op1=ALU.add),
                     reads=["xT%d_%d" % (dc, tb), "T2_%d" % tb], writes=["xT%d_%d" % (dc, tb)])

            for ob in range(2):
                ws = wuse(WI[("out", l, ob)])
                for dcc in range(4):
                    dc = ob * 4 + dcc
                    for tb in range(2):
                        bank = (dcc * 2 + tb) % 4
                        for fc in range(8):
                            P.op("pe", lambda e, ws=ws, dcc=dcc, fc=fc, tb=tb, bank=bank: e.matmul(PB[bank][:], lhsT=wsl(ws, fc * 512 + dcc * 128, fc * 512 + (dcc + 1) * 128),
                                                                                               rhs=pool[:, MC[fc], tb * 512:(tb + 1) * 512], start=(fc == 0), stop=(fc == 7)),
                                 reads=[*WK(ws), "B%d_%d" % (MC[fc], tb)], writes=["PB%d" % bank])
                        resid_update(dc, tb, bank, modt[:, 16 + dc:17 + dc], MTK([4, 5]))
                wdone(WI[("out", l, ob)])
            gcols = colsB[:, l * 16:l * 16 + 8]
            bcols = colsB[:, 32 + l * 16:32 + l * 16 + 8]
            layer_norm_x(gcols, bcols, ["colsB"], lambda k: xT[:, k, :], lambda k: XK(k))
            if stop == "mix%d" % l:
                break

            P.op("dve", lambda e, modt=modt: e.tensor_scalar(out=modt[:, 56:64], in0=modt[:, 32:40], scalar1=1.0, scalar2=None, op0=ALU.add),
                 reads=MTK([8, 9]), writes=["ma2_%d" % mpar])
            layer_norm_x(modt[:, 56:64], modt[:, 24:32], MTK([6, 7]) + ["ma2_%d" % mpar], lambda k: pbuf_(HB[k]), lambda k: K2("B%d" % HB[k]))
            for fb in range(11):
                nj = 2
                n = 256
                wsg = wuse(WI[("gate", l, fb)])
                wsu = wuse(WI[("up", l, fb)])
                for jj in range(nj):
                    j = fb * 2 + jj
                    for tb in range(2):
                        sl = slice(tb * 512, (tb + 1) * 512)
                        bg, bu = (tb * 2) % 4, (tb * 2 + 1) % 4
                        for k in range(8):
                            P.op("pe", lambda e, k=k, wsg=wsg, jj=jj, n=n, sl=sl, bg=bg: e.matmul(PB[bg][:], lhsT=wsl(wsg, k * n + jj * 128, k * n + (jj + 1) * 128),
                                                                                            rhs=pool[:, HB[k], sl], start=(k == 0), stop=(k == 7)),
                                 reads=[*WK(wsg), "B%d_%d" % (HB[k], tb)], writes=["PB%d" % bg])
                        for k in range(8):
                            P.op("pe", lambda e, k=k, wsu=wsu, jj=jj, n=n, sl=sl, bu=bu: e.matmul(PB[bu][:], lhsT=wsl(wsu, k * n + jj * 128, k * n + (jj + 1) * 128),
                                                                                            rhs=pool[:, HB[k], sl], start=(k == 0), stop=(k == 7)),
                                 reads=[*WK(wsu), "B%d_%d" % (HB[k], tb)], writes=["PB%d" % bu])
                        sgt = sgtb[:, tb, :]
                        P.op("act", lambda e, bg=bg, sgt=sgt: e.activation(out=sgt, in_=PB[bg][:], func=AF.Silu), reads=["PB%d" % bg], writes=["sgt%d" % tb])
                        P.op("dve", lambda e, bu=bu, sgt=sgt, j=j, sl=sl: e.tensor_tensor(out=pool[:, AB[j], sl], in0=PB[bu][:], in1=sgt, op=ALU.mult),
                             reads=["PB%d" % bu, "sgt%d" % tb], writes=["B%d_%d" % (AB[j], tb)])
                wdone(WI[("gate", l, fb)])
                wdone(WI[("up", l, fb)])
            for dc in range(8):
                ws = wuse(WI[("down", l, dc)])
                for tb in range(2):
                    bank = (dc * 2 + tb) % 4
                    for j in range(NFF):
                        P.op("pe", lambda e, ws=ws, j=j, tb=tb, bank=bank: e.matmul(PB[bank][:], lhsT=wsl(ws, j * 128, (j + 1) * 128),
                                                                                  rhs=pool[:, AB[j], tb * 512:(tb + 1) * 512], start=(j == 0), stop=(j == NFF - 1)),
                             reads=[*WK(ws), "B%d_%d" % (AB[j], tb)], writes=["PB%d" % bank])
                    resid_update(dc, tb, bank, modt[:, 40 + dc:41 + dc], MTK([10, 11]))
                wdone(WI[("down", l, dc)])
            gcols = colsB[:, l * 16 + 8:l * 16 + 16]
            bcols = colsB[:, 32 + l * 16 + 8:32 + l * 16 + 16]
            layer_norm_x(gcols, bcols, ["colsB"], lambda k: xT[:, k, :], lambda k: XK(k))
            if stop == "ffn%d" % l:
                break

        for t in range(8):
            b = t % 2
            for kh in range(2):
                bk = 2 * b + kh
                for kk in range(4):
                    k = kh * 4 + kk
                    P.op("pe", lambda e, bk=bk, kk=kk, k=k, t=t: e.transpose(out=PB[bk][:, kk * 128:(kk + 1) * 128], in_=xT[:, k, t * 128:(t + 1) * 128], identity=ident),
                         reads=["xT%d_%d" % (k, t // 4), "ctab"], writes=["PB%d" % bk])
                P.op("act" if kh == 0 else "dve",
                     (lambda e, kh=kh, b=b, bk=bk: e.activation(out=t32[:, b, kh * 512:(kh + 1) * 512], in_=PB[bk][:], func=AF.Copy)) if kh == 0 else
                     (lambda e, kh=kh, b=b, bk=bk: e.tensor_copy(out=t32[:, b, kh * 512:(kh + 1) * 512], in_=PB[bk][:])),
                     reads=["PB%d" % bk], writes=["T%d_%d" % (b, kh)])
            P.op("sp", lambda e, t=t, b=b: e.dma_start(out=y_d[t * 128:(t + 1) * 128, :], in_=t32[:, b, :]),
                 reads=K2("T%d" % b), writes=["y%d" % t], dma="yout%d" % b)
        P.emit()
    return nc


def _const_tables(is_sample):
    ct = np.zeros((128, NCT), np.float32)
    ct[:, C_ID:C_ID + 128] = np.eye(128, dtype=np.float32)
    tpos = np.arange(128, dtype=np.float32)
    ct[:, C_POSP1:C_POSP1 + 128] = tpos[None, :] + 1.0
    ct[:, C_POSREV:C_POSREV + 128] = 128.0 - tpos[None, :]
    bm = np.zeros((128, 128), np.float32)
    bm[:64, :64] = 1.0
    bm[64:, 64:] = 1.0
    mb = 1.0 if is_sample else 0.0
    ct[:, C_BM:C_BM + 128] = bm
    ct[:, C_BMB:C_BMB + 128] = bm * mb
    ct[:, C_PCOL] = 127.0 - tpos
    ct[:, C_PCOL + 1] = tpos
    for off_f, off_b, G, cps in ((C_MRF, C_MRB, 8, 2), (C_MHF, C_MHB, 32, 8)):
        mf = np.ones(G, np.float32)
        mbk = np.ones(G, np.float32)
        for g in range(G):
            if g % cps == 0 and g > 0:
                mf[g] = mb
            if (g + 1) % cps == 0 and g != G - 1:
                mbk[g] = mb
        ct[:, off_f:off_f + G] = mf[None, :]
        ct[:, off_b:off_b + G] = mbk[None, :]
    n = np.arange(64)
    ang = 2.0 * np.pi * np.outer(n, n) / 64.0
    c64 = np.cos(ang) / 8.0
    s64 = np.sin(ang) / 8.0
    bdc = np.zeros((128, 128))
    bds = np.zeros((128, 128))
    bdc[:64, :64] = c64
    bdc[64:, 64:] = c64
    bds[:64, :64] = -s64
    bds[64:, 64:] = -s64
    ct[:, C_DFTC:C_DFTC + 128] = bdc
    ct[:, C_DFTS:C_DFTS + 128] = bds
    j = np.arange(128)[:, None]
    i = np.arange(128)[None, :]
    ct[:, C_RMF:C_RMF + 128] = (j <= i)
    ct[:, C_RMB:C_RMB + 128] = (j >= i)
    same = (j // 32 == i // 32)
    ct[:, C_HMF:C_HMF + 128] = (j <= i) & same
    ct[:, C_HMB:C_HMB + 128] = (j >= i) & same
    seg = np.ones(1024, np.float32)
    seg[::32] = 0.0
    ct[:, C_SEG:C_SEG + 1024] = seg[None, :]
    ct[:, C_PZF:C_PZF + 128] = 127.0 - tpos[None, :]
    ct[:, C_PZB:C_PZB + 128] = tpos[None, :]
    ct[:, C_MB] = mb
    ct[:, C_MB + 1] = 1.0
    return ct


def _rope_tables(is_sample):
    r = np.zeros((128, 2, T), np.float64)
    if not is_sample:
        r[:, 0, :] = 1.0
        return r.astype(np.float32)
    tok = np.arange(T)
    rows = (tok // 64).astype(np.float64)
    cols = (tok % 64).astype(np.float64)
    half = 32
    inv = 10000.0 ** (-np.arange(0, half, 2, dtype=np.float64) / half)
    for pp in range(128):
        dd = pp % 64
        pos = rows if dd < 32 else cols
        w = dd % 32
        fi = w % 16
        ang = pos * inv[fi]
        r[pp, 0, :] = np.cos(ang)
        r[pp, 1, :] = -np.sin(ang) if w < 16 else np.sin(ang)
    return r.astype(np.float32)


def _dft_tables(is_sample):
    L = 1024 if is_sample else 256
    n = np.arange(L)
    ang = 2.0 * np.pi * np.outer(n, n) / L
    c = np.cos(ang) / np.sqrt(L)
    s = np.sin(ang) / np.sqrt(L)
    out = np.zeros((2, T, T), np.float32)
    for b in range(T // L):
        out[0, b * L:(b + 1) * L, b * L:(b + 1) * L] = c
        out[1, b * L:(b + 1) * L, b * L:(b + 1) * L] = s
    return out


def _bd_state(s):
    out = np.zeros((DEPTH, 2, 3, 128, 128), np.float32)
    for p in range(3):
        out[:, :, p, :64, :64] = s[:, :, 2 * p]
        out[:, :, p, 64:, 64:] = s[:, :, 2 * p + 1]
    return out


_NC_CACHE = {}


def kernel(x_prompt, x_sample, c, state_ret, state_hgrn, c_ctx, w_mod, b_mod, w_in, w_out,
           ret_log_decay, hg_lower_bound, ln_g, ln_b, w_gate, w_up, w_down):
    f = lambda a: np.ascontiguousarray(np.asarray(a, dtype=np.float32))
    x_prompt, x_sample, c, state_ret, state_hgrn, c_ctx = map(f, (x_prompt, x_sample, c, state_ret, state_hgrn, c_ctx))
    w_mod, b_mod, w_in, w_out, w_gate, w_up, w_down = map(f, (w_mod, b_mod, w_in, w_out, w_gate, w_up, w_down))
    ret_log_decay, hg_lower_bound, ln_g, ln_b = map(f, (ret_log_decay, hg_lower_bound, ln_g, ln_b))

    if "nc" not in _NC_CACHE:
        import os
        _NC_CACHE["nc"] = build_program(os.environ.get("KSTOP"))
    nc = _NC_CACHE["nc"]

    smB = np.zeros((128, 128), np.float32)
    smB[0:32] = ln_g.reshape(32, 128)
    smB[32:64] = ln_b.reshape(32, 128)
    smB[64:76] = hg_lower_bound.reshape(12, 128)
    dec = np.repeat(ret_log_decay.reshape(DEPTH, 2, 6), 64, axis=-1).reshape(12, 128)
    smB[76:88] = dec
    tabs = {s: (_const_tables(s), _rope_tables(s), _dft_tables(s)) for s in (False, True)}
    zs = np.zeros((DEPTH, 2, 3, 128, 128), np.float32)
    in_maps = []
    for core in range(NCORES):
        is_sample = core >= 4
        if is_sample:
            b = core - 4
            xin = x_sample[b]
            cvec = c[b]
            s0r = _bd_state(state_ret[b])
            s0h = _bd_state(state_hgrn[b])
        else:
            xin = x_prompt[core * 4:(core + 1) * 4].reshape(T, D)
            cvec = c_ctx
            s0r, s0h = zs, zs
        smA = np.zeros((128, 128), np.float32)
        smA[0:96] = b_mod.reshape(96, 128)
        smA[96:104] = cvec.reshape(8, 128)
        ct, rp, dl = tabs[is_sample]
        in_maps.append(dict(x=np.ascontiguousarray(xin), smA=smA, smB=smB, ctab=ct, rope=rp, dftL=dl,
                            s0r=s0r, s0h=s0h, w_mod=w_mod, w_in=w_in, w_out=w_out, w_gate=w_gate, w_up=w_up, w_down=w_down))
    res = run_bass_kernel_spmd(nc, in_maps, core_ids=list(range(NCORES)))
    R = res.results
    y_prompt = np.stack([R[i]["y"] for i in range(4)]).reshape(16, 256, D)
    y_sample = np.stack([R[i]["y"] for i in range(4, 8)])

    def unpack(name):
        out = np.zeros((16, DEPTH, 2, 6, 64, 64), np.float32)
        for core in range(4):
            o = R[core][name]
            for p in range(3):
                out[core * 4:(core + 1) * 4, :, :, 2 * p] = o[:, :, :, p, :64, :64].transpose(2, 0, 1, 3, 4)
                out[core * 4:(core + 1) * 4, :, :, 2 * p + 1] = o[:, :, :, p, 64:, 64:].transpose(2, 0, 1, 3, 4)
        return out

    return (y_prompt.astype(np.float32), y_sample.astype(np.float32), unpack("osr"), unpack("osh"))
```

```python
import contextlib
import math
import numpy as np
import concourse.bass as bass
import concourse.mybir as mybir
from concourse.bass_utils import run_bass_kernel_spmd

F32 = mybir.dt.float32
BF16 = mybir.dt.bfloat16
AF = mybir.ActivationFunctionType
ALU = mybir.AluOpType

D = 1024
T = 1024
NCORES = 8
DEPTH = 2
DFF = 2816
NFF = 22
ALPHA = (2 * DEPTH) ** 0.25
LN_EPS = 1e-5
LN8 = math.log(0.125)

C_ID, C_POSP1, C_POSREV, C_BM, C_BMB, C_PCOL = 0, 128, 256, 384, 512, 640
C_MRF, C_MRB, C_MHF, C_MHB = 642, 650, 658, 690
C_DFTC, C_DFTS = 722, 850
C_RMF, C_RMB, C_HMF, C_HMB = 978, 1106, 1234, 1362
C_SEG = 1490
C_PZF, C_PZB = 1490 + 1024, 1490 + 1024 + 128
C_MB = 1490 + 1024 + 256
NCT = 1490 + 1024 + 256 + 2
WUNIT = 2560
NU = 6
NPOOL = 30


class Prog:
    ENG = ("pe", "act", "dve", "pool", "sp")

    def __init__(self, nc):
        self.nc = nc
        self.ops = []

    def op(self, eng, fn, reads=(), writes=(), dma=None):
        self.ops.append(dict(eng=eng, fn=fn, reads=tuple(reads), writes=tuple(writes), dma=dma))

    def emit(self, final_waits=()):
        nc = self.nc
        ops = self.ops
        last_w, readers = {}, {}
        eng_idx = {e: 0 for e in self.ENG}
        dma_gen = {}
        signaling = set()
        for o in ops:
            e = o["eng"]
            idx = eng_idx[e]
            eng_idx[e] += 1
            o["idx"] = idx
            deps = set()
            for r in o["reads"]:
                if r in last_w:
                    deps.add(last_w[r])
            for w in o["writes"]:
                if w in last_w:
                    deps.add(last_w[w])
                for rd in readers.get(w, ()):
                    deps.add(rd)
            if o["dma"] is not None:
                g = dma_gen.get(o["dma"], 0) + 1
                dma_gen[o["dma"]] = g
                ev = ("dma", o["dma"], g)
            else:
                ev = ("eng", e, idx)
            deps.discard(ev)
            o["deps"] = deps
            for d in deps:
                if d[0] == "eng":
                    signaling.add((d[1], d[2]))
            for r in o["reads"]:
                readers.setdefault(r, []).append(ev)
            for w in o["writes"]:
                last_w[w] = ev
                readers[w] = []
        final_events = [last_w[k] for k in final_waits]
        count = {}
        run = {e: 0 for e in self.ENG}
        per_eng = {e: [] for e in self.ENG}
        for o in ops:
            per_eng[o["eng"]].append(o)
            key = (o["eng"], o["idx"])
            if o["dma"] is None and key in signaling:
                run[o["eng"]] += 1
                count[key] = run[o["eng"]]
                o["signal"] = True
            else:
                o["signal"] = False
        dma_keys = sorted(dma_gen.keys())
        with contextlib.ExitStack() as st:
            sem_e = {e: st.enter_context(nc.semaphore("sem_" + e)) for e in self.ENG}
            sem_d = {k: st.enter_context(nc.semaphore("semd_" + k)) for k in dma_keys}
            block = st.enter_context(nc.Block())

            def run_engine(e, eo):
                wm = {}

                def do_waits(deps):
                    need = {}
                    for d in deps:
                        if d[0] == "eng":
                            if d[1] == "pe" and e == "pe":
                                continue
                            k = ("eng", d[1])
                            v = count[(d[1], d[2])]
                        else:
                            k = ("dma", d[1])
                            v = 16 * d[2]
                        if v > need.get(k, 0):
                            need[k] = v
                    for k, v in need.items():
                        if wm.get(k, 0) >= v:
                            continue
                        wm[k] = v
                        s = sem_e[k[1]] if k[0] == "eng" else sem_d[k[1]]
                        eo.wait_ge(s, v)

                for o in per_eng[e]:
                    do_waits(o["deps"])
                    ins = o["fn"](eo)
                    if o["dma"] is not None:
                        ins.then_inc(sem_d[o["dma"]], 16)
                    elif o["signal"]:
                        ins.then_inc(sem_e[e], 1)
                if e == "sp":
                    do_waits(list(final_events) + [("dma", k, g) for k, g in dma_gen.items()])

            @block.tensor
            def _(eng):
                run_engine("pe", eng)

            @block.scalar
            def _(eng):
                run_engine("act", eng)

            @block.vector
            def _(eng):
                run_engine("dve", eng)

            @block.gpsimd
            def _(eng):
                run_engine("pool", eng)

            @block.sync
            def _(eng):
                run_engine("sp", eng)


def K2(name):
    return [name + "_0", name + "_1"]


def build_program(stop=None):
    nc = bass.Bass("TRN2", target_bir_lowering=False)

    def din(name, shape):
        return nc.dram_tensor(name, list(shape), F32, kind="ExternalInput").ap()

    def dout(name, shape):
        return nc.dram_tensor(name, list(shape), F32, kind="ExternalOutput").ap()

    x_d = din("x", [T, D])
    smA_d = din("smA", [128, 128])
    smB_d = din("smB", [128, 128])
    ctab_d = din("ctab", [128, NCT])
    rope_d = din("rope", [128, 2, T])
    dftL_d = din("dftL", [2, T, T])
    s0r_d = din("s0r", [DEPTH, 2, 3, 128, 128])
    s0h_d = din("s0h", [DEPTH, 2, 3, 128, 128])
    wmod_d = din("w_mod", [DEPTH, D, 6 * D])
    win_d = din("w_in", [DEPTH, D, 3712])
    wout_d = din("w_out", [DEPTH, D, D])
    wg_d = din("w_gate", [DEPTH, D, DFF])
    wu_d = din("w_up", [DEPTH, D, DFF])
    wd_d = din("w_down", [DEPTH, DFF, D])
    y_d = dout("y", [T, D])
    osr_d = dout("osr", [DEPTH, 2, 4, 3, 128, 128])
    osh_d = dout("osh", [DEPTH, 2, 4, 3, 128, 128])

    P = Prog(nc)
    st = contextlib.ExitStack()
    with st:
        def sb(name, shape, dt):
            return st.enter_context(nc.sbuf_tensor(name, list(shape), dt))

        def ps(name, shape, dt):
            return st.enter_context(nc.psum_tensor(name, list(shape), dt))

        xT = sb("xT", [128, 8, T], F32)
        ctab = sb("ctab_sb", [128, NCT], F32)
        colsA = sb("colsA", [128, 128], F32)
        colsB = sb("colsB", [128, 128], F32)
        wring = sb("wring", [128, NU, WUNIT], BF16)
        pool = sb("pool", [128, NPOOL, T], BF16)
        t32 = sb("t32", [128, 5, T], F32)
        smt = t32[:, 4, 0:256]
        vbuf = sb("vbuf", [128, 8, 128], BF16)
        vh = sb("vh", [128, 2, 8, 128], BF16)
        vblk = sb("vblk", [128, 8, 4, 128], BF16)
        sst = sb("sst", [128, 2, 8, 128], F32)
        sbfall = sb("sbfall", [128, 2, 32, 128], BF16)
        pbuf = sb("pbuf", [128, 2, 4, 128], BF16)
        modt_t = sb("modt", [128, 2, 64], F32)
        scb = sb("scb", [128, 8], BF16)
        misc = sb("misc", [128, 160], F32)
        dect = sb("dect", [128, 2, 6, 128], F32)
        dmt = sb("dmt", [128, 2, 2, 32], F32)
        onesD = sb("onesD", [128, 128], BF16)
        bd64 = sb("bd64", [128, 128], BF16)
        identb = sb("identb", [128, 128], BF16)
        dftcb = sb("dftcb", [128, 2, 128], BF16)
        sgtb = sb("sgtb", [128, 2, 512], BF16)
        PB = [ps("PB%d" % i, [128, 512], F32) for i in range(4)]
        PS4 = ps("PS4", [128, 512], F32)
        PT5 = ps("PT5", [128, 1024], BF16)
        PU = [ps("PU%d" % i, [128, 512], F32) for i in range(2)]

        ident = ctab[:, C_ID:C_ID + 128]
        PT5f = PT5[:, :].bitcast(F32)

        def pbuf_(i):
            return pool[:, i, :]

        def tt(i):
            return t32[:, i, :]

        wloads = []
        wstate = dict(issued=0, pos=0)
        wflat = wring[:, :, :].rearrange("p a b -> p (a b)")
        unit_occ = [None] * NU
        slot_of = {}
        wdone_flags = {}
        cur_units = {}

        def wsl(ws, a_, b_):
            return wflat[:, ws * WUNIT + a_:ws * WUNIT + b_]

        def WK(ws):
            return ["W%d" % (ws + i) for i in range(cur_units[ws])]

        def wreq(make, nu=2):
            wloads.append((nu, make))
            return len(wloads) - 1

        def _pump():
            while wstate["issued"] < len(wloads):
                j = wstate["issued"]
                nu, make = wloads[j]
                pos = wstate["pos"]
                if pos + nu > NU:
                    pos = 0
                ok = all(unit_occ[u] is None or wdone_flags.get(unit_occ[u], False) for u in range(pos, pos + nu))
                if not ok:
                    return
                for u in range(pos, pos + nu):
                    unit_occ[u] = j
                slot_of[j] = pos
                for (o_ap, i_ap) in make(pos):
                    P.op("pool", (lambda e, o_ap=o_ap, i_ap=i_ap: e.dma_start(out=o_ap, in_=i_ap)),
                         writes=["W%d" % u for u in range(pos, pos + nu)], dma="W%d" % pos)
                wstate["pos"] = (pos + nu) % NU
                wstate["issued"] += 1

        def wuse(i):
            _pump()
            assert wstate["issued"] > i, (i, wstate["issued"])
            cur_units[slot_of[i]] = wloads[i][0]
            return slot_of[i]

        def wdone(i):
            wdone_flags[i] = True
            _pump()

        def wview(s, k, n):
            return wsl(s, 0, k * n).rearrange("p (k n) -> p k n", n=n)

        def mk_std(dram2d, c0, n):
            def make(s):
                return [(wview(s, 8, n), dram2d[:, c0:c0 + n].rearrange("(k p) n -> p k n", p=128))]
            return make

        def mk_grp(dram2d, c0, ng, stride):
            def make(s):
                res = []
                v = wview(s, 8, ng * 128)
                for g in range(ng):
                    res.append((v[:, :, g * 128:(g + 1) * 128],
                                dram2d[:, c0 + g * stride:c0 + g * stride + 128].rearrange("(k p) n -> p k n", p=128)))
                return res
            return make

        def mk_down(dram2d, c0):
            def make(s):
                return [(wview(s, NFF, 128), dram2d[:, c0:c0 + 128].rearrange("(k p) n -> p k n", p=128))]
            return make

        WI = {}

        def reg_mods(lm, js):
            for j in js:
                WI[("mod", lm, j)] = wreq(mk_std(wmod_d[lm], j * 512, 512))

        def slot_mods(l, i):
            if i <= 3:
                return [(l, 4 + 2 * i), (l, 5 + 2 * i)]
            if i <= 5 and l + 1 < DEPTH:
                return [(l + 1, 2 * (i - 4)), (l + 1, 2 * (i - 4) + 1)]
            return []

        reg_mods(0, range(4))
        for l in range(DEPTH):
            WI[("fnet", l)] = wreq(mk_std(win_d[l], 0, 256), 1)
            for (lm, j) in slot_mods(l, 0):
                reg_mods(lm, [j])
            for tbl in range(2):
                for ob in range(2):
                    WI[("dft", l, tbl, ob)] = wreq(mk_std(dftL_d[tbl], ob * 512, 512))
            for p in range(3):
                WI[("ret", l, p)] = wreq(mk_grp(win_d[l], 256 + 128 * p, 4, 384))
                for (lm, j) in slot_mods(l, 1 + p):
                    reg_mods(lm, [j])
            for p in range(3):
                WI[("hg", l, p)] = wreq(mk_grp(win_d[l], 1792 + 128 * p, 5, 384))
                for (lm, j) in slot_mods(l, 4 + p):
                    reg_mods(lm, [j])
            for ob in range(2):
                WI[("out", l, ob)] = wreq(mk_std(wout_d[l], ob * 512, 512))
            for fb in range(11):
                WI[("gate", l, fb)] = wreq(mk_std(wg_d[l], fb * 256, 256), 1)
                WI[("up", l, fb)] = wreq(mk_std(wu_d[l], fb * 256, 256), 1)
            for dc in range(8):
                WI[("down", l, dc)] = wreq(mk_down(wd_d[l], dc * 128))

        P.op("sp", lambda e: e.dma_start(out=ctab[:], in_=ctab_d), writes=["ctab"], dma="c0")
        P.op("sp", lambda e: e.dma_start(out=smt[:, 0:128], in_=smA_d), writes=["T4_0"], dma="c2")
        P.op("sp", lambda e: e.dma_start(out=smt[:, 128:256], in_=smB_d), writes=["T4_0"], dma="c3")
        P.op("dve", lambda e: e.memset(onesD[:], 1.0 / 1024.0), writes=["onesD"])
        P.op("dve", lambda e: e.tensor_scalar(out=bd64[:], in0=ctab[:, C_BM:C_BM + 128], scalar1=1.0 / 64.0, scalar2=None, op0=ALU.mult),
             reads=["ctab"], writes=["bd64"])
        P.op("dve", lambda e: e.tensor_copy(out=identb[:], in_=ident), reads=["ctab"], writes=["identb"])
        P.op("dve", lambda e: e.tensor_copy(out=dftcb[:].rearrange("p a b -> p (a b)"), in_=ctab[:, C_DFTC:C_DFTC + 256]),
             reads=["ctab"], writes=["dftcb"])
        P.op("dve", lambda e: e.memset(sbfall[:].rearrange("p a b c -> p (a b c)"), 0.0), writes=["SBA%d_%d" % (d_, t_) for d_ in range(2) for t_ in range(8)])
        P.op("pool", lambda e: e.memset(vh[:].rearrange("p a b c -> p (a b c)"), 0.0), writes=["vh"])
        P.op("pool", lambda e: e.memset(vblk[:].rearrange("p a b c -> p (a b c)"), 0.0), writes=["vblk"])
        P.op("pe", lambda e: e.transpose(out=PB[0][:, 0:128], in_=smt[:, 0:128], identity=ident), reads=["T4_0", "ctab"], writes=["PB0"])
        P.op("pe", lambda e: e.transpose(out=PB[0][:, 128:256], in_=smt[:, 128:256], identity=ident), reads=["T4_0", "ctab"], writes=["PB0"])
        P.op("act", lambda e: e.activation(out=colsA[:], in_=PB[0][:, 0:128], func=AF.Copy), reads=["PB0"], writes=["colsA"])
        P.op("act", lambda e: e.activation(out=colsB[:], in_=PB[0][:, 128:256], func=AF.Copy), reads=["PB0"], writes=["colsB"])
        P.op("act", lambda e: e.activation(out=scb[:], in_=colsA[:, 96:104], func=AF.Silu), reads=["colsA"], writes=["scb"])
        P.op("act", lambda e: e.activation(out=misc[:, 28:40], in_=colsB[:, 76:88], func=AF.Exp), reads=["colsB"], writes=["misc_lg"])
        P.op("dve", lambda e: e.tensor_scalar(out=misc[:, 16:28], in0=misc[:, 28:40], scalar1=-1.0, scalar2=None, op0=ALU.mult),
             reads=["misc_lg"], writes=["misc_lg2"])
        P.op("act", lambda e: e.activation(out=misc[:, 40:52], in_=misc[:, 16:28], func=AF.Exp, scale=128.0), reads=["misc_lg2"], writes=["misc_D"])
        P.op("act", lambda e: e.activation(out=misc[:, 52:64], in_=colsB[:, 64:76], func=AF.Exp), reads=["colsB"], writes=["misc_e"])
        P.op("dve", lambda e: e.memset(misc[:, 64:76], 0.0), writes=["misc_lb"])
        for d_ in range(2):
            e0 = misc[:, 52 + d_ * 6:52 + d_ * 6 + 3]
            e1 = misc[:, 52 + d_ * 6 + 3:52 + d_ * 6 + 6]
            dst = misc[:, 64 + d_ * 6 + 3:64 + d_ * 6 + 6]
            tmpc = misc[:, 100 + d_ * 3:103 + d_ * 3]
            P.op("dve", lambda e, e0=e0, e1=e1, tmpc=tmpc: e.tensor_tensor(out=tmpc, in0=e0, in1=e1, op=ALU.add), reads=["misc_e"], writes=["misc_t%d" % d_])
            P.op("dve", lambda e, tmpc=tmpc: e.reciprocal(out=tmpc, in_=tmpc), reads=["misc_t%d" % d_], writes=["misc_t%d" % d_])
            P.op("dve", lambda e, e1=e1, tmpc=tmpc, dst=dst: e.tensor_tensor(out=dst, in0=e1, in1=tmpc, op=ALU.mult),
                 reads=["misc_t%d" % d_, "misc_e", "misc_lb"], writes=["misc_lb"])
        P.op("dve", lambda e: e.tensor_scalar(out=misc[:, 76:88], in0=misc[:, 64:76], scalar1=-1.0, scalar2=1.0, op0=ALU.mult, op1=ALU.add),
             reads=["misc_lb"], writes=["misc_oml"])
        P.op("dve", lambda e: e.tensor_scalar(out=misc[:, 88:100], in0=misc[:, 64:76], scalar1=-1.0, scalar2=None, op0=ALU.add),
             reads=["misc_lb"], writes=["misc_lbm1"])

        for t in range(8):
            b = t % 2
            P.op("sp", lambda e, t=t, b=b: e.dma_start(out=t32[:, b, :], in_=x_d[t * 128:(t + 1) * 128, :]),
                 writes=K2("T%d" % b), dma="xin%d" % b)
            for kh in range(2):
                bk = 2 * b + kh
                for kk in range(4):
                    k = kh * 4 + kk
                    src = t32[:, b, k * 128:(k + 1) * 128]
                    P.op("pe", lambda e, bk=bk, kk=kk, src=src: e.transpose(out=PB[bk][:, kk * 128:(kk + 1) * 128], in_=src, identity=ident),
                         reads=K2("T%d" % b) + ["ctab"], writes=["PB%d" % bk])
                P.op("act" if kh == 0 else "dve",
                     (lambda e, kh=kh, t=t, bk=bk: e.activation(out=xT[:, kh * 4:kh * 4 + 4, t * 128:(t + 1) * 128],
                                                                 in_=PB[bk][:].rearrange("p (a b) -> p a b", b=128), func=AF.Copy)) if kh == 0 else
                     (lambda e, kh=kh, t=t, bk=bk: e.tensor_copy(out=xT[:, kh * 4:kh * 4 + 4, t * 128:(t + 1) * 128],
                                                                  in_=PB[bk][:].rearrange("p (a b) -> p a b", b=128))),
                     reads=["PB%d" % bk], writes=["xT%d_%d" % (k_, t // 4) for k_ in range(kh * 4, kh * 4 + 4)])

        XK = lambda k: ["xT%d_0" % k, "xT%d_1" % k]

        def norm_stats(chunks, chunk_keys, ones_ap, ones_key, center, mean_t, rstd_t, tmpb_all):
            n = len(chunks)
            for ci, (ch, ck) in enumerate(zip(chunks, chunk_keys)):
                tmpb = tmpb_all[ci % len(tmpb_all)]
                xb, xq = pbuf_(tmpb[0]), pbuf_(tmpb[1])
                if n == 1:
                    for tb in range(2):
                        hsl = slice(tb * 512, (tb + 1) * 512)
                        if center:
                            P.op("act", lambda e, ch=ch, xb=xb, hsl=hsl: e.activation(out=xb[:, hsl], in_=ch[:, hsl], func=AF.Copy),
                                 reads=[ck[tb]], writes=["B%d_%d" % (tmpb[0], tb)])
                        P.op("dve", lambda e, ch=ch, xq=xq, hsl=hsl: e.tensor_tensor(out=xq[:, hsl], in0=ch[:, hsl], in1=ch[:, hsl], op=ALU.mult),
                             reads=[ck[tb]], writes=["B%d_%d" % (tmpb[1], tb)])
                else:
                    if center:
                        P.op("act", lambda e, ch=ch, xb=xb: e.activation(out=xb, in_=ch, func=AF.Copy), reads=ck, writes=K2("B%d" % tmpb[0]))
                    P.op("dve", lambda e, ch=ch, xq=xq: e.tensor_tensor(out=xq, in0=ch, in1=ch, op=ALU.mult), reads=ck, writes=K2("B%d" % tmpb[1]))
                for tb in range(2):
                    if center:
                        P.op("pe", lambda e, tb=tb, xb=xb, ci=ci: e.matmul(PB[tb][:], lhsT=ones_ap, rhs=xb[:, tb * 512:(tb + 1) * 512],
                                                                         start=(ci == 0), stop=(ci == n - 1)),
                             reads=["B%d_%d" % (tmpb[0], tb), ones_key], writes=["PB%d" % tb])
                    P.op("pe", lambda e, tb=tb, xq=xq, ci=ci: e.matmul(PB[2 + tb][:], lhsT=ones_ap, rhs=xq[:, tb * 512:(tb + 1) * 512],
                                                                     start=(ci == 0), stop=(ci == n - 1)),
                         reads=["B%d_%d" % (tmpb[1], tb), ones_key], writes=["PB%d" % (2 + tb)])
            mean, rstd = tt(mean_t), tt(rstd_t)
            SL = [slice(0, 512), slice(512, 1024)]
            MKk = ["T%d_%d" % (mean_t, tb) for tb in range(2)]
            RKk = ["T%d_%d" % (rstd_t, tb) for tb in range(2)]
            if center:
                for tb in range(2):
                    P.op("act", lambda e, tb=tb: e.activation(out=mean[:, SL[tb]], in_=PB[tb][:], func=AF.Copy),
                         reads=["PB%d" % tb], writes=[MKk[tb]])
                for tb in range(2):
                    P.op("dve", lambda e, tb=tb: e.tensor_tensor(out=rstd[:, SL[tb]], in0=mean[:, SL[tb]], in1=mean[:, SL[tb]], op=ALU.mult),
                         reads=[MKk[tb]], writes=[RKk[tb]])
                for tb in range(2):
                    P.op("dve", lambda e, tb=tb: e.tensor_tensor(out=rstd[:, SL[tb]], in0=PB[2 + tb][:], in1=rstd[:, SL[tb]], op=ALU.subtract),
                         reads=["PB%d" % (2 + tb), RKk[tb]], writes=[RKk[tb]])
                for tb in range(2):
                    P.op("act", lambda e, tb=tb: e.activation(out=rstd[:, SL[tb]], in_=rstd[:, SL[tb]], func=AF.Ln, bias=epsc, scale=1.0),
                         reads=[RKk[tb], "epsc"], writes=[RKk[tb]])
            else:
                for tb in range(2):
                    P.op("act", lambda e, tb=tb: e.activation(out=rstd[:, SL[tb]], in_=PB[2 + tb][:], func=AF.Ln, bias=epsc, scale=1.0),
                         reads=["PB%d" % (2 + tb), "epsc"], writes=[RKk[tb]])
            for tb in range(2):
                P.op("act", lambda e, tb=tb: e.activation(out=rstd[:, SL[tb]], in_=rstd[:, SL[tb]], func=AF.Exp, scale=-0.5),
                     reads=[RKk[tb]], writes=[RKk[tb]])

        epsc = misc[:, 110:111]
        P.op("dve", lambda e: e.memset(epsc, LN_EPS), writes=["epsc"])
        c80p = misc[:, 112:113]
        c80n = misc[:, 113:114]
        P.op("dve", lambda e: e.memset(c80p, 80.0), writes=["c80"])
        P.op("dve", lambda e: e.memset(c80n, -80.0), writes=["c80"])
        ln8c = misc[:, 111:112]
        P.op("dve", lambda e: e.memset(ln8c, LN8), writes=["ln8c"])

        def ln_apply(scale_cols, bias_cols, col_keys, dst_fn, dst_keys_fn, tmp_t):
            mean, rstd = tt(0), tt(1)
            for kp in range(4):
                ks_ = (2 * kp, 2 * kp + 1)
                tqs = {k: tmp_t + (k % 2) for k in ks_}
                for k in ks_:
                    tq = tqs[k]
                    tmp = tt(tq)
                    P.op("dve", lambda e, k=k, tmp=tmp: e.tensor_tensor(out=tmp, in0=xT[:, k, :], in1=mean, op=ALU.subtract),
                         reads=XK(k) + K2("T0"), writes=K2("T%d" % tq))
                for k in ks_:
                    tq = tqs[k]
                    tmp = tt(tq)
                    P.op("dve", lambda e, tmp=tmp: e.tensor_tensor(out=tmp, in0=tmp, in1=rstd, op=ALU.mult),
                         reads=K2("T%d" % tq) + K2("T1"), writes=K2("T%d" % tq))
                for k in ks_:
                    tq = tqs[k]
                    tmp = tt(tq)
                    P.op("act", lambda e, k=k, tmp=tmp: e.activation(out=dst_fn(k), in_=tmp, func=AF.Identity, scale=scale_cols[:, k:k + 1], bias=bias_cols[:, k:k + 1]),
                         reads=K2("T%d" % tq) + col_keys, writes=dst_keys_fn(k))

        def layer_norm_x(scale_cols, bias_cols, col_keys, dst_fn, dst_keys_fn):
            norm_stats([xT[:, k, :] for k in range(8)], [XK(k) for k in range(8)], onesD[:], "onesD", True, 0, 1, [(26, 27), (28, 29)])
            ln_apply(scale_cols, bias_cols, col_keys, dst_fn, dst_keys_fn, 2)

        HB = list(range(0, 8))
        MC = list(range(8, 16))
        OPB = list(range(16, 25))
        AB = list(range(8, 30))

        def fm_proj(ws, n_, c0, tb, bank):
            for k in range(8):
                P.op("pe", lambda e, k=k: e.matmul(PB[bank][:], lhsT=wsl(ws, k * n_ + c0, k * n_ + c0 + 128),
                                                   rhs=pool[:, HB[k], tb * 512:(tb + 1) * 512], start=(k == 0), stop=(k == 7)),
                     reads=[*WK(ws), "B%d_%d" % (HB[k], tb)], writes=["PB%d" % bank])

        def fm_proj2(ws, n_, c0, tb, bank_ap, bank_key):
            for k in range(8):
                P.op("pe", lambda e, k=k: e.matmul(bank_ap, lhsT=wsl(ws, k * n_ + c0, k * n_ + c0 + 128),
                                                   rhs=pool[:, HB[k], tb * 512:(tb + 1) * 512], start=(k == 0), stop=(k == 7)),
                     reads=[*WK(ws), "B%d_%d" % (HB[k], tb)], writes=[bank_key])

        def tm_proj_tile(ws, n, c0, ncols, t, out_ap, out_key):
            for k in range(8):
                P.op("pe", lambda e, k=k: e.matmul(out_ap, lhsT=pool[:, HB[k], t * 128:(t + 1) * 128],
                                                   rhs=wsl(ws, k * n + c0, k * n + c0 + ncols), start=(k == 0), stop=(k == 7)),
                     reads=[*WK(ws), "B%d_%d" % (HB[k], t // 4)], writes=[out_key])

        def gla_pair(l, p, csz, qq, kkh, ks, masks, s0_d, os_d, mcol_off, gate_b, kind, extra_w=()):
            n = 128 // csz
            G = 8 * n
            cps = 256 // csz
            oacc = tt(0)
            cur = {}
            um = tt(1)
            for d in range(2):
                g0 = 0 if d == 0 else G - 1
                P.op("sp", lambda e, d=d: e.dma_start(out=sst[:, d, 1, :], in_=s0_d[l, d, p]), writes=["S%d_1" % d], dma="s0_%d" % d)
                P.op("act", lambda e, d=d, g0=g0: e.activation(out=sbfall[:, d, g0, :], in_=sst[:, d, 1, :], func=AF.Copy),
                     reads=["S%d_1" % d], writes=["SBA%d_%d" % (d, g0 // n)] + list(extra_w))
                cur[d] = 1
            UB = [[(PU[0], "PU0"), (PS4, "PS4")], [(PU[1], "PU1"), (PB[3], "PB3")]]
            for s in range(8):
                tiles = [s, 7 - s]
                for d in range(2):
                    t = tiles[d]
                    hk = "_%d" % (t // 4)
                    ub_t, ub_k = UB[d][s % 2]
                    for c_ in range(n):
                        for hh in range(2):
                            if csz == 32:
                                rhs_ap, rkey = vblk[:, t, c_, hh * 64:(hh + 1) * 64], "vblk"
                            else:
                                rhs_ap, rkey = vbuf[:, t, hh * 64:(hh + 1) * 64], "vbuf"
                            P.op("pe", lambda e, d=d, t=t, hh=hh, c_=c_, ub_t=ub_t, rhs_ap=rhs_ap: e.matmul(
                                    ub_t[:, c_ * 128 + hh * 64:c_ * 128 + (hh + 1) * 64],
                                    lhsT=pool[:, ks[d][hh], t * 128:(t + 1) * 128],
                                    rhs=rhs_ap, start=True, stop=True),
                                 reads=["B%d%s" % (ks[d][hh], hk), rkey], writes=[ub_k])
                for ci in range(n):
                    for d in range(2):
                        t = tiles[d]
                        c = ci if d == 0 else n - 1 - ci
                        g = t * n + c
                        si = cur[d]
                        if d == 0:
                            fin = ((g + 1) % cps == 0)
                            seq = (g + 1) // cps - 1
                            nxt_boundary = fin and (g != G - 1)
                            last = (g == G - 1)
                            gn = g + 1
                        else:
                            fin = (g % cps == 0)
                            seq = g // cps
                            nxt_boundary = fin and (g != 0)
                            last = (g == 0)
                            gn = g - 1
                        ni = (4 + seq) if fin else ((si + 1) % 4 if si < 4 else 0)
                        dcol = dmt[:, mcol_off, d, g:g + 1]
                        ub_t, ub_k = UB[d][s % 2]
                        ucol, ukey = ub_t[:, c * 128:(c + 1) * 128], ub_k
                        P.op("dve", lambda e, d=d, si=si, ni=ni, dcol=dcol, ucol=ucol: e.scalar_tensor_tensor(
                                out=sst[:, d, ni, :], in0=sst[:, d, si, :], scalar=dcol, in1=ucol, op0=ALU.mult, op1=ALU.add),
                             reads=["S%d_%d" % (d, si), "dmt%d_%d" % (mcol_off, d), ukey], writes=["S%d_%d" % (d, ni)])
                        if fin:
                            P.op("sp", lambda e, d=d, ni=ni, seq=seq: e.dma_start(out=os_d[l, d, seq, p], in_=sst[:, d, ni, :]),
                                 reads=["S%d_%d" % (d, ni)], writes=["os_%s_%d_%d_%d_%d" % (kind, l, d, seq, p)],
                                 dma="os%d_%d" % (d, ni))
                        if not last:
                            mk = C_MB if nxt_boundary else C_MB + 1
                            P.op("act", lambda e, d=d, ni=ni, gn=gn, mk=mk: e.activation(out=sbfall[:, d, gn, :], in_=sst[:, d, ni, :],
                                                                                      func=AF.Identity, scale=ctab[:, mk:mk + 1]),
                                 reads=["S%d_%d" % (d, ni), "ctab"], writes=["SBA%d_%d" % (d, gn // n)])
                        cur[d] = ni
            SCB = [[PS4, PB[3]], [PB[2], PU[0]]]
            SCK = [["PS4", "PB3"], ["PB2", "PU0"]]
            OAB, OAK = [PB[0], PB[1]], ["PB0", "PB1"]

            def scores(t):
                tsl = slice(t * 128, (t + 1) * 128)
                hk = "_%d" % (t // 4)
                par = t % 2
                for d in range(2):
                    for h in range(2):
                        P.op("pe", lambda e, d=d, h=h, par=par, tsl=tsl: e.matmul(SCB[d][par][:, h * 128:(h + 1) * 128], lhsT=pool[:, kkh[d][h], tsl],
                                                                                 rhs=pool[:, qq[d], tsl], start=True, stop=True),
                             reads=["B%d%s" % (kkh[d][h], hk), "B%d%s" % (qq[d], hk)], writes=[SCK[d][par]])
                for d in range(2):
                    P.op("dve", lambda e, d=d, par=par: e.tensor_tensor(
                            out=pbuf[:, par, 2 * d:2 * d + 2, :], in0=SCB[d][par][:, 0:256].rearrange("p (h c) -> p h c", h=2),
                            in1=ctab[:, masks[d]:masks[d] + 128].unsqueeze(1).to_broadcast([128, 2, 128]), op=ALU.mult),
                         reads=[SCK[d][par], "ctab"], writes=["pbuf%d_%d" % (par, d)])

            def outputs(t):
                tsl = slice(t * 128, (t + 1) * 128)
                hk = "_%d" % (t // 4)
                par = t % 2
                osl = OAB[par][:, 0:128]
                first = True
                for d in range(2):
                    for h in range(2):
                        P.op("pe", lambda e, d=d, h=h, t=t, par=par, osl=osl, first=first: e.matmul(osl, lhsT=vh[:, h, t, :], rhs=pbuf[:, par, 2 * d + h, :],
                                                                                                start=first, stop=False),
                             reads=["vh", "pbuf%d_%d" % (par, d)], writes=[OAK[par]])
                        first = False
                for d in range(2):
                    for c in range(n):
                        g = t * n + c
                        csl = slice(t * 128 + c * csz, t * 128 + (c + 1) * csz)
                        lastmm = (d == 1 and c == n - 1)
                        P.op("pe", lambda e, d=d, g=g, c=c, csl=csl, par=par, lastmm=lastmm: e.matmul(
                                OAB[par][:, c * csz:(c + 1) * csz], lhsT=sbfall[:, d, g, :], rhs=pool[:, qq[d], csl],
                                start=False, stop=lastmm),
                             reads=["SBA%d_%d" % (d, t), "B%d%s" % (qq[d], hk)], writes=[OAK[par]])
                P.op("act", lambda e, osl=osl, tsl=tsl: e.activation(out=oacc[:, tsl], in_=osl, func=AF.Copy),
                     reads=[OAK[par]], writes=["T0%s" % hk])

            for t in range(9):
                if t < 8:
                    scores(t)
                if t >= 1:
                    outputs(t - 1)

        for l in range(DEPTH):
            mpar = l % 2
            modt = modt_t[:, mpar, :]
            MTK = lambda js: ["mt%d_%d" % (mpar, j) for j in js]

            def mod_block(lm, j12, bank, bank_key):
                ws_ = wuse(WI[("mod", lm, j12)])
                for jj in range(4):
                    for k in range(8):
                        P.op("pe", lambda e, ws_=ws_, jj=jj, k=k: e.matmul(bank[:, jj:jj + 1], lhsT=wsl(ws_, k * 512 + jj * 128, k * 512 + (jj + 1) * 128),
                                                                          rhs=scb[:, k:k + 1], start=(k == 0), stop=(k == 7)),
                             reads=[*WK(ws_), "scb"], writes=[bank_key])
                wdone(WI[("mod", lm, j12)])
                P.op("dve", lambda e: e.tensor_tensor(out=modt_t[:, lm % 2, 4 * j12:4 * j12 + 4], in0=bank[:, 0:4],
                                                      in1=colsA[:, lm * 48 + 4 * j12:lm * 48 + 4 * j12 + 4], op=ALU.add),
                     reads=[bank_key, "colsA"], writes=["mt%d_%d" % (lm % 2, j12)])

            def run_slot_mods(i):
                for (lm, j) in slot_mods(l, i):
                    mod_block(lm, j, PU[1], "PU1")

            if l == 0:
                for j12 in range(4):
                    mod_block(0, j12, PB[j12 % 2], "PB%d" % (j12 % 2))
            P.op("dve", lambda e, modt=modt: e.tensor_scalar(out=modt[:, 48:56], in0=modt[:, 8:16], scalar1=1.0, scalar2=None, op0=ALU.add),
                 reads=MTK([2, 3]), writes=["ma1_%d" % mpar])
            layer_norm_x(modt[:, 48:56], modt[:, 0:8], MTK([0, 1]) + ["ma1_%d" % mpar], lambda k: pbuf_(HB[k]), lambda k: K2("B%d" % HB[k]))

            ws = wuse(WI[("fnet", l)])
            ub_, pc_ = OPB[0:2], OPB[2:6]
            uview = pool[:, ub_[0]:ub_[0] + 2, :].rearrange("p a b -> p (a b)").rearrange("p (t c) -> p t c", c=256)
            UK = K2("B%d" % ub_[0]) + K2("B%d" % ub_[1])
            for t in range(8):
                bank = t % 2
                tm_proj_tile(ws, 256, 0, 256, t, PB[bank][:, 0:256], "PB%d" % bank)
                P.op("act", lambda e, t=t, bank=bank: e.activation(out=uview[:, t, :], in_=PB[bank][:, 0:256], func=AF.Copy),
                     reads=["PB%d" % bank], writes=UK)
            wdone(WI[("fnet", l)])
            run_slot_mods(0)
            for tbl in range(2):
                for ob in range(2):
                    ws = wuse(WI[("dft", l, tbl, ob)])
                    for ct in range(2):
                        bank = 2 + ct
                        for kt in range(8):
                            P.op("pe", lambda e, ws=ws, ct=ct, kt=kt, bank=bank: e.matmul(PB[bank][:], lhsT=uview[:, kt, ct * 128:(ct + 1) * 128],
                                                                                          rhs=wsl(ws, kt * 512, (kt + 1) * 512), start=(kt == 0), stop=(kt == 7)),
                                 reads=UK + [*WK(ws)], writes=["PB%d" % bank])
                        dstb = pc_[tbl * 2 + ct]
                        P.op("act", lambda e, bank=bank, dstb=dstb, ob=ob: e.activation(out=pool[:, dstb, ob * 512:(ob + 1) * 512], in_=PB[bank][:], func=AF.Copy),
                             reads=["PB%d" % bank], writes=["B%d_%d" % (dstb, ob)])
                    wdone(WI[("dft", l, tbl, ob)])
            for ct in range(2):
                for ob in range(2):
                    bank = ob
                    for tbl in range(2):
                        srcb = pc_[tbl * 2 + ct]
                        P.op("pe", lambda e, tbl=tbl, srcb=srcb, ob=ob, bank=bank: e.matmul(PB[bank][:], lhsT=dftcb[:, tbl, :], rhs=pool[:, srcb, ob * 512:(ob + 1) * 512],
                                                                                          start=(tbl == 0), stop=(tbl == 1)),
                             reads=["dftcb", "B%d_%d" % (srcb, ob)], writes=["PB%d" % bank])
                    P.op("act", lambda e, ct=ct, ob=ob, bank=bank: e.activation(out=pool[:, MC[ct], ob * 512:(ob + 1) * 512], in_=PB[bank][:], func=AF.Copy),
                         reads=["PB%d" % bank], writes=["B%d_%d" % (MC[ct], ob)])

            def v_proj(ws, n, c0, need_blk):
                for t in range(8):
                    bank = t % 2
                    tm_proj_tile(ws, n, c0, 128, t, PB[bank][:, 0:128], "PB%d" % bank)
                    P.op("act", lambda e, t=t, bank=bank: e.activation(out=vbuf[:, t, :], in_=PB[bank][:, 0:128], func=AF.Copy),
                         reads=["PB%d" % bank], writes=["vbuf"])
                for h in range(2):
                    P.op("act", lambda e, h=h: e.activation(out=vh[:, h, :, h * 64:(h + 1) * 64], in_=vbuf[:, :, h * 64:(h + 1) * 64], func=AF.Copy),
                         reads=["vbuf"], writes=["vh"])
                if need_blk:
                    for c in range(4):
                        P.op("dve", lambda e, c=c: e.tensor_copy(out=vblk[c * 32:(c + 1) * 32, :, c, :], in_=vbuf[c * 32:(c + 1) * 32, :, :]),
                             reads=["vbuf"], writes=["vblk"])

            def v_proj2(ws, n, c0, need_blk):
                for half in range(2):
                    for tq in range(4):
                        t = half * 4 + tq
                        tm_proj_tile(ws, n, c0, 128, t, PU[1][:, tq * 128:(tq + 1) * 128], "PU1")
                    P.op("act", lambda e, half=half: e.activation(out=vbuf[:, half * 4:(half + 1) * 4, :],
                                                                  in_=PU[1][:].rearrange("p (a b) -> p a b", b=128), func=AF.Copy),
                         reads=["PU1"], writes=["vbuf"])
                for h in range(2):
                    P.op("act", lambda e, h=h: e.activation(out=vh[:, h, :, h * 64:(h + 1) * 64], in_=vbuf[:, :, h * 64:(h + 1) * 64], func=AF.Copy),
                         reads=["vbuf"], writes=["vh"])
                if need_blk:
                    for c in range(4):
                        P.op("dve", lambda e, c=c: e.tensor_copy(out=vblk[c * 32:(c + 1) * 32, :, c, :], in_=vbuf[c * 32:(c + 1) * 32, :, :]),
                             reads=["vbuf"], writes=["vblk"])

            def finish_pair(kind, gate_b, mc_idx):
                oacc = tt(0)
                center = (kind == "ret")
                norm_stats([oacc], [K2("T0")], bd64[:], "bd64", center, 3, 4, [(26, 27)])
                tmp = tt(2)
                HSL = [slice(0, 512), slice(512, 1024)]
                if center:
                    for tb in range(2):
                        P.op("dve", lambda e, tb=tb: e.tensor_tensor(out=tmp[:, HSL[tb]], in0=oacc[:, HSL[tb]], in1=tt(3)[:, HSL[tb]], op=ALU.subtract),
                             reads=["T0_%d" % tb, "T3_%d" % tb], writes=["T2_%d" % tb])
                    for tb in range(2):
                        P.op("dve", lambda e, tb=tb: e.tensor_tensor(out=tmp[:, HSL[tb]], in0=tmp[:, HSL[tb]], in1=tt(4)[:, HSL[tb]], op=ALU.mult),
                             reads=["T2_%d" % tb, "T4_%d" % tb], writes=["T2_%d" % tb])
                else:
                    for tb in range(2):
                        P.op("dve", lambda e, tb=tb: e.tensor_tensor(out=tmp[:, HSL[tb]], in0=oacc[:, HSL[tb]], in1=tt(4)[:, HSL[tb]], op=ALU.mult),
                             reads=["T0_%d" % tb, "T4_%d" % tb], writes=["T2_%d" % tb])
                for tb in range(2):
                    P.op("dve", lambda e, tb=tb: e.tensor_tensor(out=pool[:, MC[mc_idx], HSL[tb]], in0=tmp[:, HSL[tb]], in1=pool[:, gate_b, HSL[tb]], op=ALU.mult),
                         reads=["T2_%d" % tb, "B%d_%d" % (gate_b, tb)], writes=["B%d_%d" % (MC[mc_idx], tb)])

            def ks_transposes(src_b, dst_b_list, evac):
                for t in range(8):
                    P.op("pe", lambda e, t=t: e.transpose(out=PT5[:, (t % 8) * 128:(t % 8 + 1) * 128], in_=pool[:, src_b, t * 128:(t + 1) * 128], identity=identb[:]),
                         reads=["B%d_%d" % (src_b, t // 4), "identb"], writes=["PT5"])
                for t in range(8):
                    evac(t, PT5[:, (t % 8) * 128:(t % 8 + 1) * 128], "PT5")

            for d in range(2):
                for h in range(2):
                    bi_ = OPB[2 + d * 2 + h]
                    P.op("dve", lambda e, h=h, bi_=bi_: e.memset(pool[(1 - h) * 64:(2 - h) * 64, bi_, :], 0.0), writes=K2("B%d" % bi_))
            def ret_tables(p_, par):
                for d in range(2):
                    cidx = l * 6 + d * 3 + p_
                    lgc = misc[:, 16 + cidx:17 + cidx]
                    nlgc = misc[:, 28 + cidx:29 + cidx]
                    pos = ctab[:, C_POSP1:C_POSP1 + 128] if d == 0 else ctab[:, C_POSREV:C_POSREV + 128]
                    P.op("act", lambda e, d=d, lgc=lgc, pos=pos: e.activation(out=dect[:, par, 2 * d, :], in_=pos, func=AF.Exp, scale=lgc),
                         reads=["ctab", "misc_lg2"], writes=["dect%d_%d" % (par, 2 * d)])
                    P.op("act", lambda e, d=d, nlgc=nlgc, pos=pos: e.activation(out=dect[:, par, 2 * d + 1, :], in_=pos, func=AF.Exp, scale=nlgc, bias=ln8c),
                         reads=["ctab", "misc_lg", "ln8c"], writes=["dect%d_%d" % (par, 2 * d + 1)])
                    pz = ctab[:, C_PZF:C_PZF + 128] if d == 0 else ctab[:, C_PZB:C_PZB + 128]
                    P.op("act", lambda e, d=d, lgc=lgc, pz=pz: e.activation(out=dect[:, par, 4 + d, :], in_=pz, func=AF.Exp, scale=lgc, bias=ln8c),
                         reads=["ctab", "misc_lg2", "ln8c"], writes=["dect%d_%d" % (par, 4 + d)])
                    P.op("pe", lambda e, d=d: e.transpose(out=PT5f[:, d * 128:(d + 1) * 128], in_=dect[:, par, 4 + d, :], identity=ident),
                         reads=["dect%d_%d" % (par, 4 + d), "ctab"], writes=["PT5"])
                for d in range(2):
                    cidx = l * 6 + d * 3 + p_
                    P.op("act", lambda e, d=d: e.activation(out=dect[:, par, 4 + d, :], in_=PT5f[:, d * 128:(d + 1) * 128], func=AF.Copy),
                         reads=["PT5"], writes=["dect%d_%d" % (par, 4 + d)])
                    P.op("dve", lambda e, d=d, cidx=cidx: e.tensor_tensor(out=dmt[:, par, d, 0:8], in0=misc[:, 40 + cidx:41 + cidx].to_broadcast([128, 8]),
                                                                         in1=ctab[:, (C_MRF if d == 0 else C_MRB):(C_MRF if d == 0 else C_MRB) + 8], op=ALU.mult),
                         reads=["misc_D", "ctab"], writes=["dmt%d_%d" % (par, d)])

            ret_tables(0, 0)
            for p in range(3):
                ws = wuse(WI[("ret", l, p)])
                qq = [OPB[0], OPB[1]]
                kkh = [[OPB[2], OPB[3]], [OPB[4], OPB[5]]]
                ks = [[OPB[6], 26], [OPB[7], 27]]
                for d_ in range(2):
                    for hh_ in range(2):
                        bz = ks[d_][hh_]
                        if p == 0 or hh_ == 1:
                            P.op("dve", lambda e, bz=bz: e.memset(pool[:, bz, :], 0.0), writes=K2("B%d" % bz))
                gate_b = OPB[8]
                par = p % 2
                wv = wsl(ws, 0, 4096).rearrange("p (k g a c) -> p k g a c", k=8, g=16, a=2)
                wsw = pool[:, 28:30, :].rearrange("p a b -> p (a b)").rearrange("p (k n) -> p k n", n=256)
                sv = wsw.rearrange("p k (g a c) -> p k g a c", g=8, a=2)
                for a in range(2):
                    P.op("act", lambda e, a=a, wv=wv, sv=sv: e.activation(out=sv[:, :, :, a, :], in_=wv[:, :, 0:8, 1 - a, :], func=AF.Copy),
                         reads=[*WK(ws)], writes=K2("B28") + K2("B29"))
                rope = t32[:, 0:2, :]
                P.op("sp", lambda e: e.dma_start(out=t32[:, 0:2, :], in_=rope_d), writes=K2("T0") + K2("T1"), dma="c1")
                for which in range(2):
                    rot = tt(2 + which)
                    SLs = [slice(0, 512), slice(512, 1024)]
                    for tb in range(2):
                        ba, bb_ = 2 * tb, 2 * tb + 1
                        fm_proj(ws, 512, which * 128, tb, ba)
                        for k in range(8):
                            P.op("pe", lambda e, k=k, which=which, tb=tb, bb_=bb_: e.matmul(PB[bb_][:], lhsT=wsw[:, k, which * 128:(which + 1) * 128],
                                                                                        rhs=pool[:, HB[k], tb * 512:(tb + 1) * 512], start=(k == 0), stop=(k == 7)),
                                 reads=K2("B28") + K2("B29") + ["B%d_%d" % (HB[k], tb)], writes=["PB%d" % bb_])
                    for tb in range(2):
                        ba, bb_ = 2 * tb, 2 * tb + 1
                        sl = SLs[tb]
                        P.op("dve", lambda e, rot=rot, sl=sl, ba=ba: e.tensor_tensor(out=rot[:, sl], in0=PB[ba][:], in1=rope[:, 0, sl], op=ALU.mult),
                             reads=["PB%d" % ba, "T0_%d" % tb], writes=["T%d_%d" % (2 + which, tb)])
                        P.op("dve", lambda e, sl=sl, bb_=bb_: e.tensor_tensor(out=tt(4)[:, sl], in0=PB[bb_][:], in1=rope[:, 1, sl], op=ALU.mult),
                             reads=["PB%d" % bb_, "T1_%d" % tb], writes=["T4_%d" % tb])
                    for tb in range(2):
                        sl = SLs[tb]
                        P.op("dve", lambda e, rot=rot, sl=sl: e.tensor_tensor(out=rot[:, sl], in0=rot[:, sl], in1=tt(4)[:, sl], op=ALU.add),
                             reads=["T%d_%d" % (2 + which, tb), "T4_%d" % tb], writes=["T%d_%d" % (2 + which, tb)])
                qrot, krot = tt(2), tt(3)
                r3 = lambda ap: ap.rearrange("p (t c) -> p t c", c=128)
                for d in range(2):
                    eq = dect[:, par, 2 * d, :]
                    ek = dect[:, par, 2 * d + 1, :]
                    P.op("dve", lambda e, d=d, eq=eq: e.tensor_tensor(out=r3(pbuf_(qq[d])), in0=r3(qrot), in1=eq.unsqueeze(1).to_broadcast([128, 8, 128]), op=ALU.mult),
                         reads=K2("T2") + ["dect%d_%d" % (par, 2 * d)], writes=K2("B%d" % qq[d]))
                    for h in range(2):
                        hs = slice(h * 64, (h + 1) * 64)
                        P.op("dve", lambda e, d=d, h=h, hs=hs, ek=ek: e.tensor_tensor(out=r3(pool[hs, kkh[d][h], :]), in0=r3(krot[hs, :]),
                                                                                  in1=ek[hs, :].unsqueeze(1).to_broadcast([64, 8, 128]), op=ALU.mult),
                             reads=K2("T3") + ["dect%d_%d" % (par, 2 * d + 1)], writes=K2("B%d" % kkh[d][h]))
                kb = 25
                P.op("act", lambda e: e.activation(out=pbuf_(kb), in_=krot, func=AF.Copy), reads=K2("T3"), writes=K2("B%d" % kb))

                def evac_ret(t, src, skey, par=par, ks=ks):
                    for d in range(2):
                        zt = dect[:, par, 4 + d, :]
                        for hh in range(2):
                            hcs = slice(hh * 64, (hh + 1) * 64)
                            P.op("dve", lambda e, d=d, t=t, src=src, zt=zt, ks=ks, hh=hh, hcs=hcs: e.tensor_tensor(
                                    out=pool[:, ks[d][hh], t * 128 + hh * 64:t * 128 + (hh + 1) * 64], in0=src[:, hcs], in1=zt[:, hcs], op=ALU.mult),
                                 reads=[skey, "dect%d_%d" % (par, 4 + d)], writes=["B%d_%d" % (ks[d][hh], t // 4)])
                ks_transposes(kb, ks, evac_ret)
                v_proj(ws, 512, 256, False)
                for tb in range(2):
                    fm_proj(ws, 512, 384, tb, 2 + tb)
                    P.op("act", lambda e, tb=tb: e.activation(out=pool[:, gate_b, tb * 512:(tb + 1) * 512], in_=PB[2 + tb][:], func=AF.Silu),
                         reads=["PB%d" % (2 + tb)], writes=["B%d_%d" % (gate_b, tb)])
                wdone(WI[("ret", l, p)])
                run_slot_mods(1 + p)
                if p < 2:
                    ret_tables(p + 1, (p + 1) % 2)
                gla_pair(l, p, 128, qq, kkh, ks, (C_RMF, C_RMB), s0r_d, osr_d, par, gate_b, "r")
                finish_pair("ret", gate_b, 2 + p)

            GBK = [(PS4[:], "PS4"), (PU[0][:], "PU0")]
            xf = sbfall[:, :, :, :].rearrange("p a b c -> p (a b c)").bitcast(F32)
            ALL_SBA = ["SBA%d_%d" % (d_, t_) for d_ in range(2) for t_ in range(8)]
            XKEYS = ["X%d_%d" % (i_, h_) for i_ in range(4) for h_ in range(2)]
            HS = [slice(0, 512), slice(512, 1024)]
            for p in range(3):
                ws = wuse(WI[("hg", l, p)])
                qq = [OPB[0], OPB[1]]
                kkh = [[OPB[2], OPB[3]], [OPB[4], OPB[5]]]
                ks = [[OPB[6], 28], [OPB[7], 29]]
                for d_ in range(2):
                    for hh_ in range(2):
                        bz = ks[d_][hh_]
                        P.op("dve", lambda e, bz=bz: e.memset(pool[:, bz, :], 0.0), writes=K2("B%d" % bz))
                gate_b = OPB[8]
                dpar = (p + 1) % 2
                TSET = [dict(sig=tt(0), kf=tt(1), bb=tt(3), einv=tt(4), K=("T0", "T1", "T3", "T4"), kb=25,
                             banks=[(PB[2][:], "PB2"), (PB[3][:], "PB3")]),
                        dict(sig=xf[:, 0:1024], kf=xf[:, 1024:2048], bb=xf[:, 2048:3072], einv=xf[:, 3072:4096],
                             K=("X0", "X1", "X2", "X3"), kb=26, banks=GBK)]
                for tb in range(2):
                    fm_proj2(ws, 640, 512, tb, GBK[tb][0], GBK[tb][1])
                    P.op("act", lambda e, tb=tb, gate_b=gate_b: e.activation(out=pool[:, gate_b, tb * 512:(tb + 1) * 512], in_=GBK[tb][0], func=AF.Silu),
                         reads=[GBK[tb][1]], writes=["B%d_%d" % (gate_b, tb)])
                v_proj2(ws, 640, 384, True)
                qf = tt(2)
                for tb in range(2):
                    fm_proj(ws, 640, 0, tb, tb)
                    P.op("act", lambda e, tb=tb: e.activation(out=qf[:, tb * 512:(tb + 1) * 512], in_=PB[tb][:], func=AF.Silu),
                         reads=["PB%d" % tb], writes=["T2_%d" % tb])
                for d in range(2):
                    for tb in range(2):
                        bk_ap, bk_key = TSET[d]["banks"][tb]
                        fm_proj2(ws, 640, 128 * (1 + d), tb, bk_ap, bk_key)
                cols = []
                for d in range(2):
                    lidx = d * 6 + l * 3 + p
                    cols.append(dict(oml=misc[:, 76 + lidx:77 + lidx], lb=misc[:, 64 + lidx:65 + lidx], lbm1=misc[:, 88 + lidx:89 + lidx],
                                     edge=(31 if d == 0 else 0)))
                DT = [(d, tb) for d in range(2) for tb in range(2)]
                for (d, tb) in DT:
                    T_, (bk_ap, bk_key) = TSET[d], TSET[d]["banks"][tb]
                    extra = ALL_SBA if (d == 1 and tb == 0) else []
                    P.op("act", lambda e, T_=T_, tb=tb, bk_ap=bk_ap: e.activation(out=T_["sig"][:, HS[tb]], in_=bk_ap, func=AF.Sigmoid),
                         reads=[bk_key], writes=["%s_%d" % (T_["K"][0], tb)] + extra)
                for (d, tb) in DT:
                    T_, C_ = TSET[d], cols[d]
                    P.op("dve", lambda e, T_=T_, C_=C_, tb=tb: e.tensor_scalar(out=T_["kf"][:, HS[tb]], in0=T_["sig"][:, HS[tb]], scalar1=C_["lbm1"], scalar2=C_["oml"],
                                                                               op0=ALU.mult, op1=ALU.add),
                         reads=["%s_%d" % (T_["K"][0], tb), "misc_lbm1", "misc_oml"], writes=["%s_%d" % (T_["K"][1], tb)])
                for (d, tb) in DT:
                    T_, C_ = TSET[d], cols[d]
                    P.op("act", lambda e, T_=T_, C_=C_, tb=tb: e.activation(out=T_["sig"][:, HS[tb]], in_=T_["sig"][:, HS[tb]], func=AF.Ln, scale=C_["oml"], bias=C_["lb"]),
                         reads=["%s_%d" % (T_["K"][0], tb), "misc_oml", "misc_lb"], writes=["%s_%d" % (T_["K"][0], tb)])
                for (d, tb) in DT:
                    T_ = TSET[d]
                    P.op("dve", lambda e, T_=T_, tb=tb: e.tensor_tensor_scan(out=T_["bb"][:, HS[tb]], data0=ctab[:, C_SEG:C_SEG + 512], data1=T_["sig"][:, HS[tb]],
                                                                             initial=0.0, op0=ALU.mult, op1=ALU.add),
                         reads=["%s_%d" % (T_["K"][0], tb), "ctab"], writes=["%s_%d" % (T_["K"][2], tb)])
                T1 = TSET[1]
                for tb in range(2):
                    b3 = T1["bb"][:, HS[tb]].rearrange("p (c s) -> p c s", s=32)
                    P.op("dve", lambda e, b3=b3: e.tensor_tensor(out=b3, in0=b3, in1=b3[:, :, 31:32].to_broadcast([128, 16, 32]), op=ALU.subtract),
                         reads=["X2_%d" % tb], writes=["X2_%d" % tb])
                for tb in range(2):
                    P.op("dve", lambda e, T1=T1, tb=tb: e.tensor_tensor(out=T1["bb"][:, HS[tb]], in0=T1["sig"][:, HS[tb]], in1=T1["bb"][:, HS[tb]], op=ALU.subtract),
                         reads=["X2_%d" % tb, "X0_%d" % tb], writes=["X2_%d" % tb])
                for (d, tb) in DT:
                    T_ = TSET[d]
                    P.op("act", lambda e, T_=T_, tb=tb: e.activation(out=T_["bb"][:, HS[tb]], in_=T_["bb"][:, HS[tb]], func=AF.Relu, bias=c80p, scale=1.0),
                         reads=["%s_%d" % (T_["K"][2], tb), "c80"], writes=["%s_%d" % (T_["K"][2], tb)])
                for (d, tb) in DT:
                    T_ = TSET[d]
                    P.op("act", lambda e, T_=T_, tb=tb: e.activation(out=T_["sig"][:, HS[tb]], in_=T_["bb"][:, HS[tb]], func=AF.Exp, bias=c80n, scale=1.0),
                         reads=["%s_%d" % (T_["K"][2], tb), "c80"], writes=["%s_%d" % (T_["K"][0], tb)])
                    P.op("act", lambda e, T_=T_, tb=tb: e.activation(out=T_["einv"][:, HS[tb]], in_=T_["bb"][:, HS[tb]], func=AF.Exp, bias=c80p, scale=-1.0),
                         reads=["%s_%d" % (T_["K"][2], tb), "c80"], writes=["%s_%d" % (T_["K"][3], tb)])
                for (d, tb) in DT:
                    T_, edge = TSET[d], cols[d]["edge"]
                    kE, kK, kI = "%s_%d" % (T_["K"][0], tb), "%s_%d" % (T_["K"][1], tb), "%s_%d" % (T_["K"][3], tb)
                    e3 = T_["sig"][:, HS[tb]].rearrange("p (c s) -> p c s", s=32)
                    i3 = T_["einv"][:, HS[tb]].rearrange("p (c s) -> p c s", s=32)
                    moff = (C_MHF if d == 0 else C_MHB) + tb * 16
                    P.op("dve", lambda e, d=d, tb=tb, e3=e3, moff=moff, edge=edge, dpar=dpar: e.tensor_tensor(out=dmt[:, dpar, d, tb * 16:(tb + 1) * 16], in0=e3[:, :, edge],
                                                                                                      in1=ctab[:, moff:moff + 16], op=ALU.mult),
                         reads=[kE, "ctab"], writes=["dmt%d_%d" % (dpar, d)])
                    P.op("dve", lambda e, d=d, tb=tb, T_=T_, qq=qq: e.tensor_tensor(out=pool[:, qq[d], HS[tb]], in0=qf[:, HS[tb]], in1=T_["sig"][:, HS[tb]], op=ALU.mult),
                         reads=["T2_%d" % tb, kE], writes=["B%d_%d" % (qq[d], tb)])
                    for h in range(2):
                        hs = slice(h * 64, (h + 1) * 64)
                        P.op("dve", lambda e, d=d, h=h, hs=hs, tb=tb, T_=T_, kkh=kkh: e.tensor_tensor(out=pool[hs, kkh[d][h], HS[tb]], in0=T_["kf"][hs, HS[tb]],
                                                                                                  in1=T_["einv"][hs, HS[tb]], op=ALU.mult),
                             reads=[kK, kI], writes=["B%d_%d" % (kkh[d][h], tb)])
                    P.op("dve", lambda e, i3=i3, e3=e3, edge=edge: e.tensor_tensor(out=i3, in0=i3, in1=e3[:, :, edge:edge + 1].to_broadcast([128, 16, 32]), op=ALU.mult),
                         reads=[kI, kE], writes=[kI])
                for (d, tb) in DT:
                    T_ = TSET[d]
                    P.op("dve", lambda e, T_=T_, tb=tb: e.tensor_tensor(out=pool[:, T_["kb"], HS[tb]], in0=T_["kf"][:, HS[tb]], in1=T_["einv"][:, HS[tb]], op=ALU.mult),
                         reads=["%s_%d" % (T_["K"][1], tb), "%s_%d" % (T_["K"][3], tb)], writes=["B%d_%d" % (T_["kb"], tb)])
                for d in range(2):
                    def evac_h(t, src, skey, d=d, ks=ks):
                        for hh in range(2):
                            P.op("act", lambda e, t=t, src=src, hh=hh: e.activation(out=pool[:, ks[d][hh], t * 128 + hh * 64:t * 128 + (hh + 1) * 64],
                                                                                in_=src[:, hh * 64:(hh + 1) * 64], func=AF.Copy),
                                 reads=[skey], writes=["B%d_%d" % (ks[d][hh], t // 4)])
                    ks_transposes(TSET[d]["kb"], ks, evac_h)
                wdone(WI[("hg", l, p)])
                run_slot_mods(4 + p)
                gla_pair(l, p, 32, qq, kkh, ks, (C_HMF, C_HMB), s0h_d, osh_d, dpar, gate_b, "h", extra_w=XKEYS)
                finish_pair("hg", gate_b, 5 + p)

            def resid_update(dc, tb, bank, gcol, gkeys):
                sl = slice(tb * 512, (tb + 1) * 512)
                tmp = tt(2)
                P.op("act", lambda e: e.activation(out=tmp[:, sl], in_=PB[bank][:], func=AF.Identity, scale=gcol),
                     reads=["PB%d" % bank] + gkeys, writes=["T2_%d" % tb])
                P.op("dve", lambda e: e.scalar_tensor_tensor(out=xT[:, dc, sl], in0=xT[:, dc, sl], scalar=float(ALPHA), in1=tmp[:, sl], op0=ALU.mult, op1=ALU.add),
                     reads=["xT%d_%d" % (dc, tb), "T2_%d" % tb], writes=["xT%d_%d" % (dc, tb)])

            for ob in range(2):
                ws = wuse(WI[("out", l, ob)])
                for dcc in range(4):
                    dc = ob * 4 + dcc
                    for tb in range(2):
                        bank = (dcc * 2 + tb) % 4
                        for fc in range(8):
                            P.op("pe", lambda e, ws=ws, dcc=dcc, fc=fc, tb=tb, bank=bank: e.matmul(PB[bank][:], lhsT=wsl(ws, fc * 512 + dcc * 128, fc * 512 + (dcc + 1) * 128),
                                                                                               rhs=pool[:, MC[fc], tb * 512:(tb + 1) * 512], start=(fc == 0), stop=(fc == 7)),
                                 reads=[*WK(ws), "B%d_%d" % (MC[fc], tb)], writes=["PB%d" % bank])
                        resid_update(dc, tb, bank, modt[:, 16 + dc:17 + dc], MTK([4, 5]))
                wdone(WI[("out", l, ob)])
            gcols = colsB[:, l * 16:l * 16 + 8]
            bcols = colsB[:, 32 + l * 16:32 + l * 16 + 8]
            layer_norm_x(gcols, bcols, ["colsB"], lambda k: xT[:, k, :], lambda k: XK(k))
            if stop == "mix%d" % l:
                break

            P.op("dve", lambda e, modt=modt: e.tensor_scalar(out=modt[:, 56:64], in0=modt[:, 32:40], scalar1=1.0, scalar2=None, op0=ALU.add),
                 reads=MTK([8, 9]), writes=["ma2_%d" % mpar])
            layer_norm_x(modt[:, 56:64], modt[:, 24:32], MTK([6, 7]) + ["ma2_%d" % mpar], lambda k: pbuf_(HB[k]), lambda k: K2("B%d" % HB[k]))
            for fb in range(11):
                nj = 2
                n = 256
                wsg = wuse(WI[("gate", l, fb)])
                wsu = wuse(WI[("up", l, fb)])
                for jj in range(nj):
                    j = fb * 2 + jj
                    for tb in range(2):
                        sl = slice(tb * 512, (tb + 1) * 512)
                        bg, bu = (tb * 2) % 4, (tb * 2 + 1) % 4
                        for k in range(8):
                            P.op("pe", lambda e, k=k, wsg=wsg, jj=jj, n=n, sl=sl, bg=bg: e.matmul(PB[bg][:], lhsT=wsl(wsg, k * n + jj * 128, k * n + (jj + 1) * 128),
                                                                                            rhs=pool[:, HB[k], sl], start=(k == 0), stop=(k == 7)),
                                 reads=[*WK(wsg), "B%d_%d" % (HB[k], tb)], writes=["PB%d" % bg])
                        for k in range(8):
                            P.op("pe", lambda e, k=k, wsu=wsu, jj=jj, n=n, sl=sl, bu=bu: e.matmul(PB[bu][:], lhsT=wsl(wsu, k * n + jj * 128, k * n + (jj + 1) * 128),
                                                                                            rhs=pool[:, HB[k], sl], start=(k == 0), stop=(k == 7)),
                                 reads=[*WK(wsu), "B%d_%d" % (HB[k], tb)], writes=["PB%d" % bu])
                        sgt = sgtb[:, tb, :]
                        P.op("act", lambda e, bg=bg, sgt=sgt: e.activation(out=sgt, in_=PB[bg][:], func=AF.Silu), reads=["PB%d" % bg], writes=["sgt%d" % tb])
                        P.op("dve", lambda e, bu=bu, sgt=sgt, j=j, sl=sl: e.tensor_tensor(out=pool[:, AB[j], sl], in0=PB[bu][:], in1=sgt, op=ALU.mult),
                             reads=["PB%d" % bu, "sgt%d" % tb], writes=["B%d_%d" % (AB[j], tb)])
                wdone(WI[("gate", l, fb)])
                wdone(WI[("up", l, fb)])
            for dc in range(8):
                ws = wuse(WI[("down", l, dc)])
                for tb in range(2):
                    bank = (dc * 2 + tb) % 4
                    for j in range(NFF):
                        P.op("pe", lambda e, ws=ws, j=j, tb=tb, bank=bank: e.matmul(PB[bank][:], lhsT=wsl(ws, j * 128, (j + 1) * 128),
                                                                                  rhs=pool[:, AB[j], tb * 512:(tb + 1) * 512], start=(j == 0), stop=(j == NFF - 1)),
                             reads=[*WK(ws), "B%d_%d" % (AB[j], tb)], writes=["PB%d" % bank])
                    resid_update(dc, tb, bank, modt[:, 40 + dc:41 + dc], MTK([10, 11]))
                wdone(WI[("down", l, dc)])
            gcols = colsB[:, l * 16 + 8:l * 16 + 16]
            bcols = colsB[:, 32 + l * 16 + 8:32 + l * 16 + 16]
            layer_norm_x(gcols, bcols, ["colsB"], lambda k: xT[:, k, :], lambda k: XK(k))
            if stop == "ffn%d" % l:
                break

        for t in range(8):
            b = t % 2
            for kh in range(2):
                bk = 2 * b + kh
                for kk in range(4):
                    k = kh * 4 + kk
                    P.op("pe", lambda e, bk=bk, kk=kk, k=k, t=t: e.transpose(out=PB[bk][:, kk * 128:(kk + 1) * 128], in_=xT[:, k, t * 128:(t + 1) * 128], identity=ident),
                         reads=["xT%d_%d" % (k, t // 4), "ctab"], writes=["PB%d" % bk])
                P.op("act" if kh == 0 else "dve",
                     (lambda e, kh=kh, b=b, bk=bk: e.activation(out=t32[:, b, kh * 512:(kh + 1) * 512], in_=PB[bk][:], func=AF.Copy)) if kh == 0 else
                     (lambda e, kh=kh, b=b, bk=bk: e.tensor_copy(out=t32[:, b, kh * 512:(kh + 1) * 512], in_=PB[bk][:])),
                     reads=["PB%d" % bk], writes=["T%d_%d" % (b, kh)])
            P.op("sp", lambda e, t=t, b=b: e.dma_start(out=y_d[t * 128:(t + 1) * 128, :], in_=t32[:, b, :]),
                 reads=K2("T%d" % b), writes=["y%d" % t], dma="yout%d" % b)
        P.emit()
    return nc


def _const_tables(is_sample):
    ct = np.zeros((128, NCT), np.float32)
    ct[:, C_ID:C_ID + 128] = np.eye(128, dtype=np.float32)
    tpos = np.arange(128, dtype=np.float32)
    ct[:, C_POSP1:C_POSP1 + 128] = tpos[None, :] + 1.0
    ct[:, C_POSREV:C_POSREV + 128] = 128.0 - tpos[None, :]
    bm = np.zeros((128, 128), np.float32)
    bm[:64, :64] = 1.0
    bm[64:, 64:] = 1.0
    mb = 1.0 if is_sample else 0.0
    ct[:, C_BM:C_BM + 128] = bm
    ct[:, C_BMB:C_BMB + 128] = bm * mb
    ct[:, C_PCOL] = 127.0 - tpos
    ct[:, C_PCOL + 1] = tpos
    for off_f, off_b, G, cps in ((C_MRF, C_MRB, 8, 2), (C_MHF, C_MHB, 32, 8)):
        mf = np.ones(G, np.float32)
        mbk = np.ones(G, np.float32)
        for g in range(G):
            if g % cps == 0 and g > 0:
                mf[g] = mb
            if (g + 1) % cps == 0 and g != G - 1:
                mbk[g] = mb
        ct[:, off_f:off_f + G] = mf[None, :]
        ct[:, off_b:off_b + G] = mbk[None, :]
    n = np.arange(64)
    ang = 2.0 * np.pi * np.outer(n, n) / 64.0
    c64 = np.cos(ang) / 8.0
    s64 = np.sin(ang) / 8.0
    bdc = np.zeros((128, 128))
    bds = np.zeros((128, 128))
    bdc[:64, :64] = c64
    bdc[64:, 64:] = c64
    bds[:64, :64] = -s64
    bds[64:, 64:] = -s64
    ct[:, C_DFTC:C_DFTC + 128] = bdc
    ct[:, C_DFTS:C_DFTS + 128] = bds
    j = np.arange(128)[:, None]
    i = np.arange(128)[None, :]
    ct[:, C_RMF:C_RMF + 128] = (j <= i)
    ct[:, C_RMB:C_RMB + 128] = (j >= i)
    same = (j // 32 == i // 32)
    ct[:, C_HMF:C_HMF + 128] = (j <= i) & same
    ct[:, C_HMB:C_HMB + 128] = (j >= i) & same
    seg = np.ones(1024, np.float32)
    seg[::32] = 0.0
    ct[:, C_SEG:C_SEG + 1024] = seg[None, :]
    ct[:, C_PZF:C_PZF + 128] = 127.0 - tpos[None, :]
    ct[:, C_PZB:C_PZB + 128] = tpos[None, :]
    ct[:, C_MB] = mb
    ct[:, C_MB + 1] = 1.0
    return ct


def _rope_tables(is_sample):
    r = np.zeros((128, 2, T), np.float64)
    if not is_sample:
        r[:, 0, :] = 1.0
        return r.astype(np.float32)
    tok = np.arange(T)
    rows = (tok // 64).astype(np.float64)
    cols = (tok % 64).astype(np.float64)
    half = 32
    inv = 10000.0 ** (-np.arange(0, half, 2, dtype=np.float64) / half)
    for pp in range(128):
        dd = pp % 64
        pos = rows if dd < 32 else cols
        w = dd % 32
        fi = w % 16
        ang = pos * inv[fi]
        r[pp, 0, :] = np.cos(ang)
        r[pp, 1, :] = -np.sin(ang) if w < 16 else np.sin(ang)
    return r.astype(np.float32)


def _dft_tables(is_sample):
    L = 1024 if is_sample else 256
    n = np.arange(L)
    ang = 2.0 * np.pi * np.outer(n, n) / L
    c = np.cos(ang) / np.sqrt(L)
    s = np.sin(ang) / np.sqrt(L)
    out = np.zeros((2, T, T), np.float32)
    for b in range(T // L):
        out[0, b * L:(b + 1) * L, b * L:(b + 1) * L] = c
        out[1, b * L:(b + 1) * L, b * L:(b + 1) * L] = s
    return out


def _bd_state(s):
    out = np.zeros((DEPTH, 2, 3, 128, 128), np.float32)
    for p in range(3):
        out[:, :, p, :64, :64] = s[:, :, 2 * p]
        out[:, :, p, 64:, 64:] = s[:, :, 2 * p + 1]
    return out


_NC_CACHE = {}


def kernel(x_prompt, x_sample, c, state_ret, state_hgrn, c_ctx, w_mod, b_mod, w_in, w_out,
           ret_log_decay, hg_lower_bound, ln_g, ln_b, w_gate, w_up, w_down):
    f = lambda a: np.ascontiguousarray(np.asarray(a, dtype=np.float32))
    x_prompt, x_sample, c, state_ret, state_hgrn, c_ctx = map(f, (x_prompt, x_sample, c, state_ret, state_hgrn, c_ctx))
    w_mod, b_mod, w_in, w_out, w_gate, w_up, w_down = map(f, (w_mod, b_mod, w_in, w_out, w_gate, w_up, w_down))
    ret_log_decay, hg_lower_bound, ln_g, ln_b = map(f, (ret_log_decay, hg_lower_bound, ln_g, ln_b))

    if "nc" not in _NC_CACHE:
        import os
        _NC_CACHE["nc"] = build_program(os.environ.get("KSTOP"))
    nc = _NC_CACHE["nc"]

    smB = np.zeros((128, 128), np.float32)
    smB[0:32] = ln_g.reshape(32, 128)
    smB[32:64] = ln_b.reshape(32, 128)
    smB[64:76] = hg_lower_bound.reshape(12, 128)
    dec = np.repeat(ret_log_decay.reshape(DEPTH, 2, 6), 64, axis=-1).reshape(12, 128)
    smB[76:88] = dec
    tabs = {s: (_const_tables(s), _rope_tables(s), _dft_tables(s)) for s in (False, True)}
    zs = np.zeros((DEPTH, 2, 3, 128, 128), np.float32)
    in_maps = []
    for core in range(NCORES):
        is_sample = core >= 4
        if is_sample:
            b = core - 4
            xin = x_sample[b]
            cvec = c[b]
            s0r = _bd_state(state_ret[b])
            s0h = _bd_state(state_hgrn[b])
        else:
            xin = x_prompt[core * 4:(core + 1) * 4].reshape(T, D)
            cvec = c_ctx
            s0r, s0h = zs, zs
        smA = np.zeros((128, 128), np.float32)
        smA[0:96] = b_mod.reshape(96, 128)
        smA[96:104] = cvec.reshape(8, 128)
        ct, rp, dl = tabs[is_sample]
        in_maps.append(dict(x=np.ascontiguousarray(xin), smA=smA, smB=smB, ctab=ct, rope=rp, dftL=dl,
                            s0r=s0r, s0h=s0h, w_mod=w_mod, w_in=w_in, w_out=w_out, w_gate=w_gate, w_up=w_up, w_down=w_down))
    res = run_bass_kernel_spmd(nc, in_maps, core_ids=list(range(NCORES)))
    R = res.results
    y_prompt = np.stack([R[i]["y"] for i in range(4)]).reshape(16, 256, D)
    y_sample = np.stack([R[i]["y"] for i in range(4, 8)])

    def unpack(name):
        out = np.zeros((16, DEPTH, 2, 6, 64, 64), np.float32)
        for core in range(4):
            o = R[core][name]
            for p in range(3):
                out[core * 4:(core + 1) * 4, :, :, 2 * p] = o[:, :, :, p, :64, :64].transpose(2, 0, 1, 3, 4)
                out[core * 4:(core + 1) * 4, :, :, 2 * p + 1] = o[:, :, :, p, 64:, 64:].transpose(2, 0, 1, 3, 4)
        return out

    return (y_prompt.astype(np.float32), y_sample.astype(np.float32), unpack("osr"), unpack("osh"))
```

```python
import contextlib
import math
import numpy as np
import concourse.bass as bass
import concourse.mybir as mybir
from concourse.bass_utils import run_bass_kernel_spmd

F32 = mybir.dt.float32
BF16 = mybir.dt.bfloat16
AF = mybir.ActivationFunctionType
ALU = mybir.AluOpType

D = 1024
T = 1024
NCORES = 8
DEPTH = 2
DFF = 2816
NFF = 22
ALPHA = (2 * DEPTH) ** 0.25
LN_EPS = 1e-5
LN8 = math.log(0.125)

C_ID, C_POSP1, C_POSREV, C_BM, C_BMB, C_PCOL = 0, 128, 256, 384, 512, 640
C_MRF, C_MRB, C_MHF, C_MHB = 642, 650, 658, 690
C_DFTC, C_DFTS = 722, 850
C_RMF, C_RMB, C_HMF, C_HMB = 978, 1106, 1234, 1362
C_SEG = 1490
C_PZF, C_PZB = 1490 + 1024, 1490 + 1024 + 128
C_MB = 1490 + 1024 + 256
NCT = 1490 + 1024 + 256 + 2
WUNIT = 2560
NU = 6
NPOOL = 30


class Prog:
    ENG = ("pe", "act", "dve", "pool", "sp")

    def __init__(self, nc):
        self.nc = nc
        self.ops = []

    def op(self, eng, fn, reads=(), writes=(), dma=None):
        self.ops.append(dict(eng=eng, fn=fn, reads=tuple(reads), writes=tuple(writes), dma=dma))

    def emit(self, final_waits=()):
        nc = self.nc
        ops = self.ops
        last_w, readers = {}, {}
        eng_idx = {e: 0 for e in self.ENG}
        dma_gen = {}
        signaling = set()
        for o in ops:
            e = o["eng"]
            idx = eng_idx[e]
            eng_idx[e] += 1
            o["idx"] = idx
            deps = set()
            for r in o["reads"]:
                if r in last_w:
                    deps.add(last_w[r])
            for w in o["writes"]:
                if w in last_w:
                    deps.add(last_w[w])
                for rd in readers.get(w, ()):
                    deps.add(rd)
            if o["dma"] is not None:
                g = dma_gen.get(o["dma"], 0) + 1
                dma_gen[o["dma"]] = g
                ev = ("dma", o["dma"], g)
            else:
                ev = ("eng", e, idx)
            deps.discard(ev)
            o["deps"] = deps
            for d in deps:
                if d[0] == "eng":
                    signaling.add((d[1], d[2]))
            for r in o["reads"]:
                readers.setdefault(r, []).append(ev)
            for w in o["writes"]:
                last_w[w] = ev
                readers[w] = []
        final_events = [last_w[k] for k in final_waits]
        count = {}
        run = {e: 0 for e in self.ENG}
        per_eng = {e: [] for e in self.ENG}
        for o in ops:
            per_eng[o["eng"]].append(o)
            key = (o["eng"], o["idx"])
            if o["dma"] is None and key in signaling:
                run[o["eng"]] += 1
                count[key] = run[o["eng"]]
                o["signal"] = True
            else:
                o["signal"] = False
        dma_keys = sorted(dma_gen.keys())
        with contextlib.ExitStack() as st:
            sem_e = {e: st.enter_context(nc.semaphore("sem_" + e)) for e in self.ENG}
            sem_d = {k: st.enter_context(nc.semaphore("semd_" + k)) for k in dma_keys}
            block = st.enter_context(nc.Block())

            def run_engine(e, eo):
                wm = {}

                def do_waits(deps):
                    need = {}
                    for d in deps:
                        if d[0] == "eng":
                            if d[1] == "pe" and e == "pe":
                                continue
                            k = ("eng", d[1])
                            v = count[(d[1], d[2])]
                        else:
                            k = ("dma", d[1])
                            v = 16 * d[2]
                        if v > need.get(k, 0):
                            need[k] = v
                    for k, v in need.items():
                        if wm.get(k, 0) >= v:
                            continue
                        wm[k] = v
                        s = sem_e[k[1]] if k[0] == "eng" else sem_d[k[1]]
                        eo.wait_ge(s, v)

                for o in per_eng[e]:
                    do_waits(o["deps"])
                    ins = o["fn"](eo)
                    if o["dma"] is not None:
                        ins.then_inc(sem_d[o["dma"]], 16)
                    elif o["signal"]:
                        ins.then_inc(sem_e[e], 1)
                if e == "sp":
                    do_waits(list(final_events) + [("dma", k, g) for k, g in dma_gen.items()])

            @block.tensor
            def _(eng):
                run_engine("pe", eng)

            @block.scalar
            def _(eng):
                run_engine("act", eng)

            @block.vector
            def _(eng):
                run_engine("dve", eng)

            @block.gpsimd
            def _(eng):
                run_engine("pool", eng)

            @block.sync
            def _(eng):
                run_engine("sp", eng)


def K2(name):
    return [name + "_0", name + "_1"]


def build_program(stop=None):
    nc = bass.Bass("TRN2", target_bir_lowering=False)

    def din(name, shape):
        return nc.dram_tensor(name, list(shape), F32, kind="ExternalInput").ap()

    def dout(name, shape):
        return nc.dram_tensor(name, list(shape), F32, kind="ExternalOutput").ap()

    x_d = din("x", [T, D])
    smA_d = din("smA", [128, 128])
    smB_d = din("smB", [128, 128])
    ctab_d = din("ctab", [128, NCT])
    rope_d = din("rope", [128, 2, T])
    dftL_d = din("dftL", [2, T, T])
    s0r_d = din("s0r", [DEPTH, 2, 3, 128, 128])
    s0h_d = din("s0h", [DEPTH, 2, 3, 128, 128])
    wmod_d = din("w_mod", [DEPTH, D, 6 * D])
    win_d = din("w_in", [DEPTH, D, 3712])
    wout_d = din("w_out", [DEPTH, D, D])
    wg_d = din("w_gate", [DEPTH, D, DFF])
    wu_d = din("w_up", [DEPTH, D, DFF])
    wd_d = din("w_down", [DEPTH, DFF, D])
    y_d = dout("y", [T, D])
    osr_d = dout("osr", [DEPTH, 2, 4, 3, 128, 128])
    osh_d = dout("osh", [DEPTH, 2, 4, 3, 128, 128])

    P = Prog(nc)
    st = contextlib.ExitStack()
    with st:
        def sb(name, shape, dt):
            return st.enter_context(nc.sbuf_tensor(name, list(shape), dt))

        def ps(name, shape, dt):
            return st.enter_context(nc.psum_tensor(name, list(shape), dt))

        xT = sb("xT", [128, 8, T], F32)
        ctab = sb("ctab_sb", [128, NCT], F32)
        colsA = sb("colsA", [128, 128], F32)
        colsB = sb("colsB", [128, 128], F32)
        wring = sb("wring", [128, NU, WUNIT], BF16)
        pool = sb("pool", [128, NPOOL, T], BF16)
        t32 = sb("t32", [128, 5, T], F32)
        smt = t32[:, 4, 0:256]
        vbuf = sb("vbuf", [128, 8, 128], BF16)
        vh = sb("vh", [128, 2, 8, 128], BF16)
        vblk = sb("vblk", [128, 8, 4, 128], BF16)
        sst = sb("sst", [128, 2, 8, 128], F32)
        sbfall = sb("sbfall", [128, 2, 32, 128], BF16)
        pbuf = sb("pbuf", [128, 2, 4, 128], BF16)
        modt_t = sb("modt", [128, 2, 64], F32)
        scb = sb("scb", [128, 8], BF16)
        misc = sb("misc", [128, 160], F32)
        dect = sb("dect", [128, 2, 6, 128], F32)
        dmt = sb("dmt", [128, 2, 2, 32], F32)
        onesD = sb("onesD", [128, 128], BF16)
        bd64 = sb("bd64", [128, 128], BF16)
        identb = sb("identb", [128, 128], BF16)
        dftcb = sb("dftcb", [128, 2, 128], BF16)
        sgtb = sb("sgtb", [128, 2, 512], BF16)
        PB = [ps("PB%d" % i, [128, 512], F32) for i in range(4)]
        PS4 = ps("PS4", [128, 512], F32)
        PT5 = ps("PT5", [128, 1024], BF16)
        PU = [ps("PU%d" % i, [128, 512], F32) for i in range(2)]

        ident = ctab[:, C_ID:C_ID + 128]
        PT5f = PT5[:, :].bitcast(F32)

        def pbuf_(i):
            return pool[:, i, :]

        def tt(i):
            return t32[:, i, :]

        wloads = []
        wstate = dict(issued=0, pos=0)
        wflat = wring[:, :, :].rearrange("p a b -> p (a b)")
        unit_occ = [None] * NU
        slot_of = {}
        wdone_flags = {}
        cur_units = {}

        def wsl(ws, a_, b_):
            return wflat[:, ws * WUNIT + a_:ws * WUNIT + b_]

        def WK(ws):
            return ["W%d" % (ws + i) for i in range(cur_units[ws])]

        def wreq(make, nu=2):
            wloads.append((nu, make))
            return len(wloads) - 1

        def _pump():
            while wstate["issued"] < len(wloads):
                j = wstate["issued"]
                nu, make = wloads[j]
                pos = wstate["pos"]
                if pos + nu > NU:
                    pos = 0
                ok = all(unit_occ[u] is None or wdone_flags.get(unit_occ[u], False) for u in range(pos, pos + nu))
                if not ok:
                    return
                for u in range(pos, pos + nu):
                    unit_occ[u] = j
                slot_of[j] = pos
                for (o_ap, i_ap) in make(pos):
                    P.op("pool", (lambda e, o_ap=o_ap, i_ap=i_ap: e.dma_start(out=o_ap, in_=i_ap)),
                         writes=["W%d" % u for u in range(pos, pos + nu)], dma="W%d" % pos)
                wstate["pos"] = (pos + nu) % NU
                wstate["issued"] += 1

        def wuse(i):
            _pump()
            assert wstate["issued"] > i, (i, wstate["issued"])
            cur_units[slot_of[i]] = wloads[i][0]
            return slot_of[i]

        def wdone(i):
            wdone_flags[i] = True
            _pump()

        def wview(s, k, n):
            return wsl(s, 0, k * n).rearrange("p (k n) -> p k n", n=n)

        def mk_std(dram2d, c0, n):
            def make(s):
                return [(wview(s, 8, n), dram2d[:, c0:c0 + n].rearrange("(k p) n -> p k n", p=128))]
            return make

        def mk_grp(dram2d, c0, ng, stride):
            def make(s):
                res = []
                v = wview(s, 8, ng * 128)
                for g in range(ng):
                    res.append((v[:, :, g * 128:(g + 1) * 128],
                                dram2d[:, c0 + g * stride:c0 + g * stride + 128].rearrange("(k p) n -> p k n", p=128)))
                return res
            return make

        def mk_down(dram2d, c0):
            def make(s):
                return [(wview(s, NFF, 128), dram2d[:, c0:c0 + 128].rearrange("(k p) n -> p k n", p=128))]
            return make

        WI = {}

        def reg_mods(lm, js):
            for j in js:
                WI[("mod", lm, j)] = wreq(mk_std(wmod_d[lm], j * 512, 512))

        def slot_mods(l, i):
            if i <= 3:
                return [(l, 4 + 2 * i), (l, 5 + 2 * i)]
            if i <= 5 and l + 1 < DEPTH:
                return [(l + 1, 2 * (i - 4)), (l + 1, 2 * (i - 4) + 1)]
            return []

        reg_mods(0, range(4))
        for l in range(DEPTH):
            WI[("fnet", l)] = wreq(mk_std(win_d[l], 0, 256), 1)
            for (lm, j) in slot_mods(l, 0):
                reg_mods(lm, [j])
            for tbl in range(2):
                for ob in range(2):
                    WI[("dft", l, tbl, ob)] = wreq(mk_std(dftL_d[tbl], ob * 512, 512))
            for p in range(3):
                WI[("ret", l, p)] = wreq(mk_grp(win_d[l], 256 + 128 * p, 4, 384))
                for (lm, j) in slot_mods(l, 1 + p):
                    reg_mods(lm, [j])
            for p in range(3):
                WI[("hg", l, p)] = wreq(mk_grp(win_d[l], 1792 + 128 * p, 5, 384))
                for (lm, j) in slot_mods(l, 4 + p):
                    reg_mods(lm, [j])
            for ob in range(2):
                WI[("out", l, ob)] = wreq(mk_std(wout_d[l], ob * 512, 512))
            for fb in range(11):
                WI[("gate", l, fb)] = wreq(mk_std(wg_d[l], fb * 256, 256), 1)
                WI[("up", l, fb)] = wreq(mk_std(wu_d[l], fb * 256, 256), 1)
            for dc in range(8):
                WI[("down", l, dc)] = wreq(mk_down(wd_d[l], dc * 128))

        P.op("sp", lambda e: e.dma_start(out=ctab[:], in_=ctab_d), writes=["ctab"], dma="c0")
        P.op("sp", lambda e: e.dma_start(out=smt[:, 0:128], in_=smA_d), writes=["T4_0"], dma="c2")
        P.op("sp", lambda e: e.dma_start(out=smt[:, 128:256], in_=smB_d), writes=["T4_0"], dma="c3")
        P.op("dve", lambda e: e.memset(onesD[:], 1.0 / 1024.0), writes=["onesD"])
        P.op("dve", lambda e: e.tensor_scalar(out=bd64[:], in0=ctab[:, C_BM:C_BM + 128], scalar1=1.0 / 64.0, scalar2=None, op0=ALU.mult),
             reads=["ctab"], writes=["bd64"])
        P.op("dve", lambda e: e.tensor_copy(out=identb[:], in_=ident), reads=["ctab"], writes=["identb"])
        P.op("dve", lambda e: e.tensor_copy(out=dftcb[:].rearrange("p a b -> p (a b)"), in_=ctab[:, C_DFTC:C_DFTC + 256]),
             reads=["ctab"], writes=["dftcb"])
        P.op("dve", lambda e: e.memset(sbfall[:].rearrange("p a b c -> p (a b c)"), 0.0), writes=["SBA%d_%d" % (d_, t_) for d_ in range(2) for t_ in range(8)])
        P.op("pool", lambda e: e.memset(vh[:].rearrange("p a b c -> p (a b c)"), 0.0), writes=["vh"])
        P.op("pool", lambda e: e.memset(vblk[:].rearrange("p a b c -> p (a b c)"), 0.0), writes=["vblk"])
        P.op("pe", lambda e: e.transpose(out=PB[0][:, 0:128], in_=smt[:, 0:128], identity=ident), reads=["T4_0", "ctab"], writes=["PB0"])
        P.op("pe", lambda e: e.transpose(out=PB[0][:, 128:256], in_=smt[:, 128:256], identity=ident), reads=["T4_0", "ctab"], writes=["PB0"])
        P.op("act", lambda e: e.activation(out=colsA[:], in_=PB[0][:, 0:128], func=AF.Copy), reads=["PB0"], writes=["colsA"])
        P.op("act", lambda e: e.activation(out=colsB[:], in_=PB[0][:, 128:256], func=AF.Copy), reads=["PB0"], writes=["colsB"])
        P.op("act", lambda e: e.activation(out=scb[:], in_=colsA[:, 96:104], func=AF.Silu), reads=["colsA"], writes=["scb"])
        P.op("act", lambda e: e.activation(out=misc[:, 28:40], in_=colsB[:, 76:88], func=AF.Exp), reads=["colsB"], writes=["misc_lg"])
        P.op("dve", lambda e: e.tensor_scalar(out=misc[:, 16:28], in0=misc[:, 28:40], scalar1=-1.0, scalar2=None, op0=ALU.mult),
             reads=["misc_lg"], writes=["misc_lg2"])
        P.op("act", lambda e: e.activation(out=misc[:, 40:52], in_=misc[:, 16:28], func=AF.Exp, scale=128.0), reads=["misc_lg2"], writes=["misc_D"])
        P.op("act", lambda e: e.activation(out=misc[:, 52:64], in_=colsB[:, 64:76], func=AF.Exp), reads=["colsB"], writes=["misc_e"])
        P.op("dve", lambda e: e.memset(misc[:, 64:76], 0.0), writes=["misc_lb"])
        for d_ in range(2):
            e0 = misc[:, 52 + d_ * 6:52 + d_ * 6 + 3]
            e1 = misc[:, 52 + d_ * 6 + 3:52 + d_ * 6 + 6]
            dst = misc[:, 64 + d_ * 6 + 3:64 + d_ * 6 + 6]
            tmpc = misc[:, 100 + d_ * 3:103 + d_ * 3]
            P.op("dve", lambda e, e0=e0, e1=e1, tmpc=tmpc: e.tensor_tensor(out=tmpc, in0=e0, in1=e1, op=ALU.add), reads=["misc_e"], writes=["misc_t%d" % d_])
            P.op("dve", lambda e, tmpc=tmpc: e.reciprocal(out=tmpc, in_=tmpc), reads=["misc_t%d" % d_], writes=["misc_t%d" % d_])
            P.op("dve", lambda e, e1=e1, tmpc=tmpc, dst=dst: e.tensor_tensor(out=dst, in0=e1, in1=tmpc, op=ALU.mult),
                 reads=["misc_t%d" % d_, "misc_e", "misc_lb"], writes=["misc_lb"])
        P.op("dve", lambda e: e.tensor_scalar(out=misc[:, 76:88], in0=misc[:, 64:76], scalar1=-1.0, scalar2=1.0, op0=ALU.mult, op1=ALU.add),
             reads=["misc_lb"], writes=["misc_oml"])
        P.op("dve", lambda e: e.tensor_scalar(out=misc[:, 88:100], in0=misc[:, 64:76], scalar1=-1.0, scalar2=None, op0=ALU.add),
             reads=["misc_lb"], writes=["misc_lbm1"])

        for t in range(8):
            b = t % 2
            P.op("sp", lambda e, t=t, b=b: e.dma_start(out=t32[:, b, :], in_=x_d[t * 128:(t + 1) * 128, :]),
                 writes=K2("T%d" % b), dma="xin%d" % b)
            for kh in range(2):
                bk = 2 * b + kh
                for kk in range(4):
                    k = kh * 4 + kk
                    src = t32[:, b, k * 128:(k + 1) * 128]
                    P.op("pe", lambda e, bk=bk, kk=kk, src=src: e.transpose(out=PB[bk][:, kk * 128:(kk + 1) * 128], in_=src, identity=ident),
                         reads=K2("T%d" % b) + ["ctab"], writes=["PB%d" % bk])
                P.op("act" if kh == 0 else "dve",
                     (lambda e, kh=kh, t=t, bk=bk: e.activation(out=xT[:, kh * 4:kh * 4 + 4, t * 128:(t + 1) * 128],
                                                                 in_=PB[bk][:].rearrange("p (a b) -> p a b", b=128), func=AF.Copy)) if kh == 0 else
                     (lambda e, kh=kh, t=t, bk=bk: e.tensor_copy(out=xT[:, kh * 4:kh * 4 + 4, t * 128:(t + 1) * 128],
                                                                  in_=PB[bk][:].rearrange("p (a b) -> p a b", b=128))),
                     reads=["PB%d" % bk], writes=["xT%d_%d" % (k_, t // 4) for k_ in range(kh * 4, kh * 4 + 4)])

        XK = lambda k: ["xT%d_0" % k, "xT%d_1" % k]

        def norm_stats(chunks, chunk_keys, ones_ap, ones_key, center, mean_t, rstd_t, tmpb_all):
            n = len(chunks)
            for ci, (ch, ck) in enumerate(zip(chunks, chunk_keys)):
                tmpb = tmpb_all[ci % len(tmpb_all)]
                xb, xq = pbuf_(tmpb[0]), pbuf_(tmpb[1])
                if n == 1:
                    for tb in range(2):
                        hsl = slice(tb * 512, (tb + 1) * 512)
                        if center:
                            P.op("act", lambda e, ch=ch, xb=xb, hsl=hsl: e.activation(out=xb[:, hsl], in_=ch[:, hsl], func=AF.Copy),
                                 reads=[ck[tb]], writes=["B%d_%d" % (tmpb[0], tb)])
                        P.op("dve", lambda e, ch=ch, xq=xq, hsl=hsl: e.tensor_tensor(out=xq[:, hsl], in0=ch[:, hsl], in1=ch[:, hsl], op=ALU.mult),
                             reads=[ck[tb]], writes=["B%d_%d" % (tmpb[1], tb)])
                else:
                    if center:
                        P.op("act", lambda e, ch=ch, xb=xb: e.activation(out=xb, in_=ch, func=AF.Copy), reads=ck, writes=K2("B%d" % tmpb[0]))
                    P.op("dve", lambda e, ch=ch, xq=xq: e.tensor_tensor(out=xq, in0=ch, in1=ch, op=ALU.mult), reads=ck, writes=K2("B%d" % tmpb[1]))
                for tb in range(2):
                    if center:
                        P.op("pe", lambda e, tb=tb, xb=xb, ci=ci: e.matmul(PB[tb][:], lhsT=ones_ap, rhs=xb[:, tb * 512:(tb + 1) * 512],
                                                                         start=(ci == 0), stop=(ci == n - 1)),
                             reads=["B%d_%d" % (tmpb[0], tb), ones_key], writes=["PB%d" % tb])
                    P.op("pe", lambda e, tb=tb, xq=xq, ci=ci: e.matmul(PB[2 + tb][:], lhsT=ones_ap, rhs=xq[:, tb * 512:(tb + 1) * 512],
                                                                     start=(ci == 0), stop=(ci == n - 1)),
                         reads=["B%d_%d" % (tmpb[1], tb), ones_key], writes=["PB%d" % (2 + tb)])
            mean, rstd = tt(mean_t), tt(rstd_t)
            SL = [slice(0, 512), slice(512, 1024)]
            MKk = ["T%d_%d" % (mean_t, tb) for tb in range(2)]
            RKk = ["T%d_%d" % (rstd_t, tb) for tb in range(2)]
            if center:
                for tb in range(2):
                    P.op("act", lambda e, tb=tb: e.activation(out=mean[:, SL[tb]], in_=PB[tb][:], func=AF.Copy),
                         reads=["PB%d" % tb], writes=[MKk[tb]])
                for tb in range(2):
                    P.op("act", lambda e, tb=tb: e.activation(out=rstd[:, SL[tb]], in_=PB[tb][:], func=AF.Square),
                         reads=["PB%d" % tb], writes=[RKk[tb]])
                for tb in range(2):
                    P.op("dve", lambda e, tb=tb: e.tensor_tensor(out=rstd[:, SL[tb]], in0=PB[2 + tb][:], in1=rstd[:, SL[tb]], op=ALU.subtract),
                         reads=["PB%d" % (2 + tb), RKk[tb]], writes=[RKk[tb]])
                for tb in range(2):
                    P.op("act", lambda e, tb=tb: e.activation(out=rstd[:, SL[tb]], in_=rstd[:, SL[tb]], func=AF.Ln, bias=epsc, scale=1.0),
                         reads=[RKk[tb], "epsc"], writes=[RKk[tb]])
            else:
                for tb in range(2):
                    P.op("act", lambda e, tb=tb: e.activation(out=rstd[:, SL[tb]], in_=PB[2 + tb][:], func=AF.Ln, bias=epsc, scale=1.0),
                         reads=["PB%d" % (2 + tb), "epsc"], writes=[RKk[tb]])
            for tb in range(2):
                P.op("act", lambda e, tb=tb: e.activation(out=rstd[:, SL[tb]], in_=rstd[:, SL[tb]], func=AF.Exp, scale=-0.5),
                     reads=[RKk[tb]], writes=[RKk[tb]])

        epsc = misc[:, 110:111]
        P.op("dve", lambda e: e.memset(epsc, LN_EPS), writes=["epsc"])
        c80p = misc[:, 112:113]
        c80n = misc[:, 113:114]
        P.op("dve", lambda e: e.memset(c80p, 80.0), writes=["c80"])
        P.op("dve", lambda e: e.memset(c80n, -80.0), writes=["c80"])
        ln8c = misc[:, 111:112]
        P.op("dve", lambda e: e.memset(ln8c, LN8), writes=["ln8c"])

        def ln_apply(scale_cols, bias_cols, col_keys, dst_fn, dst_keys_fn, tmp_t):
            mean, rstd = tt(0), tt(1)
            for kp in range(4):
                ks_ = (2 * kp, 2 * kp + 1)
                tqs = {k: tmp_t + (k % 2) for k in ks_}
                for k in ks_:
                    tq = tqs[k]
                    tmp = tt(tq)
                    P.op("dve", lambda e, k=k, tmp=tmp: e.tensor_tensor(out=tmp, in0=xT[:, k, :], in1=mean, op=ALU.subtract),
                         reads=XK(k) + K2("T0"), writes=K2("T%d" % tq))
                for k in ks_:
                    tq = tqs[k]
                    tmp = tt(tq)
                    P.op("dve", lambda e, tmp=tmp: e.tensor_tensor(out=tmp, in0=tmp, in1=rstd, op=ALU.mult),
                         reads=K2("T%d" % tq) + K2("T1"), writes=K2("T%d" % tq))
                for k in ks_:
                    tq = tqs[k]
                    tmp = tt(tq)
                    P.op("act", lambda e, k=k, tmp=tmp: e.activation(out=dst_fn(k), in_=tmp, func=AF.Identity, scale=scale_cols[:, k:k + 1], bias=bias_cols[:, k:k + 1]),
                         reads=K2("T%d" % tq) + col_keys, writes=dst_keys_fn(k))

        def layer_norm_x(scale_cols, bias_cols, col_keys, dst_fn, dst_keys_fn):
            norm_stats([xT[:, k, :] for k in range(8)], [XK(k) for k in range(8)], onesD[:], "onesD", True, 0, 1, [(26, 27), (28, 29)])
            ln_apply(scale_cols, bias_cols, col_keys, dst_fn, dst_keys_fn, 2)

        HB = list(range(0, 8))
        MC = list(range(8, 16))
        OPB = list(range(16, 25))
        AB = list(range(8, 30))

        def fm_proj(ws, n_, c0, tb, bank):
            for k in range(8):
                P.op("pe", lambda e, k=k: e.matmul(PB[bank][:], lhsT=wsl(ws, k * n_ + c0, k * n_ + c0 + 128),
                                                   rhs=pool[:, HB[k], tb * 512:(tb + 1) * 512], start=(k == 0), stop=(k == 7)),
                     reads=[*WK(ws), "B%d_%d" % (HB[k], tb)], writes=["PB%d" % bank])

        def fm_proj2(ws, n_, c0, tb, bank_ap, bank_key):
            for k in range(8):
                P.op("pe", lambda e, k=k: e.matmul(bank_ap, lhsT=wsl(ws, k * n_ + c0, k * n_ + c0 + 128),
                                                   rhs=pool[:, HB[k], tb * 512:(tb + 1) * 512], start=(k == 0), stop=(k == 7)),
                     reads=[*WK(ws), "B%d_%d" % (HB[k], tb)], writes=[bank_key])

        def tm_proj_tile(ws, n, c0, ncols, t, out_ap, out_key):
            for k in range(8):
                P.op("pe", lambda e, k=k: e.matmul(out_ap, lhsT=pool[:, HB[k], t * 128:(t + 1) * 128],
                                                   rhs=wsl(ws, k * n + c0, k * n + c0 + ncols), start=(k == 0), stop=(k == 7)),
                     reads=[*WK(ws), "B%d_%d" % (HB[k], t // 4)], writes=[out_key])

        def gla_pair(l, p, csz, qq, kkh, ks, masks, s0_d, os_d, mcol_off, gate_b, kind, extra_w=()):
            n = 128 // csz
            G = 8 * n
            cps = 256 // csz
            oacc = tt(0)
            cur = {}
            um = tt(1)
            for d in range(2):
                g0 = 0 if d == 0 else G - 1
                P.op("sp", lambda e, d=d: e.dma_start(out=sst[:, d, 1, :], in_=s0_d[l, d, p]), writes=["S%d_1" % d], dma="s0_%d" % d)
                P.op("act", lambda e, d=d, g0=g0: e.activation(out=sbfall[:, d, g0, :], in_=sst[:, d, 1, :], func=AF.Copy),
                     reads=["S%d_1" % d], writes=["SBA%d_%d" % (d, g0 // n)] + list(extra_w))
                cur[d] = 1
            UB = [[(PU[0], "PU0"), (PS4, "PS4")], [(PU[1], "PU1"), (PB[3], "PB3")]]
            for s in range(8):
                tiles = [s, 7 - s]
                for d in range(2):
                    t = tiles[d]
                    hk = "_%d" % (t // 4)
                    ub_t, ub_k = UB[d][s % 2]
                    for c_ in range(n):
                        for hh in range(2):
                            if csz == 32:
                                rhs_ap, rkey = vblk[:, t, c_, hh * 64:(hh + 1) * 64], "vblk"
                            else:
                                rhs_ap, rkey = vbuf[:, t, hh * 64:(hh + 1) * 64], "vbuf"
                            P.op("pe", lambda e, d=d, t=t, hh=hh, c_=c_, ub_t=ub_t, rhs_ap=rhs_ap: e.matmul(
                                    ub_t[:, c_ * 128 + hh * 64:c_ * 128 + (hh + 1) * 64],
                                    lhsT=pool[:, ks[d][hh], t * 128:(t + 1) * 128],
                                    rhs=rhs_ap, start=True, stop=True),
                                 reads=["B%d%s" % (ks[d][hh], hk), rkey], writes=[ub_k])
                for ci in range(n):
                    for d in range(2):
                        t = tiles[d]
                        c = ci if d == 0 else n - 1 - ci
                        g = t * n + c
                        si = cur[d]
                        if d == 0:
                            fin = ((g + 1) % cps == 0)
                            seq = (g + 1) // cps - 1
                            nxt_boundary = fin and (g != G - 1)
                            last = (g == G - 1)
                            gn = g + 1
                        else:
                            fin = (g % cps == 0)
                            seq = g // cps
                            nxt_boundary = fin and (g != 0)
                            last = (g == 0)
                            gn = g - 1
                        ni = (4 + seq) if fin else ((si + 1) % 4 if si < 4 else 0)
                        dcol = dmt[:, mcol_off, d, g:g + 1]
                        ub_t, ub_k = UB[d][s % 2]
                        ucol, ukey = ub_t[:, c * 128:(c + 1) * 128], ub_k
                        P.op("dve", lambda e, d=d, si=si, ni=ni, dcol=dcol, ucol=ucol: e.scalar_tensor_tensor(
                                out=sst[:, d, ni, :], in0=sst[:, d, si, :], scalar=dcol, in1=ucol, op0=ALU.mult, op1=ALU.add),
                             reads=["S%d_%d" % (d, si), "dmt%d_%d" % (mcol_off, d), ukey], writes=["S%d_%d" % (d, ni)])
                        if fin:
                            P.op("sp", lambda e, d=d, ni=ni, seq=seq: e.dma_start(out=os_d[l, d, seq, p], in_=sst[:, d, ni, :]),
                                 reads=["S%d_%d" % (d, ni)], writes=["os_%s_%d_%d_%d_%d" % (kind, l, d, seq, p)],
                                 dma="os%d_%d" % (d, ni))
                        if not last:
                            mk = C_MB if nxt_boundary else C_MB + 1
                            P.op("act", lambda e, d=d, ni=ni, gn=gn, mk=mk: e.activation(out=sbfall[:, d, gn, :], in_=sst[:, d, ni, :],
                                                                                      func=AF.Identity, scale=ctab[:, mk:mk + 1]),
                                 reads=["S%d_%d" % (d, ni), "ctab"], writes=["SBA%d_%d" % (d, gn // n)])
                        cur[d] = ni
            SCB = [[PS4, PB[3]], [PB[2], PU[0]]]
            SCK = [["PS4", "PB3"], ["PB2", "PU0"]]
            OAB, OAK = [PB[0], PB[1]], ["PB0", "PB1"]

            def scores(t):
                tsl = slice(t * 128, (t + 1) * 128)
                hk = "_%d" % (t // 4)
                par = t % 2
                for d in range(2):
                    for h in range(2):
                        P.op("pe", lambda e, d=d, h=h, par=par, tsl=tsl: e.matmul(SCB[d][par][:, h * 128:(h + 1) * 128], lhsT=pool[:, kkh[d][h], tsl],
                                                                                 rhs=pool[:, qq[d], tsl], start=True, stop=True),
                             reads=["B%d%s" % (kkh[d][h], hk), "B%d%s" % (qq[d], hk)], writes=[SCK[d][par]])
                for d in range(2):
                    P.op("dve", lambda e, d=d, par=par: e.tensor_tensor(
                            out=pbuf[:, par, 2 * d:2 * d + 2, :], in0=SCB[d][par][:, 0:256].rearrange("p (h c) -> p h c", h=2),
                            in1=ctab[:, masks[d]:masks[d] + 128].unsqueeze(1).to_broadcast([128, 2, 128]), op=ALU.mult),
                         reads=[SCK[d][par], "ctab"], writes=["pbuf%d_%d" % (par, d)])

            def outputs(t):
                tsl = slice(t * 128, (t + 1) * 128)
                hk = "_%d" % (t // 4)
                par = t % 2
                osl = OAB[par][:, 0:128]
                first = True
                for d in range(2):
                    for h in range(2):
                        P.op("pe", lambda e, d=d, h=h, t=t, par=par, osl=osl, first=first: e.matmul(osl, lhsT=vh[:, h, t, :], rhs=pbuf[:, par, 2 * d + h, :],
                                                                                                start=first, stop=False),
                             reads=["vh", "pbuf%d_%d" % (par, d)], writes=[OAK[par]])
                        first = False
                for d in range(2):
                    for c in range(n):
                        g = t * n + c
                        csl = slice(t * 128 + c * csz, t * 128 + (c + 1) * csz)
                        lastmm = (d == 1 and c == n - 1)
                        P.op("pe", lambda e, d=d, g=g, c=c, csl=csl, par=par, lastmm=lastmm: e.matmul(
                                OAB[par][:, c * csz:(c + 1) * csz], lhsT=sbfall[:, d, g, :], rhs=pool[:, qq[d], csl],
                                start=False, stop=lastmm),
                             reads=["SBA%d_%d" % (d, t), "B%d%s" % (qq[d], hk)], writes=[OAK[par]])
                P.op("act", lambda e, osl=osl, tsl=tsl: e.activation(out=oacc[:, tsl], in_=osl, func=AF.Copy),
                     reads=[OAK[par]], writes=["T0%s" % hk])

            for t in range(9):
                if t < 8:
                    scores(t)
                if t >= 1:
                    outputs(t - 1)

        for l in range(DEPTH):
            mpar = l % 2
            modt = modt_t[:, mpar, :]
            MTK = lambda js: ["mt%d_%d" % (mpar, j) for j in js]

            def mod_block(lm, j12, bank, bank_key):
                ws_ = wuse(WI[("mod", lm, j12)])
                for jj in range(4):
                    for k in range(8):
                        P.op("pe", lambda e, ws_=ws_, jj=jj, k=k: e.matmul(bank[:, jj:jj + 1], lhsT=wsl(ws_, k * 512 + jj * 128, k * 512 + (jj + 1) * 128),
                                                                          rhs=scb[:, k:k + 1], start=(k == 0), stop=(k == 7)),
                             reads=[*WK(ws_), "scb"], writes=[bank_key])
                wdone(WI[("mod", lm, j12)])
                P.op("dve", lambda e: e.tensor_tensor(out=modt_t[:, lm % 2, 4 * j12:4 * j12 + 4], in0=bank[:, 0:4],
                                                      in1=colsA[:, lm * 48 + 4 * j12:lm * 48 + 4 * j12 + 4], op=ALU.add),
                     reads=[bank_key, "colsA"], writes=["mt%d_%d" % (lm % 2, j12)])

            def run_slot_mods(i):
                for (lm, j) in slot_mods(l, i):
                    mod_block(lm, j, PU[1], "PU1")

            if l == 0:
                for j12 in range(4):
                    mod_block(0, j12, PB[j12 % 2], "PB%d" % (j12 % 2))
            P.op("dve", lambda e, modt=modt: e.tensor_scalar(out=modt[:, 48:56], in0=modt[:, 8:16], scalar1=1.0, scalar2=None, op0=ALU.add),
                 reads=MTK([2, 3]), writes=["ma1_%d" % mpar])
            layer_norm_x(modt[:, 48:56], modt[:, 0:8], MTK([0, 1]) + ["ma1_%d" % mpar], lambda k: pbuf_(HB[k]), lambda k: K2("B%d" % HB[k]))

            ws = wuse(WI[("fnet", l)])
            ub_, pc_ = OPB[0:2], OPB[2:6]
            uview = pool[:, ub_[0]:ub_[0] + 2, :].rearrange("p a b -> p (a b)").rearrange("p (t c) -> p t c", c=256)
            UK = K2("B%d" % ub_[0]) + K2("B%d" % ub_[1])
            for t in range(8):
                bank = t % 2
                tm_proj_tile(ws, 256, 0, 256, t, PB[bank][:, 0:256], "PB%d" % bank)
                P.op("act", lambda e, t=t, bank=bank: e.activation(out=uview[:, t, :], in_=PB[bank][:, 0:256], func=AF.Copy),
                     reads=["PB%d" % bank], writes=UK)
            wdone(WI[("fnet", l)])
            run_slot_mods(0)
            for tbl in range(2):
                for ob in range(2):
                    ws = wuse(WI[("dft", l, tbl, ob)])
                    for ct in range(2):
                        bank = 2 + ct
                        for kt in range(8):
                            P.op("pe", lambda e, ws=ws, ct=ct, kt=kt, bank=bank: e.matmul(PB[bank][:], lhsT=uview[:, kt, ct * 128:(ct + 1) * 128],
                                                                                          rhs=wsl(ws, kt * 512, (kt + 1) * 512), start=(kt == 0), stop=(kt == 7)),
                                 reads=UK + [*WK(ws)], writes=["PB%d" % bank])
                        dstb = pc_[tbl * 2 + ct]
                        P.op("act", lambda e, bank=bank, dstb=dstb, ob=ob: e.activation(out=pool[:, dstb, ob * 512:(ob + 1) * 512], in_=PB[bank][:], func=AF.Copy),
                             reads=["PB%d" % bank], writes=["B%d_%d" % (dstb, ob)])
                    wdone(WI[("dft", l, tbl, ob)])
            for ct in range(2):
                for ob in range(2):
                    bank = ob
                    for tbl in range(2):
                        srcb = pc_[tbl * 2 + ct]
                        P.op("pe", lambda e, tbl=tbl, srcb=srcb, ob=ob, bank=bank: e.matmul(PB[bank][:], lhsT=dftcb[:, tbl, :], rhs=pool[:, srcb, ob * 512:(ob + 1) * 512],
                                                                                          start=(tbl == 0), stop=(tbl == 1)),
                             reads=["dftcb", "B%d_%d" % (srcb, ob)], writes=["PB%d" % bank])
                    P.op("act", lambda e, ct=ct, ob=ob, bank=bank: e.activation(out=pool[:, MC[ct], ob * 512:(ob + 1) * 512], in_=PB[bank][:], func=AF.Copy),
                         reads=["PB%d" % bank], writes=["B%d_%d" % (MC[ct], ob)])

            def v_proj(ws, n, c0, need_blk):
                for t in range(8):
                    bank = t % 2
                    tm_proj_tile(ws, n, c0, 128, t, PB[bank][:, 0:128], "PB%d" % bank)
                    P.op("act", lambda e, t=t, bank=bank: e.activation(out=vbuf[:, t, :], in_=PB[bank][:, 0:128], func=AF.Copy),
                         reads=["PB%d" % bank], writes=["vbuf"])
                for h in range(2):
                    P.op("act", lambda e, h=h: e.activation(out=vh[:, h, :, h * 64:(h + 1) * 64], in_=vbuf[:, :, h * 64:(h + 1) * 64], func=AF.Copy),
                         reads=["vbuf"], writes=["vh"])
                if need_blk:
                    for c in range(4):
                        P.op("dve", lambda e, c=c: e.tensor_copy(out=vblk[c * 32:(c + 1) * 32, :, c, :], in_=vbuf[c * 32:(c + 1) * 32, :, :]),
                             reads=["vbuf"], writes=["vblk"])

            def v_proj2(ws, n, c0, need_blk):
                for half in range(2):
                    for tq in range(4):
                        t = half * 4 + tq
                        tm_proj_tile(ws, n, c0, 128, t, PU[1][:, tq * 128:(tq + 1) * 128], "PU1")
                    P.op("act", lambda e, half=half: e.activation(out=vbuf[:, half * 4:(half + 1) * 4, :],
                                                                  in_=PU[1][:].rearrange("p (a b) -> p a b", b=128), func=AF.Copy),
                         reads=["PU1"], writes=["vbuf"])
                for h in range(2):
                    P.op("act", lambda e, h=h: e.activation(out=vh[:, h, :, h * 64:(h + 1) * 64], in_=vbuf[:, :, h * 64:(h + 1) * 64], func=AF.Copy),
                         reads=["vbuf"], writes=["vh"])
                if need_blk:
                    for c in range(4):
                        P.op("dve", lambda e, c=c: e.tensor_copy(out=vblk[c * 32:(c + 1) * 32, :, c, :], in_=vbuf[c * 32:(c + 1) * 32, :, :]),
                             reads=["vbuf"], writes=["vblk"])

            def finish_pair(kind, gate_b, mc_idx):
                oacc = tt(0)
                center = (kind == "ret")
                norm_stats([oacc], [K2("T0")], bd64[:], "bd64", center, 3, 4, [(26, 27)])
                tmp = tt(2)
                HSL = [slice(0, 512), slice(512, 1024)]
                if center:
                    for tb in range(2):
                        P.op("dve", lambda e, tb=tb: e.tensor_tensor(out=tmp[:, HSL[tb]], in0=oacc[:, HSL[tb]], in1=tt(3)[:, HSL[tb]], op=ALU.subtract),
                             reads=["T0_%d" % tb, "T3_%d" % tb], writes=["T2_%d" % tb])
                    for tb in range(2):
                        P.op("dve", lambda e, tb=tb: e.tensor_tensor(out=tmp[:, HSL[tb]], in0=tmp[:, HSL[tb]], in1=tt(4)[:, HSL[tb]], op=ALU.mult),
                             reads=["T2_%d" % tb, "T4_%d" % tb], writes=["T2_%d" % tb])
                else:
                    for tb in range(2):
                        P.op("dve", lambda e, tb=tb: e.tensor_tensor(out=tmp[:, HSL[tb]], in0=oacc[:, HSL[tb]], in1=tt(4)[:, HSL[tb]], op=ALU.mult),
                             reads=["T0_%d" % tb, "T4_%d" % tb], writes=["T2_%d" % tb])
                for tb in range(2):
                    P.op("dve", lambda e, tb=tb: e.tensor_tensor(out=pool[:, MC[mc_idx], HSL[tb]], in0=tmp[:, HSL[tb]], in1=pool[:, gate_b, HSL[tb]], op=ALU.mult),
                         reads=["T2_%d" % tb, "B%d_%d" % (gate_b, tb)], writes=["B%d_%d" % (MC[mc_idx], tb)])

            def ks_transposes(src_b, dst_b_list, evac):
                for t in range(8):
                    P.op("pe", lambda e, t=t: e.transpose(out=PT5[:, (t % 8) * 128:(t % 8 + 1) * 128], in_=pool[:, src_b, t * 128:(t + 1) * 128], identity=identb[:]),
                         reads=["B%d_%d" % (src_b, t // 4), "identb"], writes=["PT5"])
                for t in range(8):
                    evac(t, PT5[:, (t % 8) * 128:(t % 8 + 1) * 128], "PT5")

            for d in range(2):
                for h in range(2):
                    bi_ = OPB[2 + d * 2 + h]
                    P.op("dve", lambda e, h=h, bi_=bi_: e.memset(pool[(1 - h) * 64:(2 - h) * 64, bi_, :], 0.0), writes=K2("B%d" % bi_))
            def ret_tables(p_, par):
                for d in range(2):
                    cidx = l * 6 + d * 3 + p_
                    lgc = misc[:, 16 + cidx:17 + cidx]
                    nlgc = misc[:, 28 + cidx:29 + cidx]
                    pos = ctab[:, C_POSP1:C_POSP1 + 128] if d == 0 else ctab[:, C_POSREV:C_POSREV + 128]
                    P.op("act", lambda e, d=d, lgc=lgc, pos=pos: e.activation(out=dect[:, par, 2 * d, :], in_=pos, func=AF.Exp, scale=lgc),
                         reads=["ctab", "misc_lg2"], writes=["dect%d_%d" % (par, 2 * d)])
                    P.op("act", lambda e, d=d, nlgc=nlgc, pos=pos: e.activation(out=dect[:, par, 2 * d + 1, :], in_=pos, func=AF.Exp, scale=nlgc, bias=ln8c),
                         reads=["ctab", "misc_lg", "ln8c"], writes=["dect%d_%d" % (par, 2 * d + 1)])
                    pz = ctab[:, C_PZF:C_PZF + 128] if d == 0 else ctab[:, C_PZB:C_PZB + 128]
                    P.op("act", lambda e, d=d, lgc=lgc, pz=pz: e.activation(out=dect[:, par, 4 + d, :], in_=pz, func=AF.Exp, scale=lgc, bias=ln8c),
                         reads=["ctab", "misc_lg2", "ln8c"], writes=["dect%d_%d" % (par, 4 + d)])
                    P.op("pe", lambda e, d=d: e.transpose(out=PT5f[:, d * 128:(d + 1) * 128], in_=dect[:, par, 4 + d, :], identity=ident),
                         reads=["dect%d_%d" % (par, 4 + d), "ctab"], writes=["PT5"])
                for d in range(2):
                    cidx = l * 6 + d * 3 + p_
                    P.op("act", lambda e, d=d: e.activation(out=dect[:, par, 4 + d, :], in_=PT5f[:, d * 128:(d + 1) * 128], func=AF.Copy),
                         reads=["PT5"], writes=["dect%d_%d" % (par, 4 + d)])
                    P.op("dve", lambda e, d=d, cidx=cidx: e.tensor_tensor(out=dmt[:, par, d, 0:8], in0=misc[:, 40 + cidx:41 + cidx].to_broadcast([128, 8]),
                                                                         in1=ctab[:, (C_MRF if d == 0 else C_MRB):(C_MRF if d == 0 else C_MRB) + 8], op=ALU.mult),
                         reads=["misc_D", "ctab"], writes=["dmt%d_%d" % (par, d)])

            ret_tables(0, 0)
            for p in range(3):
                ws = wuse(WI[("ret", l, p)])
                qq = [OPB[0], OPB[1]]
                kkh = [[OPB[2], OPB[3]], [OPB[4], OPB[5]]]
                ks = [[OPB[6], 26], [OPB[7], 27]]
                for d_ in range(2):
                    for hh_ in range(2):
                        bz = ks[d_][hh_]
                        P.op("dve", lambda e, bz=bz: e.memset(pool[:, bz, :], 0.0), writes=K2("B%d" % bz))
                gate_b = OPB[8]
                par = p % 2
                wv = wsl(ws, 0, 4096).rearrange("p (k g a c) -> p k g a c", k=8, g=16, a=2)
                wsw = pool[:, 28:30, :].rearrange("p a b -> p (a b)").rearrange("p (k n) -> p k n", n=256)
                sv = wsw.rearrange("p k (g a c) -> p k g a c", g=8, a=2)
                for a in range(2):
                    P.op("act", lambda e, a=a, wv=wv, sv=sv: e.activation(out=sv[:, :, :, a, :], in_=wv[:, :, 0:8, 1 - a, :], func=AF.Copy),
                         reads=[*WK(ws)], writes=K2("B28") + K2("B29"))
                rope = t32[:, 0:2, :]
                P.op("sp", lambda e: e.dma_start(out=t32[:, 0:2, :], in_=rope_d), writes=K2("T0") + K2("T1"), dma="c1")
                for which in range(2):
                    rot = tt(2 + which)
                    SLs = [slice(0, 512), slice(512, 1024)]
                    for tb in range(2):
                        ba, bb_ = 2 * tb, 2 * tb + 1
                        fm_proj(ws, 512, which * 128, tb, ba)
                        for k in range(8):
                            P.op("pe", lambda e, k=k, which=which, tb=tb, bb_=bb_: e.matmul(PB[bb_][:], lhsT=wsw[:, k, which * 128:(which + 1) * 128],
                                                                                        rhs=pool[:, HB[k], tb * 512:(tb + 1) * 512], start=(k == 0), stop=(k == 7)),
                                 reads=K2("B28") + K2("B29") + ["B%d_%d" % (HB[k], tb)], writes=["PB%d" % bb_])
                    for tb in range(2):
                        ba, bb_ = 2 * tb, 2 * tb + 1
                        sl = SLs[tb]
                        P.op("dve", lambda e, rot=rot, sl=sl, ba=ba: e.tensor_tensor(out=rot[:, sl], in0=PB[ba][:], in1=rope[:, 0, sl], op=ALU.mult),
                             reads=["PB%d" % ba, "T0_%d" % tb], writes=["T%d_%d" % (2 + which, tb)])
                        P.op("dve", lambda e, sl=sl, bb_=bb_: e.tensor_tensor(out=tt(4)[:, sl], in0=PB[bb_][:], in1=rope[:, 1, sl], op=ALU.mult),
                             reads=["PB%d" % bb_, "T1_%d" % tb], writes=["T4_%d" % tb])
                    for tb in range(2):
                        sl = SLs[tb]
                        P.op("dve", lambda e, rot=rot, sl=sl: e.tensor_tensor(out=rot[:, sl], in0=rot[:, sl], in1=tt(4)[:, sl], op=ALU.add),
                             reads=["T%d_%d" % (2 + which, tb), "T4_%d" % tb], writes=["T%d_%d" % (2 + which, tb)])
                qrot, krot = tt(2), tt(3)
                r3 = lambda ap: ap.rearrange("p (t c) -> p t c", c=128)
                for d in range(2):
                    eq = dect[:, par, 2 * d, :]
                    ek = dect[:, par, 2 * d + 1, :]
                    P.op("dve", lambda e, d=d, eq=eq: e.tensor_tensor(out=r3(pbuf_(qq[d])), in0=r3(qrot), in1=eq.unsqueeze(1).to_broadcast([128, 8, 128]), op=ALU.mult),
                         reads=K2("T2") + ["dect%d_%d" % (par, 2 * d)], writes=K2("B%d" % qq[d]))
                    for h in range(2):
                        hs = slice(h * 64, (h + 1) * 64)
                        P.op("dve", lambda e, d=d, h=h, hs=hs, ek=ek: e.tensor_tensor(out=r3(pool[hs, kkh[d][h], :]), in0=r3(krot[hs, :]),
                                                                                  in1=ek[hs, :].unsqueeze(1).to_broadcast([64, 8, 128]), op=ALU.mult),
                             reads=K2("T3") + ["dect%d_%d" % (par, 2 * d + 1)], writes=K2("B%d" % kkh[d][h]))
                kb = 25
                P.op("act", lambda e: e.activation(out=pbuf_(kb), in_=krot, func=AF.Copy), reads=K2("T3"), writes=K2("B%d" % kb))

                def evac_ret(t, src, skey, par=par, ks=ks):
                    for d in range(2):
                        zt = dect[:, par, 4 + d, :]
                        for hh in range(2):
                            hcs = slice(hh * 64, (hh + 1) * 64)
                            P.op("dve", lambda e, d=d, t=t, src=src, zt=zt, ks=ks, hh=hh, hcs=hcs: e.tensor_tensor(
                                    out=pool[:, ks[d][hh], t * 128 + hh * 64:t * 128 + (hh + 1) * 64], in0=src[:, hcs], in1=zt[:, hcs], op=ALU.mult),
                                 reads=[skey, "dect%d_%d" % (par, 4 + d)], writes=["B%d_%d" % (ks[d][hh], t // 4)])
                ks_transposes(kb, ks, evac_ret)
                v_proj(ws, 512, 256, False)
                for tb in range(2):
                    fm_proj(ws, 512, 384, tb, 2 + tb)
                    P.op("act", lambda e, tb=tb: e.activation(out=pool[:, gate_b, tb * 512:(tb + 1) * 512], in_=PB[2 + tb][:], func=AF.Silu),
                         reads=["PB%d" % (2 + tb)], writes=["B%d_%d" % (gate_b, tb)])
                wdone(WI[("ret", l, p)])
                run_slot_mods(1 + p)
                if p < 2:
                    ret_tables(p + 1, (p + 1) % 2)
                gla_pair(l, p, 128, qq, kkh, ks, (C_RMF, C_RMB), s0r_d, osr_d, par, gate_b, "r")
                finish_pair("ret", gate_b, 2 + p)

            GBK = [(PS4[:], "PS4"), (PU[0][:], "PU0")]
            xf = sbfall[:, :, :, :].rearrange("p a b c -> p (a b c)").bitcast(F32)
            ALL_SBA = ["SBA%d_%d" % (d_, t_) for d_ in range(2) for t_ in range(8)]
            XKEYS = ["X%d_%d" % (i_, h_) for i_ in range(4) for h_ in range(2)]
            HS = [slice(0, 512), slice(512, 1024)]
            for p in range(3):
                ws = wuse(WI[("hg", l, p)])
                qq = [OPB[0], OPB[1]]
                kkh = [[OPB[2], OPB[3]], [OPB[4], OPB[5]]]
                ks = [[OPB[6], 28], [OPB[7], 29]]
                for d_ in range(2):
                    for hh_ in range(2):
                        bz = ks[d_][hh_]
                        P.op("dve", lambda e, bz=bz: e.memset(pool[:, bz, :], 0.0), writes=K2("B%d" % bz))
                gate_b = OPB[8]
                dpar = (p + 1) % 2
                TSET = [dict(sig=tt(0), kf=tt(1), bb=tt(3), einv=tt(4), K=("T0", "T1", "T3", "T4"), kb=25,
                             banks=[(PB[2][:], "PB2"), (PB[3][:], "PB3")]),
                        dict(sig=xf[:, 0:1024], kf=xf[:, 1024:2048], bb=xf[:, 2048:3072], einv=xf[:, 3072:4096],
                             K=("X0", "X1", "X2", "X3"), kb=26, banks=GBK)]
                for tb in range(2):
                    fm_proj2(ws, 640, 512, tb, GBK[tb][0], GBK[tb][1])
                    P.op("act", lambda e, tb=tb, gate_b=gate_b: e.activation(out=pool[:, gate_b, tb * 512:(tb + 1) * 512], in_=GBK[tb][0], func=AF.Silu),
                         reads=[GBK[tb][1]], writes=["B%d_%d" % (gate_b, tb)])
                v_proj2(ws, 640, 384, True)
                qf = tt(2)
                for tb in range(2):
                    fm_proj(ws, 640, 0, tb, tb)
                    P.op("act", lambda e, tb=tb: e.activation(out=qf[:, tb * 512:(tb + 1) * 512], in_=PB[tb][:], func=AF.Silu),
                         reads=["PB%d" % tb], writes=["T2_%d" % tb])
                for d in range(2):
                    for tb in range(2):
                        bk_ap, bk_key = TSET[d]["banks"][tb]
                        fm_proj2(ws, 640, 128 * (1 + d), tb, bk_ap, bk_key)
                cols = []
                for d in range(2):
                    lidx = d * 6 + l * 3 + p
                    cols.append(dict(oml=misc[:, 76 + lidx:77 + lidx], lb=misc[:, 64 + lidx:65 + lidx], lbm1=misc[:, 88 + lidx:89 + lidx],
                                     edge=(31 if d == 0 else 0)))
                DT = [(d, tb) for d in range(2) for tb in range(2)]
                for (d, tb) in DT:
                    T_, (bk_ap, bk_key) = TSET[d], TSET[d]["banks"][tb]
                    extra = ALL_SBA if (d == 1 and tb == 0) else []
                    P.op("act", lambda e, T_=T_, tb=tb, bk_ap=bk_ap: e.activation(out=T_["sig"][:, HS[tb]], in_=bk_ap, func=AF.Sigmoid),
                         reads=[bk_key], writes=["%s_%d" % (T_["K"][0], tb)] + extra)
                for (d, tb) in DT:
                    T_, C_ = TSET[d], cols[d]
                    P.op("dve", lambda e, T_=T_, C_=C_, tb=tb: e.tensor_scalar(out=T_["kf"][:, HS[tb]], in0=T_["sig"][:, HS[tb]], scalar1=C_["lbm1"], scalar2=C_["oml"],
                                                                               op0=ALU.mult, op1=ALU.add),
                         reads=["%s_%d" % (T_["K"][0], tb), "misc_lbm1", "misc_oml"], writes=["%s_%d" % (T_["K"][1], tb)])
                for (d, tb) in DT:
                    T_, C_ = TSET[d], cols[d]
                    P.op("act", lambda e, T_=T_, C_=C_, tb=tb: e.activation(out=T_["sig"][:, HS[tb]], in_=T_["sig"][:, HS[tb]], func=AF.Ln, scale=C_["oml"], bias=C_["lb"]),
                         reads=["%s_%d" % (T_["K"][0], tb), "misc_oml", "misc_lb"], writes=["%s_%d" % (T_["K"][0], tb)])
                for (d, tb) in DT:
                    T_ = TSET[d]
                    P.op("dve", lambda e, T_=T_, tb=tb: e.tensor_tensor_scan(out=T_["bb"][:, HS[tb]], data0=ctab[:, C_SEG:C_SEG + 512], data1=T_["sig"][:, HS[tb]],
                                                                             initial=0.0, op0=ALU.mult, op1=ALU.add),
                         reads=["%s_%d" % (T_["K"][0], tb), "ctab"], writes=["%s_%d" % (T_["K"][2], tb)])
                T1 = TSET[1]
                for tb in range(2):
                    b3 = T1["bb"][:, HS[tb]].rearrange("p (c s) -> p c s", s=32)
                    P.op("dve", lambda e, b3=b3: e.tensor_tensor(out=b3, in0=b3, in1=b3[:, :, 31:32].to_broadcast([128, 16, 32]), op=ALU.subtract),
                         reads=["X2_%d" % tb], writes=["X2_%d" % tb])
                for tb in range(2):
                    P.op("dve", lambda e, T1=T1, tb=tb: e.tensor_tensor(out=T1["bb"][:, HS[tb]], in0=T1["sig"][:, HS[tb]], in1=T1["bb"][:, HS[tb]], op=ALU.subtract),
                         reads=["X2_%d" % tb, "X0_%d" % tb], writes=["X2_%d" % tb])
                for (d, tb) in DT:
                    T_ = TSET[d]
                    P.op("act", lambda e, T_=T_, tb=tb: e.activation(out=T_["bb"][:, HS[tb]], in_=T_["bb"][:, HS[tb]], func=AF.Relu, bias=c80p, scale=1.0),
                         reads=["%s_%d" % (T_["K"][2], tb), "c80"], writes=["%s_%d" % (T_["K"][2], tb)])
                for (d, tb) in DT:
                    T_ = TSET[d]
                    P.op("act", lambda e, T_=T_, tb=tb: e.activation(out=T_["sig"][:, HS[tb]], in_=T_["bb"][:, HS[tb]], func=AF.Exp, bias=c80n, scale=1.0),
                         reads=["%s_%d" % (T_["K"][2], tb), "c80"], writes=["%s_%d" % (T_["K"][0], tb)])
                    P.op("act", lambda e, T_=T_, tb=tb: e.activation(out=T_["einv"][:, HS[tb]], in_=T_["bb"][:, HS[tb]], func=AF.Exp, bias=c80p, scale=-1.0),
                         reads=["%s_%d" % (T_["K"][2], tb), "c80"], writes=["%s_%d" % (T_["K"][3], tb)])
                for (d, tb) in DT:
                    T_, edge = TSET[d], cols[d]["edge"]
                    kE, kK, kI = "%s_%d" % (T_["K"][0], tb), "%s_%d" % (T_["K"][1], tb), "%s_%d" % (T_["K"][3], tb)
                    e3 = T_["sig"][:, HS[tb]].rearrange("p (c s) -> p c s", s=32)
                    i3 = T_["einv"][:, HS[tb]].rearrange("p (c s) -> p c s", s=32)
                    moff = (C_MHF if d == 0 else C_MHB) + tb * 16
                    P.op("dve", lambda e, d=d, tb=tb, e3=e3, moff=moff, edge=edge, dpar=dpar: e.tensor_tensor(out=dmt[:, dpar, d, tb * 16:(tb + 1) * 16], in0=e3[:, :, edge],
                                                                                                      in1=ctab[:, moff:moff + 16], op=ALU.mult),
                         reads=[kE, "ctab"], writes=["dmt%d_%d" % (dpar, d)])
                    P.op("dve", lambda e, d=d, tb=tb, T_=T_, qq=qq: e.tensor_tensor(out=pool[:, qq[d], HS[tb]], in0=qf[:, HS[tb]], in1=T_["sig"][:, HS[tb]], op=ALU.mult),
                         reads=["T2_%d" % tb, kE], writes=["B%d_%d" % (qq[d], tb)])
                    for h in range(2):
                        hs = slice(h * 64, (h + 1) * 64)
                        P.op("dve", lambda e, d=d, h=h, hs=hs, tb=tb, T_=T_, kkh=kkh: e.tensor_tensor(out=pool[hs, kkh[d][h], HS[tb]], in0=T_["kf"][hs, HS[tb]],
                                                                                                  in1=T_["einv"][hs, HS[tb]], op=ALU.mult),
                             reads=[kK, kI], writes=["B%d_%d" % (kkh[d][h], tb)])
                    P.op("dve", lambda e, i3=i3, e3=e3, edge=edge: e.tensor_tensor(out=i3, in0=i3, in1=e3[:, :, edge:edge + 1].to_broadcast([128, 16, 32]), op=ALU.mult),
                         reads=[kI, kE], writes=[kI])
                for (d, tb) in DT:
                    T_ = TSET[d]
                    P.op("dve", lambda e, T_=T_, tb=tb: e.tensor_tensor(out=pool[:, T_["kb"], HS[tb]], in0=T_["kf"][:, HS[tb]], in1=T_["einv"][:, HS[tb]], op=ALU.mult),
                         reads=["%s_%d" % (T_["K"][1], tb), "%s_%d" % (T_["K"][3], tb)], writes=["B%d_%d" % (T_["kb"], tb)])
                for d in range(2):
                    def evac_h(t, src, skey, d=d, ks=ks):
                        for hh in range(2):
                            P.op("act", lambda e, t=t, src=src, hh=hh: e.activation(out=pool[:, ks[d][hh], t * 128 + hh * 64:t * 128 + (hh + 1) * 64],
                                                                                in_=src[:, hh * 64:(hh + 1) * 64], func=AF.Copy),
                                 reads=[skey], writes=["B%d_%d" % (ks[d][hh], t // 4)])
                    ks_transposes(TSET[d]["kb"], ks, evac_h)
                wdone(WI[("hg", l, p)])
                run_slot_mods(4 + p)
                gla_pair(l, p, 32, qq, kkh, ks, (C_HMF, C_HMB), s0h_d, osh_d, dpar, gate_b, "h", extra_w=XKEYS)
                finish_pair("hg", gate_b, 5 + p)

            def resid_update(dc, tb, bank, gcol, gkeys):
                sl = slice(tb * 512, (tb + 1) * 512)
                tmp = tt(2)
                P.op("act", lambda e: e.activation(out=tmp[:, sl], in_=PB[bank][:], func=AF.Identity, scale=gcol),
                     reads=["PB%d" % bank] + gkeys, writes=["T2_%d" % tb])
                P.op("dve", lambda e: e.scalar_tensor_tensor(out=xT[:, dc, sl], in0=xT[:, dc, sl], scalar=float(ALPHA), in1=tmp[:, sl], op0=ALU.mult, op1=ALU.add),
                     reads=["xT%d_%d" % (dc, tb), "T2_%d" % tb], writes=["xT%d_%d" % (dc, tb)])

            for ob in range(2):
                ws = wuse(WI[("out", l, ob)])
                for dcc in range(4):
                    dc = ob * 4 + dcc
                    for tb in range(2):
                        bank = (dcc * 2 + tb) % 4
                        for fc in range(8):
                            P.op("pe", lambda e, ws=ws, dcc=dcc, fc=fc, tb=tb, bank=bank: e.matmul(PB[bank][:], lhsT=wsl(ws, fc * 512 + dcc * 128, fc * 512 + (dcc + 1) * 128),
                                                                                               rhs=pool[:, MC[fc], tb * 512:(tb + 1) * 512], start=(fc == 0), stop=(fc == 7)),
                                 reads=[*WK(ws), "B%d_%d" % (MC[fc], tb)], writes=["PB%d" % bank])
                        resid_update(dc, tb, bank, modt[:, 16 + dc:17 + dc], MTK([4, 5]))
                wdone(WI[("out", l, ob)])
            gcols = colsB[:, l * 16:l * 16 + 8]
            bcols = colsB[:, 32 + l * 16:32 + l * 16 + 8]
            layer_norm_x(gcols, bcols, ["colsB"], lambda k: xT[:, k, :], lambda k: XK(k))
            if stop == "mix%d" % l:
                break

            P.op("dve", lambda e, modt=modt: e.tensor_scalar(out=modt[:, 56:64], in0=modt[:, 32:40], scalar1=1.0, scalar2=None, op0=ALU.add),
                 reads=MTK([8, 9]), writes=["ma2_%d" % mpar])
            layer_norm_x(modt[:, 56:64], modt[:, 24:32], MTK([6, 7]) + ["ma2_%d" % mpar], lambda k: pbuf_(HB[k]), lambda k: K2("B%d" % HB[k]))
            for fb in range(11):
                nj = 2
                n = 256
                wsg = wuse(WI[("gate", l, fb)])
                wsu = wuse(WI[("up", l, fb)])
                for jj in range(nj):
                    j = fb * 2 + jj
                    for tb in range(2):
                        sl = slice(tb * 512, (tb + 1) * 512)
                        bg, bu = (tb * 2) % 4, (tb * 2 + 1) % 4
                        for k in range(8):
                            P.op("pe", lambda e, k=k, wsg=wsg, jj=jj, n=n, sl=sl, bg=bg: e.matmul(PB[bg][:], lhsT=wsl(wsg, k * n + jj * 128, k * n + (jj + 1) * 128),
                                                                                            rhs=pool[:, HB[k], sl], start=(k == 0), stop=(k == 7)),
                                 reads=[*WK(wsg), "B%d_%d" % (HB[k], tb)], writes=["PB%d" % bg])
                        for k in range(8):
                            P.op("pe", lambda e, k=k, wsu=wsu, jj=jj, n=n, sl=sl, bu=bu: e.matmul(PB[bu][:], lhsT=wsl(wsu, k * n + jj * 128, k * n + (jj + 1) * 128),
                                                                                            rhs=pool[:, HB[k], sl], start=(k == 0), stop=(k == 7)),
                                 reads=[*WK(wsu), "B%d_%d" % (HB[k], tb)], writes=["PB%d" % bu])
                        sgt = sgtb[:, tb, :]
                        P.op("act", lambda e, bg=bg, sgt=sgt: e.activation(out=sgt, in_=PB[bg][:], func=AF.Silu), reads=["PB%d" % bg], writes=["sgt%d" % tb])
                        P.op("dve", lambda e, bu=bu, sgt=sgt, j=j, sl=sl: e.tensor_tensor(out=pool[:, AB[j], sl], in0=PB[bu][:], in1=sgt, op=ALU.mult),
                             reads=["PB%d" % bu, "sgt%d" % tb], writes=["B%d_%d" % (AB[j], tb)])
                wdone(WI[("gate", l, fb)])
                wdone(WI[("up", l, fb)])
            for dc in range(8):
                ws = wuse(WI[("down", l, dc)])
                for tb in range(2):
                    bank = (dc * 2 + tb) % 4
                    for j in range(NFF):
                        P.op("pe", lambda e, ws=ws, j=j, tb=tb, bank=bank: e.matmul(PB[bank][:], lhsT=wsl(ws, j * 128, (j + 1) * 128),
                                                                                  rhs=pool[:, AB[j], tb * 512:(tb + 1) * 512], start=(j == 0), stop=(j == NFF - 1)),
                             reads=[*WK(ws), "B%d_%d" % (AB[j], tb)], writes=["PB%d" % bank])
                    resid_update(dc, tb, bank, modt[:, 40 + dc:41 + dc], MTK([10, 11]))
                wdone(WI[("down", l, dc)])
            gcols = colsB[:, l * 16 + 8:l * 16 + 16]
            bcols = colsB[:, 32 + l * 16 + 8:32 + l * 16 + 16]
            layer_norm_x(gcols, bcols, ["colsB"], lambda k: xT[:, k, :], lambda k: XK(k))
            if stop == "ffn%d" % l:
                break

        for t in range(8):
            b = t % 2
            for kh in range(2):
                bk = 2 * b + kh
                for kk in range(4):
                    k = kh * 4 + kk
                    P.op("pe", lambda e, bk=bk, kk=kk, k=k, t=t: e.transpose(out=PB[bk][:, kk * 128:(kk + 1) * 128], in_=xT[:, k, t * 128:(t + 1) * 128], identity=ident),
                         reads=["xT%d_%d" % (k, t // 4), "ctab"], writes=["PB%d" % bk])
                P.op("act" if kh == 0 else "dve",
                     (lambda e, kh=kh, b=b, bk=bk: e.activation(out=t32[:, b, kh * 512:(kh + 1) * 512], in_=PB[bk][:], func=AF.Copy)) if kh == 0 else
                     (lambda e, kh=kh, b=b, bk=bk: e.tensor_copy(out=t32[:, b, kh * 512:(kh + 1) * 512], in_=PB[bk][:])),
                     reads=["PB%d" % bk], writes=["T%d_%d" % (b, kh)])
            P.op("sp", lambda e, t=t, b=b: e.dma_start(out=y_d[t * 128:(t + 1) * 128, :], in_=t32[:, b, :]),
                 reads=K2("T%d" % b), writes=["y%d" % t], dma="yout%d" % b)
        P.emit()
    return nc


def _const_tables(is_sample):
    ct = np.zeros((128, NCT), np.float32)
    ct[:, C_ID:C_ID + 128] = np.eye(128, dtype=np.float32)
    tpos = np.arange(128, dtype=np.float32)
    ct[:, C_POSP1:C_POSP1 + 128] = tpos[None, :] + 1.0
    ct[:, C_POSREV:C_POSREV + 128] = 128.0 - tpos[None, :]
    bm = np.zeros((128, 128), np.float32)
    bm[:64, :64] = 1.0
    bm[64:, 64:] = 1.0
    mb = 1.0 if is_sample else 0.0
    ct[:, C_BM:C_BM + 128] = bm
    ct[:, C_BMB:C_BMB + 128] = bm * mb
    ct[:, C_PCOL] = 127.0 - tpos
    ct[:, C_PCOL + 1] = tpos
    for off_f, off_b, G, cps in ((C_MRF, C_MRB, 8, 2), (C_MHF, C_MHB, 32, 8)):
        mf = np.ones(G, np.float32)
        mbk = np.ones(G, np.float32)
        for g in range(G):
            if g % cps == 0 and g > 0:
                mf[g] = mb
            if (g + 1) % cps == 0 and g != G - 1:
                mbk[g] = mb
        ct[:, off_f:off_f + G] = mf[None, :]
        ct[:, off_b:off_b + G] = mbk[None, :]
    n = np.arange(64)
    ang = 2.0 * np.pi * np.outer(n, n) / 64.0
    c64 = np.cos(ang) / 8.0
    s64 = np.sin(ang) / 8.0
    bdc = np.zeros((128, 128))
    bds = np.zeros((128, 128))
    bdc[:64, :64] = c64
    bdc[64:, 64:] = c64
    bds[:64, :64] = -s64
    bds[64:, 64:] = -s64
    ct[:, C_DFTC:C_DFTC + 128] = bdc
    ct[:, C_DFTS:C_DFTS + 128] = bds
    j = np.arange(128)[:, None]
    i = np.arange(128)[None, :]
    ct[:, C_RMF:C_RMF + 128] = (j <= i)
    ct[:, C_RMB:C_RMB + 128] = (j >= i)
    same = (j // 32 == i // 32)
    ct[:, C_HMF:C_HMF + 128] = (j <= i) & same
    ct[:, C_HMB:C_HMB + 128] = (j >= i) & same
    seg = np.ones(1024, np.float32)
    seg[::32] = 0.0
    ct[:, C_SEG:C_SEG + 1024] = seg[None, :]
    ct[:, C_PZF:C_PZF + 128] = 127.0 - tpos[None, :]
    ct[:, C_PZB:C_PZB + 128] = tpos[None, :]
    ct[:, C_MB] = mb
    ct[:, C_MB + 1] = 1.0
    return ct


def _rope_tables(is_sample):
    r = np.zeros((128, 2, T), np.float64)
    if not is_sample:
        r[:, 0, :] = 1.0
        return r.astype(np.float32)
    tok = np.arange(T)
    rows = (tok // 64).astype(np.float64)
    cols = (tok % 64).astype(np.float64)
    half = 32
    inv = 10000.0 ** (-np.arange(0, half, 2, dtype=np.float64) / half)
    for pp in range(128):
        dd = pp % 64
        pos = rows if dd < 32 else cols
        w = dd % 32
        fi = w % 16
        ang = pos * inv[fi]
        r[pp, 0, :] = np.cos(ang)
        r[pp, 1, :] = -np.sin(ang) if w < 16 else np.sin(ang)
    return r.astype(np.float32)


def _dft_tables(is_sample):
    L = 1024 if is_sample else 256
    n = np.arange(L)
    ang = 2.0 * np.pi * np.outer(n, n) / L
    c = np.cos(ang) / np.sqrt(L)
    s = np.sin(ang) / np.sqrt(L)
    out = np.zeros((2, T, T), np.float32)
    for b in range(T // L):
        out[0, b * L:(b + 1) * L, b * L:(b + 1) * L] = c
        out[1, b * L:(b + 1) * L, b * L:(b + 1) * L] = s
    return out


def _bd_state(s):
    out = np.zeros((DEPTH, 2, 3, 128, 128), np.float32)
    for p in range(3):
        out[:, :, p, :64, :64] = s[:, :, 2 * p]
        out[:, :, p, 64:, 64:] = s[:, :, 2 * p + 1]
    return out


_NC_CACHE = {}


def kernel(x_prompt, x_sample, c, state_ret, state_hgrn, c_ctx, w_mod, b_mod, w_in, w_out,
           ret_log_decay, hg_lower_bound, ln_g, ln_b, w_gate, w_up, w_down):
    f = lambda a: np.ascontiguousarray(np.asarray(a, dtype=np.float32))
    x_prompt, x_sample, c, state_ret, state_hgrn, c_ctx = map(f, (x_prompt, x_sample, c, state_ret, state_hgrn, c_ctx))
    w_mod, b_mod, w_in, w_out, w_gate, w_up, w_down = map(f, (w_mod, b_mod, w_in, w_out, w_gate, w_up, w_down))
    ret_log_decay, hg_lower_bound, ln_g, ln_b = map(f, (ret_log_decay, hg_lower_bound, ln_g, ln_b))

    if "nc" not in _NC_CACHE:
        import os
        _NC_CACHE["nc"] = build_program(os.environ.get("KSTOP"))
    nc = _NC_CACHE["nc"]

    smB = np.zeros((128, 128), np.float32)
    smB[0:32] = ln_g.reshape(32, 128)
    smB[32:64] = ln_b.reshape(32, 128)
    smB[64:76] = hg_lower_bound.reshape(12, 128)
    dec = np.repeat(ret_log_decay.reshape(DEPTH, 2, 6), 64, axis=-1).reshape(12, 128)
    smB[76:88] = dec
    tabs = {s: (_const_tables(s), _rope_tables(s), _dft_tables(s)) for s in (False, True)}
    zs = np.zeros((DEPTH, 2, 3, 128, 128), np.float32)
    in_maps = []
    for core in range(NCORES):
        is_sample = core >= 4
        if is_sample:
            b = core - 4
            xin = x_sample[b]
            cvec = c[b]
            s0r = _bd_state(state_ret[b])
            s0h = _bd_state(state_hgrn[b])
        else:
            xin = x_prompt[core * 4:(core + 1) * 4].reshape(T, D)
            cvec = c_ctx
            s0r, s0h = zs, zs
        smA = np.zeros((128, 128), np.float32)
        smA[0:96] = b_mod.reshape(96, 128)
        smA[96:104] = cvec.reshape(8, 128)
        ct, rp, dl = tabs[is_sample]
        in_maps.append(dict(x=np.ascontiguousarray(xin), smA=smA, smB=smB, ctab=ct, rope=rp, dftL=dl,
                            s0r=s0r, s0h=s0h, w_mod=w_mod, w_in=w_in, w_out=w_out, w_gate=w_gate, w_up=w_up, w_down=w_down))
    res = run_bass_kernel_spmd(nc, in_maps, core_ids=list(range(NCORES)))
    R = res.results
    y_prompt = np.stack([R[i]["y"] for i in range(4)]).reshape(16, 256, D)
    y_sample = np.stack([R[i]["y"] for i in range(4, 8)])

    def unpack(name):
        out = np.zeros((16, DEPTH, 2, 6, 64, 64), np.float32)
        for core in range(4):
            o = R[core][name]
            for p in range(3):
                out[core * 4:(core + 1) * 4, :, :, 2 * p] = o[:, :, :, p, :64, :64].transpose(2, 0, 1, 3, 4)
                out[core * 4:(core + 1) * 4, :, :, 2 * p + 1] = o[:, :, :, p, 64:, 64:].transpose(2, 0, 1, 3, 4)
        return out

    return (y_prompt.astype(np.float32), y_sample.astype(np.float32), unpack("osr"), unpack("osh"))
```
